# Optimizing a Trainium2 kernel written in Bass

```python
import jax
import jax.numpy as jnp
from jax import lax
import numpy as np

D_MODEL = 1024
BATCH = 2
SEQ = 8192
DEPTH = 2
DEC_BATCH = 128
DEC_SEQ = 4
PAST_LEN = 8192
PAGE_SIZE = 128

N_A_LAYERS = DEPTH // 2
N_B_LAYERS = DEPTH - N_A_LAYERS
D_FF = ((8 * D_MODEL // 3 + 127) // 128) * 128
CONV_W = 3
N_HEADS = D_MODEL // 128
NOPE_DIM = 128
ROPE_DIM = 64
V_DIM = 128
Q_RANK = D_MODEL // 2
KV_RANK = D_MODEL // 4
Q_BLOCK = 128
ROPE_THETA = 10000.0
EPS = 1e-6
SCALE = (NOPE_DIM + ROPE_DIM) ** -0.5
NEG = -1e30

kernel_name = 'hybrid_shortconv_mla_yoco_step'


def rmsnorm(x, g):
    xf = x.astype(jnp.float32)
    y = xf * lax.rsqrt(jnp.mean(xf * xf, axis=-1, keepdims=True) + EPS)
    return (y * g.astype(jnp.float32)).astype(x.dtype)


def swiglu(x, w_gu, w_down):
    gate, up = jnp.split(x @ w_gu, 2, axis=-1)
    return (jax.nn.silu(gate) * up) @ w_down


def rope(x, pos):
    half = x.shape[-1] // 2
    freqs = ROPE_THETA ** (-jnp.arange(half, dtype=jnp.float32) / half)
    ang = pos.astype(jnp.float32)[:, None] * freqs
    bshape = (pos.shape[0],) + (1,) * (x.ndim - 3) + (half,)
    cos = jnp.cos(ang).reshape(bshape)
    sin = jnp.sin(ang).reshape(bshape)
    xf = x.astype(jnp.float32)
    x1, x2 = xf[..., :half], xf[..., half:]
    return jnp.concatenate([x1 * cos - x2 * sin, x2 * cos + x1 * sin], axis=-1).astype(x.dtype)


def short_conv_mixer(xn, prev, w_in, w_conv, w_out):
    b, c, h = jnp.split(xn @ w_in, 3, axis=-1)
    u = c * h
    ext = jnp.concatenate([prev.astype(u.dtype), u], axis=1)
    s = u.shape[1]
    y = w_conv[0] * ext[:, 0:s]
    for j in range(1, CONV_W):
        y = y + w_conv[j] * ext[:, j:j + s]
    return (b * y) @ w_out, ext[:, s:]


def shared_kv(h, pos, norm_kv_in, w_dkv, kv_norm, k_pe_norm):
    ckv = rmsnorm(h, norm_kv_in) @ w_dkv
    lat = rmsnorm(ckv[..., :KV_RANK], kv_norm)
    kpe = rope(rmsnorm(ckv[..., KV_RANK:], k_pe_norm), pos)
    return lat, kpe


def expand_kv(lat, w_uk, w_uv, k_nope_norm):
    lead = lat.shape[:-1]
    kn = rmsnorm((lat @ w_uk).reshape(lead + (N_HEADS, NOPE_DIM)), k_nope_norm)
    v = (lat @ w_uv).reshape(lead + (N_HEADS, V_DIM))
    return kn, v


def mla_queries(xn, pos, w_dq, q_norm, w_uq, q_nope_norm, q_pe_norm):
    cq = rmsnorm(xn @ w_dq, q_norm)
    q = (cq @ w_uq).reshape(xn.shape[:-1] + (N_HEADS, NOPE_DIM + ROPE_DIM))
    qn = rmsnorm(q[..., :NOPE_DIM], q_nope_norm)
    qp = rope(rmsnorm(q[..., NOPE_DIM:], q_pe_norm), pos)
    return qn, qp


def attend(qn, qp, kn, kp, v, q_pos, k_pos):
    s = (jnp.einsum('bqhd,bkhd->bhqk', qn, kn)
         + jnp.einsum('bqhr,bkr->bhqk', qp, kp)).astype(jnp.float32) * SCALE
    s = jnp.where(q_pos[:, None] >= k_pos[None, :], s, NEG)
    p = jax.nn.softmax(s, axis=-1).astype(v.dtype)
    return jnp.einsum('bhqk,bkhd->bqhd', p, v)


def forward(x, pos, conv_prev, attn_core, p):
    conv_states = []
    lat = None
    kpe = None
    for l in range(DEPTH):
        x = x + 0.5 * swiglu(rmsnorm(x, p['norm_ffn1'][l]), p['w_ffn1_gu'][l], p['w_ffn1_down'][l])
        xn = rmsnorm(x, p['norm_mix'][l])
        if l < N_A_LAYERS:
            mix, st = short_conv_mixer(xn, conv_prev[l], p['w_conv_in'][l], p['conv_w'][l], p['w_conv_out'][l])
            conv_states.append(st)
        else:
            j = l - N_A_LAYERS
            qn, qp = mla_queries(xn, pos, p['w_dq'][j], p['q_norm'][j], p['w_uq'][j],
                                 p['q_nope_norm'][j], p['q_pe_norm'][j])
            o = attn_core(qn, qp, lat, kpe)
            mix = o.reshape(o.shape[:2] + (N_HEADS * V_DIM,)) @ p['w_o'][j]
        x = x + mix
        x = x + 0.5 * swiglu(rmsnorm(x, p['norm_ffn2'][l]), p['w_ffn2_gu'][l], p['w_ffn2_down'][l])
        if l == N_A_LAYERS - 1:
            lat, kpe = shared_kv(x, pos, p['norm_kv_in'], p['w_dkv'], p['kv_norm'], p['k_pe_norm'])
    return x, jnp.stack(conv_states), lat, kpe


def setup_inputs(seed: int = 0) -> dict:
    key = jax.random.key(seed)
    ks = list(jax.random.split(key, 40))
    f32 = jnp.float32

    def w(shape, fan_in):
        return jax.random.normal(ks.pop(), shape, f32) * (fan_in ** -0.5)

    def g(shape):
        return 1.0 + 0.1 * jax.random.normal(ks.pop(), shape, f32)

    n_pages = PAST_LEN // PAGE_SIZE
    n_phys = (DEC_BATCH * n_pages * 5) // 4
    perm = jax.random.permutation(ks.pop(), n_phys)
    page_table = perm[:DEC_BATCH * n_pages].reshape(DEC_BATCH, n_pages).astype(jnp.int32)
    return {
        'x_prompt': jax.random.normal(ks.pop(), (BATCH, SEQ, D_MODEL), f32),
        'x_sample': jax.random.normal(ks.pop(), (DEC_BATCH, DEC_SEQ, D_MODEL), f32),
        'cache_latent': jax.random.normal(ks.pop(), (n_phys, PAGE_SIZE, KV_RANK), f32),
        'cache_kpe': jax.random.normal(ks.pop(), (n_phys, PAGE_SIZE, ROPE_DIM), f32),
        'state_conv': jax.random.normal(ks.pop(), (N_A_LAYERS, DEC_BATCH, CONV_W - 1, D_MODEL), f32),
        'page_table': page_table,
        'norm_ffn1': g((DEPTH, D_MODEL)),
        'w_ffn1_gu': w((DEPTH, D_MODEL, 2 * D_FF), D_MODEL),
        'w_ffn1_down': w((DEPTH, D_FF, D_MODEL), D_FF),
        'norm_mix': g((DEPTH, D_MODEL)),
        'norm_ffn2': g((DEPTH, D_MODEL)),
        'w_ffn2_gu': w((DEPTH, D_MODEL, 2 * D_FF), D_MODEL),
        'w_ffn2_down': w((DEPTH, D_FF, D_MODEL), D_FF),
        'w_conv_in': w((N_A_LAYERS, D_MODEL, 3 * D_MODEL), D_MODEL),
        'conv_w': w((N_A_LAYERS, CONV_W, D_MODEL), CONV_W),
        'w_conv_out': w((N_A_LAYERS, D_MODEL, D_MODEL), D_MODEL),
        'w_dq': w((N_B_LAYERS, D_MODEL, Q_RANK), D_MODEL),
        'q_norm': g((N_B_LAYERS, Q_RANK)),
        'w_uq': w((N_B_LAYERS, Q_RANK, N_HEADS * (NOPE_DIM + ROPE_DIM)), Q_RANK),
        'q_nope_norm': g((N_B_LAYERS, NOPE_DIM)),
        'q_pe_norm': g((N_B_LAYERS, ROPE_DIM)),
        'w_o': w((N_B_LAYERS, N_HEADS * V_DIM, D_MODEL), N_HEADS * V_DIM),
        'norm_kv_in': g((D_MODEL,)),
        'w_dkv': w((D_MODEL, KV_RANK + ROPE_DIM), D_MODEL),
        'kv_norm': g((KV_RANK,)),
        'k_pe_norm': g((ROPE_DIM,)),
        'w_uk': w((KV_RANK, N_HEADS * NOPE_DIM), KV_RANK),
        'w_uv': w((KV_RANK, N_HEADS * V_DIM), KV_RANK),
        'k_nope_norm': g((NOPE_DIM,)),
    }


def reference(x_prompt, x_sample, cache_latent, cache_kpe, state_conv, page_table,
              norm_ffn1, w_ffn1_gu, w_ffn1_down, norm_mix, norm_ffn2, w_ffn2_gu, w_ffn2_down,
              w_conv_in, conv_w, w_conv_out,
              w_dq, q_norm, w_uq, q_nope_norm, q_pe_norm, w_o,
              norm_kv_in, w_dkv, kv_norm, k_pe_norm, w_uk, w_uv, k_nope_norm):
    p = dict(norm_ffn1=norm_ffn1, w_ffn1_gu=w_ffn1_gu, w_ffn1_down=w_ffn1_down,
             norm_mix=norm_mix, norm_ffn2=norm_ffn2, w_ffn2_gu=w_ffn2_gu, w_ffn2_down=w_ffn2_down,
             w_conv_in=w_conv_in, conv_w=conv_w, w_conv_out=w_conv_out,
             w_dq=w_dq, q_norm=q_norm, w_uq=w_uq, q_nope_norm=q_nope_norm, q_pe_norm=q_pe_norm,
             w_o=w_o, norm_kv_in=norm_kv_in, w_dkv=w_dkv, kv_norm=kv_norm, k_pe_norm=k_pe_norm)

    def prompt_attn(qn, qp, lat, kpe):
        kn, v = expand_kv(lat, w_uk, w_uv, k_nope_norm)
        s_len = qn.shape[1]
        pos = jnp.arange(s_len)
        outs = []
        for i in range(s_len // Q_BLOCK):
            lo, hi = i * Q_BLOCK, (i + 1) * Q_BLOCK
            outs.append(attend(qn[:, lo:hi], qp[:, lo:hi], kn[:, :hi], kpe[:, :hi], v[:, :hi],
                               pos[lo:hi], pos[:hi]))
        return jnp.concatenate(outs, axis=1)

    def sample_attn(qn, qp, lat_new, kpe_new):
        s_new = qn.shape[1]
        n_past = page_table.shape[1] * PAGE_SIZE
        q_pos = n_past + jnp.arange(s_new)
        k_pos = jnp.arange(n_past + s_new)

        def one_seq(args):
            qn_b, qp_b, pt_b, lat_b, kpe_b = args
            lat_all = jnp.concatenate([cache_latent[pt_b].reshape(n_past, KV_RANK), lat_b], axis=0)
            kpe_all = jnp.concatenate([cache_kpe[pt_b].reshape(n_past, ROPE_DIM), kpe_b], axis=0)
            kn, v = expand_kv(lat_all, w_uk, w_uv, k_nope_norm)
            return attend(qn_b[None], qp_b[None], kn[None], kpe_all[None], v[None], q_pos, k_pos)[0]

        return lax.map(one_seq, (qn, qp, page_table, lat_new, kpe_new))

    pos_prompt = jnp.arange(x_prompt.shape[1])
    pos_sample = page_table.shape[1] * PAGE_SIZE + jnp.arange(x_sample.shape[1])
    conv_zero = jnp.zeros((N_A_LAYERS, x_prompt.shape[0], CONV_W - 1, D_MODEL), x_prompt.dtype)

    y_prompt, conv_p, lat_p, kpe_p = forward(x_prompt, pos_prompt, conv_zero, prompt_attn, p)
    y_sample, conv_s, lat_s, kpe_s = forward(x_sample, pos_sample, state_conv, sample_attn, p)
    return (y_prompt, y_sample, conv_p, conv_s, lat_p, kpe_p, lat_s, kpe_s)
```

```python
import numpy as np
from contextlib import ExitStack
import ml_dtypes
import concourse.bass as bass
import concourse.mybir as mybir
from concourse.bass_utils import run_bass_kernel_spmd

F32 = mybir.dt.float32
BF16 = mybir.dt.bfloat16
I32 = mybir.dt.int32
AF = mybir.ActivationFunctionType
ALU = mybir.AluOpType
AX = mybir.AxisListType

D = 1024
DFF = 2816
NJ = DFF // 128
EPS = 1e-6
SCALE = 192.0 ** -0.5
NPAGE = 64
NT = 5


class Res:
    __slots__ = ("name", "w", "r")

    def __init__(self, name=""):
        self.name = name
        self.w = None
        self.r = []


class DmaSem:
    def __init__(self, sem):
        self.sem = sem
        self.val = 0


class Prog:
    ENGS = ("pe", "act", "dve", "pool", "sp")

    def __init__(self, nc):
        self.nc = nc
        self.streams = {e: [] for e in self.ENGS}
        self.sem = {}
        self.cnt = {e: 0 for e in self.ENGS}
        self.waited = {e: {} for e in self.ENGS}
        for e in self.ENGS:
            self.sem[e] = nc.alloc_semaphore(name=f"s_{e}")

    def dma_sem(self, name=None):
        self._n = getattr(self, "_n", 0) + 1
        return DmaSem(self.nc.alloc_semaphore(name=f"{name}_{self._n}"))

    def _need(self, eng, tokens):
        wd = self.waited[eng]
        best = {}
        for t in tokens:
            if t is None:
                continue
            sem, val = t
            k = id(sem)
            if wd.get(k, 0) >= val:
                continue
            if k not in best or best[k][1] < val:
                best[k] = (sem, val)
        out = []
        for k, (sem, val) in best.items():
            wd[k] = val
            out.append((sem, val))
        return out

    @staticmethod
    def _deps(reads, writes):
        toks = []
        for r in reads:
            toks.append(r.w)
        for w in writes:
            toks.append(w.w)
            toks.extend(w.r)
        return toks

    def op(self, eng, fn, reads=(), writes=(), inc=True, touch=()):
        waits = self._need(eng, self._deps(reads, writes))
        tok = (self.sem[eng], self.cnt[eng] + 1)
        if inc:
            self.cnt[eng] += 1
        self.streams[eng].append((waits, fn, (self.sem[eng], 1) if inc else None))
        for r in reads:
            r.r.append(tok)
        for w in writes:
            w.w = tok
            w.r = []
        for w in touch:
            w.w = tok
        return tok

    def dma(self, q, fn, dsem, reads=(), writes=(), inc=16, touch=()):
        waits = self._need(q, self._deps(reads, writes))
        dsem.val += inc
        tok = (dsem.sem, dsem.val)
        self.streams[q].append((waits, fn, (dsem.sem, inc)))
        for r in reads:
            r.r.append(tok)
        for w in writes:
            w.w = tok
            w.r = []
        for w in touch:
            w.w = tok
        return tok

    def wait_all(self, eng, tokens):
        waits = self._need(eng, tokens)
        if waits:
            self.streams[eng].append((waits, None, None))

    def emit(self):
        nc = self.nc
        streams = self.streams
        self.streams = {e: [] for e in self.ENGS}

        def run(engobj, lst):
            for waits, fn, inc in lst:
                for sem, val in waits:
                    engobj.wait_ge(sem, val)
                if fn is not None:
                    inst = fn(engobj)
                    if inc is not None:
                        inst.then_inc(inc[0], inc[1])

        with nc.Block() as block:
            @block.tensor
            def _(e):
                run(e, streams["pe"])

            @block.scalar
            def _(e):
                run(e, streams["act"])

            @block.vector
            def _(e):
                run(e, streams["dve"])

            @block.gpsimd
            def _(e):
                run(e, streams["pool"])

            @block.sync
            def _(e):
                run(e, streams["sp"])


class Ring:
    def __init__(self, P, nc, name, shape, dt, n, psum=False, dma=False, alloc=None):
        self.items = []
        for i in range(n):
            if alloc is not None:
                t = alloc(f"{name}{i}", shape, dt)
            else:
                t = (nc.alloc_psum_tensor if psum else nc.alloc_sbuf_tensor)(f"{name}{i}", shape, dt)
            self.items.append((t, Res(f"{name}{i}"), P.dma_sem(f"ds_{name}{i}") if dma else None))
        self.i = 0

    def next(self):
        it = self.items[self.i % len(self.items)]
        self.i += 1
        return it


_NC_CACHE = {}


def build_program(n_phys=10240, stop=""):
    nc = bass.Bass("TRN2", target_bir_lowering=False)
    P = Prog(nc)

    def din(name, shape, dt=F32):
        return nc.dram_tensor(name, list(shape), dt, kind="ExternalInput").ap()

    def dout(name, shape, dt=F32):
        return nc.dram_tensor(name, list(shape), dt, kind="ExternalOutput").ap()

    xp = din("xp", [4, 514, D])
    xs = din("xs", [64, D])
    sconv = din("sconv", [32, D])
    ptab = din("ptab", [1, 16 * NPAGE], I32)
    HALF = (n_phys // 2) * 128
    cache_lat = [din(f"cache_lat{i}", [HALF, 256]) for i in range(2)]
    cache_kpe = [din(f"cache_kpe{i}", [HALF, 64]) for i in range(2)]
    w_ffn1_gu = din("w_ffn1_gu", [2, D, 2 * DFF])
    w_ffn1_down = din("w_ffn1_down", [2, DFF, D])
    w_ffn2_gu = din("w_ffn2_gu", [2, D, 2 * DFF])
    w_ffn2_down = din("w_ffn2_down", [2, DFF, D])
    w_conv_in = din("w_conv_in", [1, D, 3 * D])
    w_conv_out = din("w_conv_out", [1, D, D])
    w_dq = din("w_dq", [1, D, 512])
    w_uq = din("w_uq", [1, 512, 1536])
    w_o = din("w_o", [1, D, D])
    w_dkv = din("w_dkv", [D, 320])
    w_uk = din("w_uk", [256, D])
    w_uv = din("w_uv", [256, D])
    gvec_d = din("gvec", [88, 128])
    q_norm = din("q_norm", [1, 512])
    q_nope_norm = din("q_nope_norm", [1, 128])
    q_pe_norm = din("q_pe_norm", [1, 64])
    kv_norm = din("kv_norm", [256])
    k_pe_norm = din("k_pe_norm", [64])
    ident_d = din("ident", [128, 128])
    iota_d = din("iota", [128, 1], I32)
    cs_p_d = din("cs_p", [4, 512, 64])
    cs_s_d = din("cs_s", [64, 64])
    mask_p_d = din("mask_p", [128, 16 * 512], BF16)
    mask_n_d = din("mask_n", [64, 512], BF16)

    y_p = dout("y_p", [4, 512, D])
    y_s = dout("y_s", [64, D])
    cso_p = dout("cso_p", [4, 2, D])
    cso_s = dout("cso_s", [32, D])
    lat_p = dout("lat_p", [4, 512, 256])
    kpe_p = dout("kpe_p", [4, 512, 64])
    lat_s = dout("lat_s", [64, 256])
    kpe_s = dout("kpe_s", [64, 64])

    xsc = nc.dram_tensor("xsc", [NT, 128, 8 * 512], F32).ap()
    sendb = [nc.dram_tensor(f"sendb{c}", [128, 1024], F32) for c in range(3)]
    recvb = [nc.dram_tensor(f"recvb{c}", [512, 1024], F32) for c in range(3)]
    r_xsc = [Res(f"xsc{t}") for t in range(NT)]
    r_sendb = Res("sendb")
    r_recvb = Res("recvb")
    r_out = Res("out")
    out_tokens = []

    def sb(name, shape, dt=F32):
        return nc.alloc_sbuf_tensor(name, list(shape), dt)

    ident_f = sb("ident_f", [128, 128]); r_ident_f = Res()
    ident_b = sb("ident_b", [128, 128], BF16); r_ident_b = Res()
    ones_b = sb("ones_b", [128, 128], BF16); r_ones = Res()
    eps_t = sb("eps_t", [128, 1]); r_eps = Res()
    iota_t = sb("iota_t", [128, 1], I32); r_iota = Res()
    gv = sb("gv", [128, 88]); r_gv = Res()
    qn_bc = sb("qn_bc", [128, 512]); qnn_bc = sb("qnn_bc", [128, 128]); qpn_bc = sb("qpn_bc", [128, 64])
    kvn_bc = sb("kvn_bc", [128, 256]); kpn_bc = sb("kpn_bc", [128, 64]); r_bc = Res()
    r_oT = [Res(f"oT{t}") for t in range(NT)]
    QNS = sb("QNS", [128, 8, 64], BF16); r_QNS = Res("QNS")
    qsc_n = nc.dram_tensor("qsc_n", [8, 128, 2048], BF16).ap(); r_qsc_n = [Res(f"qscn{t}") for t in range(4)]
    qsc_p = nc.dram_tensor("qsc_p", [4, 128, 2048], BF16).ap(); r_qsc_p = [Res(f"qscp{t}") for t in range(4)]
    qpT_s = sb("qpT_s", [64, 8, 64], BF16); r_qpT_s = Res()
    latn_aug = sb("latn_aug", [64, 257], BF16); r_latn = Res()
    latnT = sb("latnT", [128, 3, 64], BF16); r_latnT = Res()
    ptab_t = sb("ptab_t", [128, 16 * NPAGE], I32); r_ptab = Res()

    PSALL = [nc.alloc_psum_tensor(f"ps{i}", [128, 512], F32) for i in range(8)]
    PSRES = [Res(f"ps{i}") for i in range(8)]

    class PsRing:
        def __init__(self, idxs):
            self.idxs = idxs
            self.i = 0

        def next(self):
            k = self.idxs[self.i % len(self.idxs)]
            self.i += 1
            return PSALL[k], PSRES[k], None

    PS = PsRing(list(range(8)))
    DS = [P.dma_sem(f"dso{i}") for i in range(6)]
    dsi = [0]

    def ods():
        dsi[0] += 1
        return DS[dsi[0] % len(DS)]

    s0 = P.dma_sem("setup")
    P.dma("sp", lambda e: e.dma_start(out=ident_f[:], in_=ident_d), s0, writes=[r_ident_f])
    P.dma("sp", lambda e: e.dma_start(out=iota_t[:], in_=iota_d), s0, writes=[r_iota])
    P.dma("sp", lambda e: e.dma_start(out=qn_bc[:], in_=q_norm[0].partition_broadcast(128)), s0, writes=[r_bc])
    P.dma("sp", lambda e: e.dma_start(out=qnn_bc[:], in_=q_nope_norm[0].partition_broadcast(128)), s0, writes=[r_bc])
    P.dma("sp", lambda e: e.dma_start(out=qpn_bc[:], in_=q_pe_norm[0].partition_broadcast(128)), s0, writes=[r_bc])
    P.dma("sp", lambda e: e.dma_start(out=kvn_bc[:], in_=kv_norm.partition_broadcast(128)), s0, writes=[r_bc])
    P.dma("sp", lambda e: e.dma_start(out=kpn_bc[:], in_=k_pe_norm.partition_broadcast(128)), s0, writes=[r_bc])
    P.dma("sp", lambda e: e.dma_start(out=ptab_t[:], in_=ptab[0].partition_broadcast(128)), s0, writes=[r_ptab])
    gv_in = sb("gv_in", [88, 128]); r_gv_in = Res()
    P.dma("sp", lambda e: e.dma_start(out=gv_in[:], in_=gvec_d), s0, writes=[r_gv_in])
    for _r in (r_ident_f, r_iota, r_bc, r_ptab, r_gv_in):
        _r.w = (s0.sem, s0.val)
    P.op("dve", lambda e: e.tensor_copy(out=ident_b[:], in_=ident_f[:]), reads=[r_ident_f], writes=[r_ident_b])
    P.op("dve", lambda e: e.memset(ones_b[:], 1.0), writes=[r_ones])
    P.op("dve", lambda e: e.memset(eps_t[:], EPS), writes=[r_eps])
    P.op("dve", lambda e: e.memset(latn_aug[:, 256:257], 1.0), writes=[r_latn])
    pst, pr, _ = PS.next()
    P.op("pe", lambda e: e.transpose(out=pst[:, 0:88], in_=gv_in[:], identity=ident_f[0:88, 0:88]),
         reads=[r_gv_in, r_ident_f], writes=[pr])
    P.op("dve", lambda e: e.tensor_copy(out=gv[:], in_=pst[:, 0:88]), reads=[pr], writes=[r_gv])
    GV = dict(nf1=(0, 8), nm=(16, 24), nf2=(32, 40), nkv=48, cw=(56, 64, 72), kn=80)

    def psb(pt):
        return pt[:, :].bitcast(BF16)

    def rstd_from_ss(ss_ap, out_ap, n_over, res_in, res_out, np_=128):
        P.op("act", lambda e: e.activation(out=out_ap, in_=ss_ap, func=AF.Sqrt, bias=eps_t[0:np_, 0:1], scale=1.0 / n_over),
             reads=res_in + [r_eps], writes=[res_out])
        P.op("dve", lambda e: e.reciprocal(out=out_ap, in_=out_ap), reads=[res_out], writes=[res_out])

    uniq = [0]

    def make_common(es):
        uniq[0] += 1
        sfx = f"_{uniq[0]}"

        def sbx(name, shape, dt=F32):
            return es.enter_context(nc.sbuf_tensor(name + sfx, list(shape), dt))
        C = {}
        C["sbx"] = sbx
        C["wA"] = Ring(P, nc, "wA", [128, 8, 512], BF16, 4, dma=True, alloc=sbx)
        C["X"] = sbx("X", [128, 8, 512]); C["r_X"] = [Res(f"X{k}") for k in range(8)]
        C["XH"] = sbx("XH", [128, 8, 2]); C["r_XH"] = Res("XH")
        C["XN"] = sbx("XN", [128, 8, 514], BF16); C["r_XN"] = Res("XN")
        C["RSTD"] = sbx("RSTD", [128, 514]); C["r_RSTD"] = Res("RSTD")
        C["Hh"] = sbx("Hh", [128, NJ, 514], BF16); C["r_H"] = [Res(f"H{j}") for j in range(NJ)]
        C["SG"] = Ring(P, nc, "SG", [128, 514], F32, 2, alloc=sbx)
        return C

    def load_wA(C, src_ap, ncols=512, nk=8):
        t, r, s = C["wA"].next()
        P.dma("pool", lambda e: e.dma_start(out=t[:, 0:nk, 0:ncols], in_=src_ap), s, writes=[r])
        return t, r

    def wview(w2d, c0, ncols):
        return w2d.rearrange("(k p) c -> p k c", p=128)[:, :, c0:c0 + ncols]

    def segs(N, halo):
        s = [(0, N)]
        if halo:
            s.append((512, 2))
        return s

    def rmsnorm_fm(C, N, halo, gcol):
        X, XH, XN, RSTD, Hh = C["X"], C["XH"], C["XN"], C["RSTD"], C["Hh"]
        r_X, r_XH, r_XN, r_RSTD, r_H = C["r_X"], C["r_XH"], C["r_XN"], C["r_RSTD"], C["r_H"]
        for (c0, n) in segs(N, halo):
            src = (lambda k, n=n: X[:, k, 0:n]) if c0 == 0 else (lambda k, n=n: XH[:, k, 0:n])
            rsrc = r_X if c0 == 0 else [r_XH] * 8
            for k in range(8):
                P.op("dve", lambda e, k=k, src=src, c0=c0, n=n: e.tensor_tensor(out=Hh[:, k, c0:c0 + n], in0=src(k), in1=src(k), op=ALU.mult),
                     reads=[rsrc[k]], writes=[r_H[k]])
            pt, pr, _ = PS.next()
            for k in range(8):
                P.op("pe", lambda e, k=k, pt=pt, c0=c0, n=n: e.matmul(pt[:, 0:n], ones_b[:], Hh[:, k, c0:c0 + n], start=(k == 0), stop=(k == 7)),
                     reads=r_H[0:8] + [r_ones] if k == 0 else [], writes=[pr] if k == 0 else [], inc=(k == 7))
            rstd_from_ss(pt[:, 0:n], RSTD[:, c0:c0 + n], float(D), [pr], r_RSTD)
            for k in range(8):
                P.op("dve", lambda e, k=k, src=src, c0=c0, n=n: e.scalar_tensor_tensor(out=XN[:, k, c0:c0 + n], in0=src(k), scalar=gv[:, gcol + k:gcol + k + 1],
                                                                                         in1=RSTD[:, c0:c0 + n], op0=ALU.mult, op1=ALU.mult),
                     reads=[rsrc[k], r_gv, r_RSTD], writes=[r_XN])

    def ffn(C, N, halo, w_gu, w_down):
        X, XH, XN, Hh, SG = C["X"], C["XH"], C["XN"], C["Hh"], C["SG"]
        r_X, r_XH, r_XN, r_H = C["r_X"], C["r_XH"], C["r_XN"], C["r_H"]
        sg = segs(N, halo)
        for jb in range(0, NJ, 4):
            nj = min(4, NJ - jb)
            tg, rg = load_wA(C, wview(w_gu, jb * 128, nj * 128), nj * 128)
            tu, ru = load_wA(C, wview(w_gu, DFF + jb * 128, nj * 128), nj * 128)
            for jj in range(nj):
                j = jb + jj
                for (c0, n) in sg:
                    pg, prg, _ = PS.next()
                    pu, pru, _ = PS.next()
                    for (pt, pr, tw, rw) in ((pg, prg, tg, rg), (pu, pru, tu, ru)):
                        for k in range(8):
                            P.op("pe", lambda e, pt=pt, tw=tw, k=k, jj=jj, c0=c0, n=n: e.matmul(pt[:, 0:n], tw[:, k, jj * 128:(jj + 1) * 128], XN[:, k, c0:c0 + n],
                                                                                                  start=(k == 0), stop=(k == 7)),
                                 reads=[rw, r_XN] if k == 0 else [], writes=[pr] if k == 0 else [], inc=(k == 7))
                    st, sr, _ = SG.next()
                    P.op("act", lambda e, st=st, pg=pg, n=n: e.activation(out=st[:, 0:n], in_=pg[:, 0:n], func=AF.Silu), reads=[prg], writes=[sr])
                    P.op("dve", lambda e, st=st, pu=pu, j=j, c0=c0, n=n: e.tensor_tensor(out=Hh[:, j, c0:c0 + n], in0=st[:, 0:n], in1=pu[:, 0:n], op=ALU.mult),
                         reads=[sr, pru], writes=[r_H[j]])
        for mh in range(2):
            accs = [[PS.next() for _ in sg] for _ in range(4)]
            for jb in range(0, NJ, 4):
                nj = min(4, NJ - jb)
                src = w_down[jb * 128:(jb + nj) * 128, mh * 512:(mh + 1) * 512].rearrange("(j p) m -> p j m", p=128)
                tw, rw = load_wA(C, src, 512, nj)
                for jj in range(nj):
                    j = jb + jj
                    for mm in range(4):
                        for si, (c0, n) in enumerate(sg):
                            pt, pr, _ = accs[mm][si]
                            first = (j == 0)
                            last = (j == NJ - 1)
                            P.op("pe", lambda e, pt=pt, tw=tw, jj=jj, mm=mm, j=j, c0=c0, n=n, first=first, last=last:
                                 e.matmul(pt[:, 0:n], tw[:, jj, mm * 128:(mm + 1) * 128], Hh[:, j, c0:c0 + n], start=first, stop=last),
                                 reads=([rw] if jj == 0 else []) + [r_H[j]], writes=[pr] if first else [], touch=[] if first else [pr],
                                 inc=(last or (jj == nj - 1 and mm == 3 and si == len(sg) - 1)))
            for mm in range(4):
                m = mh * 4 + mm
                for si, (c0, n) in enumerate(sg):
                    pt, pr, _ = accs[mm][si]
                    if c0 == 0:
                        P.op("dve", lambda e, pt=pt, m=m, n=n: e.scalar_tensor_tensor(out=X[:, m, 0:n], in0=pt[:, 0:n], scalar=0.5, in1=X[:, m, 0:n],
                                                                                       op0=ALU.mult, op1=ALU.add), reads=[pr, r_X[m]], writes=[r_X[m]])
                    else:
                        P.op("dve", lambda e, pt=pt, m=m, n=n: e.scalar_tensor_tensor(out=XH[:, m, 0:n], in0=pt[:, 0:n], scalar=0.5, in1=XH[:, m, 0:n],
                                                                                       op0=ALU.mult, op1=ALU.add), reads=[pr, r_XH], writes=[r_XH])

    def rope_tm(src_ap, dst_ap, nh, csv, np_, rd, wr, tmp_ap, r_tmp):
        cosb = csv[:, 0:32].unsqueeze(1).to_broadcast([np_, nh, 32])
        sinb = csv[:, 32:64].unsqueeze(1).to_broadcast([np_, nh, 32])
        x1, x2 = src_ap[:, :, 0:32], src_ap[:, :, 32:64]
        o1, o2 = dst_ap[:, :, 0:32], dst_ap[:, :, 32:64]
        t1, t2 = tmp_ap[:, :, 0:32], tmp_ap[:, :, 32:64]
        P.op("dve", lambda e: e.tensor_tensor(out=t1, in0=x2, in1=sinb, op=ALU.mult), reads=rd, writes=[r_tmp])
        P.op("dve", lambda e: e.tensor_tensor(out=t2, in0=x1, in1=sinb, op=ALU.mult), reads=rd + [r_tmp], writes=[r_tmp])
        P.op("dve", lambda e: e.tensor_tensor(out=t1, in0=x1, in1=cosb, op=ALU.mult) if False else e.tensor_tensor(out=o1, in0=x1, in1=cosb, op=ALU.mult), reads=rd, writes=wr)
        P.op("dve", lambda e: e.tensor_tensor(out=o2, in0=x2, in1=cosb, op=ALU.mult), reads=rd + wr, writes=wr)
        P.op("dve", lambda e: e.tensor_tensor(out=o1, in0=o1, in1=t1, op=ALU.subtract), reads=[r_tmp] + wr, writes=wr)
        P.op("dve", lambda e: e.tensor_tensor(out=o2, in0=o2, in1=t2, op=ALU.add), reads=[r_tmp] + wr, writes=wr)

    with ExitStack() as es:
        C = make_common(es)
        sbx = C["sbx"]
        X, XH, XN, Hh = C["X"], C["XH"], C["XN"], C["Hh"]
        r_X, r_XH, r_XN, r_H = C["r_X"], C["r_XH"], C["r_XN"], C["r_H"]
        wUQ = sbx("wUQ", [128, 4, 1536], BF16); r_wUQ = Res(); s_wUQ = P.dma_sem("wUQ")
        XINr = Ring(P, nc, "XIN", [128, D], F32, 2, dma=True, alloc=sbx)
        XHI = sbx("XHI", [32, D]); r_XHI = Res("XHI"); s_XHI = P.dma_sem("XHI")
        UEXT = sbx("UEXT", [128, 8, 516]); r_UEXT = [Res(f"U{k}") for k in range(8)]
        HS = Ring(P, nc, "HS", [128, 514], F32, 2, alloc=sbx)
        YC = Ring(P, nc, "YC", [128, 512], F32, 2, alloc=sbx)
        CSO = sbx("CSO", [128, 8, 32]); r_CSO = Res("CSO")
        CSO2 = sbx("CSO2", [32, D]); r_CSO2 = Res("CSO2"); s_CSO2 = P.dma_sem("CSO2")
        CKV = sbx("CKV", [128, 320]); r_CKV = Res("CKV")
        SQT = sbx("SQT", [128, 1536]); r_SQT = Res("SQT")
        SS = sbx("SS", [128, 16]); r_SS = Res("SS")
        RS = sbx("RS", [128, 16]); r_RS = Res("RS")
        LATO = Ring(P, nc, "LATO", [128, 320], F32, 2, dma=True, alloc=sbx)
        TMP = sbx("TMP", [128, 1024]); r_TMP = Res("TMP")
        TMP2 = sbx("TMP2", [128, 512]); r_TMP2 = Res("TMP2")
        LB = sbx("LB", [128, 384], BF16); r_LB = Res("LB")
        ST = sbx("ST", [128, 3, 512], BF16); r_ST = Res("ST"); s_ST = P.dma_sem("ST")
        CS = sbx("CS", [128, 4, 64]); r_CS = Res("CS"); s_CS = P.dma_sem("CS")
        CQ = sbx("CQ", [128, 512]); r_CQ = Res("CQ")
        CQB = sbx("CQB", [128, 512], BF16); r_CQB = Res("CQB")
        CQT = sbx("CQT", [128, 4, 512], BF16); r_CQT = Res("CQT")
        QS = sbx("QS", [128, 1536]); r_QS = Res("QS")
        QNB = sbx("QNB", [128, 1024], BF16); r_QNB = Res("QNB")
        QPF = sbx("QPF", [128, 512]); r_QPF = Res("QPF")
        QPB = sbx("QPB", [128, 512], BF16); r_QPB = Res("QPB")
        QTN = sbx("QTN", [128, 8, 512], BF16); r_QTN = Res("QTN"); s_QTN = P.dma_sem("QTN")
        QTP = sbx("QTP", [128, 4, 512], BF16); r_QTP = Res("QTP"); s_QTP = P.dma_sem("QTP"); s_XS = P.dma_sem("XS")

        def ZB(k, N):
            return Hh[:, 8 + k, 0:N]

        def load_x(t):
            N = 512 if t < 4 else 64
            nb, bp = (4, 128) if t < 4 else (1, 64)
            blocks = []
            for b in range(nb):
                xt, xr, xsem = XINr.next()
                if t < 4:
                    P.dma("sp", lambda e, xt=xt, b=b: e.dma_start(out=xt[:, :], in_=xp[t, 2 + b * 128:2 + (b + 1) * 128, :]), xsem, writes=[xr])
                else:
                    P.dma("sp", lambda e, xt=xt: e.dma_start(out=xt[0:64, :], in_=xs), xsem, writes=[xr])
                for half in range(2):
                    pt, pr, _ = PS.next()
                    for kk in range(4):
                        k = half * 4 + kk
                        P.op("pe", lambda e, pt=pt, xt=xt, k=k, kk=kk: e.transpose(out=pt[:, kk * 128:kk * 128 + bp], in_=xt[0:bp, k * 128:(k + 1) * 128], identity=ident_f[0:bp, 0:bp]),
                             reads=[xr, r_ident_f] if kk == 0 else [], writes=[pr] if kk == 0 else [], inc=(kk == 3))
                    P.op("act", lambda e, pt=pt, half=half, b=b: e.activation(out=X[:, half * 4:half * 4 + 4, b * bp:(b + 1) * bp],
                                                                             in_=pt[:, :].rearrange("p (k n) -> p k n", n=128)[:, :, 0:bp], func=AF.Copy),
                         reads=[pr], writes=r_X[half * 4:half * 4 + 4])
            if t < 4:
                P.dma("sp", lambda e: e.dma_start(out=XHI[0:2, :], in_=xp[t, 0:2, :]), s_XHI, writes=[r_XHI])
                pt, pr, _ = PS.next()
                for k in range(8):
                    P.op("pe", lambda e, pt=pt, k=k: e.transpose(out=pt[:, 2 * k:2 * k + 2], in_=XHI[0:2, k * 128:(k + 1) * 128], identity=ident_f[0:2, 0:2]),
                         reads=[r_XHI, r_ident_f] if k == 0 else [], writes=[pr] if k == 0 else [], inc=(k == 7))
                P.op("act", lambda e, pt=pt: e.activation(out=XH[:, :, :], in_=pt[:, 0:16].rearrange("p (k c) -> p k c", c=2), func=AF.Copy), reads=[pr], writes=[r_XH])
            else:
                P.dma("sp", lambda e: e.dma_start(out=XHI[0:32, :], in_=sconv), s_XHI, writes=[r_XHI])
                for k in range(8):
                    pt, pr, _ = PS.next()
                    P.op("pe", lambda e, pt=pt, k=k: e.transpose(out=pt[:, 0:32], in_=XHI[0:32, k * 128:(k + 1) * 128], identity=ident_f[0:32, 0:32]),
                         reads=[r_XHI, r_ident_f], writes=[pr])
                    P.op("act", lambda e, pt=pt, k=k: e.activation(out=UEXT[:, k, 0:96].rearrange("p (b j) -> p b j", j=6)[:, :, 0:2],
                                                                   in_=pt[:, 0:32].rearrange("p (b j) -> p b j", j=2), func=AF.Copy), reads=[pr], writes=[r_UEXT[k]])

        def conv_mixer(t):
            N = 512 if t < 4 else 64
            halo = t < 4
            nseq, L = (1, 512) if t < 4 else (16, 4)
            W = w_conv_in[0].rearrange("(k p) c -> p k c", p=128)
            rmsnorm_fm(C, N, halo, GV["nm"][0])

            def uview(k, lo, hi):
                return UEXT[:, k, 0:nseq * (L + 2)].rearrange("p (b j) -> p b j", j=L + 2)[:, :, lo:hi]

            def nview(ap2d):
                return ap2d.rearrange("p (b j) -> p b j", j=L)

            cw = GV["cw"]
            for m in range(8):
                tw, rw = load_wA(C, W[:, :, m * 128:(m + 1) * 128], 128)
                tw2, rw2 = load_wA(C, W[:, :, D + m * 128:D + (m + 1) * 128], 128)
                tw3, rw3 = load_wA(C, W[:, :, 2 * D + m * 128:2 * D + (m + 1) * 128], 128)
                pb, prb, _ = PS.next()
                pc, prc, _ = PS.next()
                ph, prh, _ = PS.next()
                for (pt, pr, twx, rwx) in ((pb, prb, tw, rw), (pc, prc, tw2, rw2), (ph, prh, tw3, rw3)):
                    for k in range(8):
                        P.op("pe", lambda e, pt=pt, twx=twx, k=k: e.matmul(pt[:, 0:N], twx[:, k, 0:128], XN[:, k, 0:N], start=(k == 0), stop=(k == 7)),
                             reads=[rwx, r_XN] if k == 0 else [], writes=[pr] if k == 0 else [], inc=(k == 7))
                hs, hr, _ = HS.next()
                P.op("act", lambda e, hs=hs, ph=ph: e.activation(out=hs[:, 0:N], in_=ph[:, 0:N], func=AF.Copy), reads=[prh], writes=[hr])
                P.op("dve", lambda e, hs=hs, pc=pc, m=m: e.tensor_tensor(out=uview(m, 2, L + 2), in0=nview(pc[:, 0:N]), in1=nview(hs[:, 0:N]), op=ALU.mult),
                     reads=[prc, hr], writes=[r_UEXT[m]])
                if halo:
                    pc2, prc2, _ = PS.next()
                    ph2, prh2, _ = PS.next()
                    for (pt, pr, twx, rwx) in ((pc2, prc2, tw2, rw2), (ph2, prh2, tw3, rw3)):
                        for k in range(8):
                            P.op("pe", lambda e, pt=pt, twx=twx, k=k: e.matmul(pt[:, 0:2], twx[:, k, 0:128], XN[:, k, 512:514], start=(k == 0), stop=(k == 7)),
                                 reads=[rwx, r_XN] if k == 0 else [], writes=[pr] if k == 0 else [], inc=(k == 7))
                    hs2, hr2, _ = HS.next()
                    P.op("act", lambda e, hs2=hs2, ph2=ph2: e.activation(out=hs2[:, 0:2], in_=ph2[:, 0:2], func=AF.Copy), reads=[prh2], writes=[hr2])
                    P.op("dve", lambda e, hs2=hs2, pc2=pc2, m=m: e.tensor_tensor(out=UEXT[:, m, 0:2], in0=pc2[:, 0:2], in1=hs2[:, 0:2], op=ALU.mult),
                         reads=[prc2, hr2, r_UEXT[m]], writes=[r_UEXT[m]])
                P.op("act", lambda e, m=m: e.activation(out=CSO[:, m, 0:2 * nseq].rearrange("p (b j) -> p b j", j=2), in_=uview(m, L, L + 2), func=AF.Copy),
                     reads=[r_UEXT[m]], writes=[r_CSO])
                yc, yr, _ = YC.next()
                P.op("dve", lambda e, yc=yc, m=m: e.tensor_scalar(out=nview(yc[:, 0:N]), in0=uview(m, 0, L), scalar1=gv[:, cw[0] + m:cw[0] + m + 1], scalar2=None, op0=ALU.mult),
                     reads=[r_UEXT[m], r_gv], writes=[yr])
                for jx in (1, 2):
                    P.op("dve", lambda e, yc=yc, m=m, jx=jx: e.scalar_tensor_tensor(out=nview(yc[:, 0:N]), in0=uview(m, jx, L + jx), scalar=gv[:, cw[jx] + m:cw[jx] + m + 1],
                                                                                     in1=nview(yc[:, 0:N]), op0=ALU.mult, op1=ALU.add),
                         reads=[r_UEXT[m], r_gv, yr], writes=[yr])
                P.op("dve", lambda e, yc=yc, pb=pb, m=m: e.tensor_tensor(out=ZB(m, N), in0=yc[:, 0:N], in1=pb[:, 0:N], op=ALU.mult),
                     reads=[yr, prb], writes=[r_H[8 + m]])
            n = 2 * nseq
            for half in range(2):
                pt, pr, _ = PS.next()
                for kk in range(4):
                    k = half * 4 + kk
                    P.op("pe", lambda e, pt=pt, k=k, kk=kk: e.transpose(out=pt[0:n, kk * 128:(kk + 1) * 128], in_=CSO[:, k, 0:n], identity=ident_f[:, :]),
                         reads=[r_CSO, r_ident_f] if kk == 0 else [], writes=[pr] if kk == 0 else [], inc=(kk == 3))
                P.op("act", lambda e, pt=pt, half=half: e.activation(out=CSO2[0:n, half * 512:(half + 1) * 512], in_=pt[0:n, :], func=AF.Copy), reads=[pr], writes=[r_CSO2])
            dst = cso_p[t] if t < 4 else cso_s
            out_tokens.append(P.dma("sp", lambda e: e.dma_start(out=dst, in_=CSO2[0:n, :]), s_CSO2, reads=[r_CSO2], writes=[r_out]))
            Wo = w_conv_out[0]
            for mb in range(0, 8, 4):
                tw, rw = load_wA(C, wview(Wo, mb * 128, 512))
                for mm in range(4):
                    m = mb + mm
                    pt, pr, _ = PS.next()
                    for k in range(8):
                        P.op("pe", lambda e, pt=pt, tw=tw, k=k, mm=mm: e.matmul(pt[:, 0:N], tw[:, k, mm * 128:(mm + 1) * 128], ZB(k, N), start=(k == 0), stop=(k == 7)),
                             reads=[rw] + r_H[8:16] if k == 0 else [], writes=[pr] if k == 0 else [], inc=(k == 7))
                    P.op("dve", lambda e, pt=pt, m=m: e.tensor_tensor(out=X[:, m, 0:N], in0=pt[:, 0:N], in1=X[:, m, 0:N], op=ALU.add), reads=[pr, r_X[m]], writes=[r_X[m]])

        def load_cs(t):
            if t < 4:
                P.dma("sp", lambda e: e.dma_start(out=CS[:, :, :], in_=cs_p_d[t].rearrange("(b p) c -> p b c", p=128)), s_CS, writes=[r_CS])
            else:
                P.dma("sp", lambda e: e.dma_start(out=CS[0:64, 0, :], in_=cs_s_d), s_CS, writes=[r_CS])

        def shared_kv(t):
            N = 512 if t < 4 else 64
            nb, bp = (4, 128) if t < 4 else (1, 64)
            rmsnorm_fm(C, N, False, GV["nkv"])
            tw, rw = load_wA(C, w_dkv.rearrange("(k p) c -> p k c", p=128), 320)
            load_cs(t)
            for b in range(nb):
                pt, pr, _ = PS.next()
                for k in range(8):
                    P.op("pe", lambda e, pt=pt, k=k, b=b: e.matmul(pt[0:bp, 0:320], XN[:, k, b * bp:(b + 1) * bp], tw[:, k, 0:320], start=(k == 0), stop=(k == 7)),
                         reads=[rw, r_XN] if k == 0 else [], writes=[pr] if k == 0 else [], inc=(k == 7))
                P.op("act", lambda e, pt=pt: e.activation(out=CKV[0:bp, :], in_=pt[0:bp, 0:320], func=AF.Copy), reads=[pr], writes=[r_CKV])
                P.op("dve", lambda e: e.tensor_tensor(out=SQT[0:bp, 0:320], in0=CKV[0:bp, :], in1=CKV[0:bp, :], op=ALU.mult), reads=[r_CKV], writes=[r_SQT])
                P.op("dve", lambda e: e.tensor_reduce(out=SS[0:bp, 0:1], in_=SQT[0:bp, 0:256], axis=AX.X, op=ALU.add), reads=[r_SQT], writes=[r_SS])
                P.op("dve", lambda e: e.tensor_reduce(out=SS[0:bp, 1:2], in_=SQT[0:bp, 256:320], axis=AX.X, op=ALU.add), reads=[r_SQT, r_SS], writes=[r_SS])
                rstd_from_ss(SS[0:bp, 0:1], RS[0:bp, 0:1], 256.0, [r_SS], r_RS, bp)
                rstd_from_ss(SS[0:bp, 1:2], RS[0:bp, 1:2], 64.0, [r_SS, r_RS], r_RS, bp)
                lo, lr, los = LATO.next()
                P.op("dve", lambda e, lo=lo: e.scalar_tensor_tensor(out=lo[0:bp, 0:256], in0=CKV[0:bp, 0:256], scalar=RS[0:bp, 0:1], in1=kvn_bc[0:bp, :],
                                                                  op0=ALU.mult, op1=ALU.mult), reads=[r_CKV, r_RS, r_bc], writes=[lr])
                P.op("dve", lambda e: e.scalar_tensor_tensor(out=TMP[0:bp, 0:64], in0=CKV[0:bp, 256:320], scalar=RS[0:bp, 1:2], in1=kpn_bc[0:bp, :],
                                                           op0=ALU.mult, op1=ALU.mult), reads=[r_CKV, r_RS, r_bc], writes=[r_TMP])
                rope_tm(TMP[0:bp, 0:64].unsqueeze(1), lo[0:bp, 256:320].unsqueeze(1), 1, CS[0:bp, b, :], bp, [r_TMP, r_CS], [lr], TMP2[0:bp, 0:64].unsqueeze(1), r_TMP2)
                if t < 4:
                    out_tokens.append(P.dma("sp", lambda e, lo=lo, b=b: e.dma_start(out=lat_p[t, b * 128:(b + 1) * 128, :], in_=lo[:, 0:256]), los, reads=[lr], writes=[r_out]))
                    out_tokens.append(P.dma("sp", lambda e, lo=lo, b=b: e.dma_start(out=kpe_p[t, b * 128:(b + 1) * 128, :], in_=lo[:, 256:320]), los, reads=[lr], writes=[r_out]))
                else:
                    out_tokens.append(P.dma("sp", lambda e, lo=lo: e.dma_start(out=lat_s, in_=lo[0:64, 0:256]), los, reads=[lr], writes=[r_out]))
                    out_tokens.append(P.dma("sp", lambda e, lo=lo: e.dma_start(out=kpe_s, in_=lo[0:64, 256:320]), los, reads=[lr], writes=[r_out]))
                    P.op("act", lambda e, lo=lo: e.activation(out=latn_aug[0:64, 0:256], in_=lo[0:64, 0:256], func=AF.Copy), reads=[lr, r_latn], writes=[r_latn])
                P.op("act", lambda e, lo=lo: e.activation(out=LB[0:bp, 0:320], in_=lo[0:bp, 0:320], func=AF.Copy), reads=[lr], writes=[r_LB])
                P.op("act", lambda e, lo=lo: e.activation(out=LB[0:bp, 320:384], in_=lo[0:bp, 256:320], func=AF.Copy), reads=[lr, r_LB], writes=[r_LB])
                pt2, pr2, _ = PS.next()
                pv = psb(pt2)
                for c in range(3):
                    P.op("pe", lambda e, pv=pv, c=c: e.transpose(out=pv[:, c * 128:c * 128 + bp], in_=LB[0:bp, c * 128:(c + 1) * 128], identity=ident_b[0:bp, 0:bp]),
                         reads=[r_LB, r_ident_b] if c == 0 else [], writes=[pr2] if c == 0 else [], inc=(c == 2))
                if t < 4:
                    P.op("act", lambda e, pv=pv, b=b: e.activation(out=ST[:, :, b * 128:(b + 1) * 128], in_=pv[:, 0:384].rearrange("p (c n) -> p c n", n=128), func=AF.Copy),
                         reads=[pr2], writes=[r_ST])
                else:
                    P.op("act", lambda e, pv=pv: e.activation(out=latnT[:, :, :], in_=pv[:, 0:384].rearrange("p (c n) -> p c n", n=128)[:, :, 0:64], func=AF.Copy),
                         reads=[pr2], writes=[r_latnT])
            if t < 4:
                for c in range(3):
                    P.dma("sp", lambda e, c=c: e.dma_start(out=sendb[c].ap().bitcast(BF16)[:, t * 512:(t + 1) * 512], in_=ST[:, c, :]), s_ST,
                          reads=[r_ST], writes=[r_sendb])

        def q_proj(t):
            N = 512 if t < 4 else 64
            nb, bp = (4, 128) if t < 4 else (1, 64)
            rmsnorm_fm(C, N, False, GV["nm"][1])
            tw, rw = load_wA(C, w_dq[0].rearrange("(k p) c -> p k c", p=128), 512)
            P.dma("pool", lambda e: e.dma_start(out=wUQ[:, :, :], in_=w_uq[0].rearrange("(k p) c -> p k c", p=128)), s_wUQ, writes=[r_wUQ])
            for b in range(nb):
                pt, pr, _ = PS.next()
                for k in range(8):
                    P.op("pe", lambda e, pt=pt, k=k, b=b: e.matmul(pt[0:bp, 0:512], XN[:, k, b * bp:(b + 1) * bp], tw[:, k, 0:512], start=(k == 0), stop=(k == 7)),
                         reads=[rw, r_XN] if k == 0 else [], writes=[pr] if k == 0 else [], inc=(k == 7))
                P.op("act", lambda e, pt=pt: e.activation(out=CQ[0:bp, :], in_=pt[0:bp, 0:512], func=AF.Copy), reads=[pr], writes=[r_CQ])
                P.op("dve", lambda e: e.tensor_tensor(out=SQT[0:bp, 0:512], in0=CQ[0:bp, :], in1=CQ[0:bp, :], op=ALU.mult), reads=[r_CQ], writes=[r_SQT])
                P.op("dve", lambda e: e.tensor_reduce(out=SS[0:bp, 0:1], in_=SQT[0:bp, 0:512], axis=AX.X, op=ALU.add), reads=[r_SQT], writes=[r_SS])
                rstd_from_ss(SS[0:bp, 0:1], RS[0:bp, 0:1], 512.0, [r_SS], r_RS, bp)
                P.op("dve", lambda e: e.scalar_tensor_tensor(out=CQB[0:bp, :], in0=CQ[0:bp, :], scalar=RS[0:bp, 0:1], in1=qn_bc[0:bp, :], op0=ALU.mult, op1=ALU.mult),
                     reads=[r_CQ, r_RS, r_bc], writes=[r_CQB])
                pt2, pr2, _ = PS.next()
                pv = psb(pt2)
                for c in range(4):
                    P.op("pe", lambda e, pv=pv, c=c: e.transpose(out=pv[:, c * 128:c * 128 + bp], in_=CQB[0:bp, c * 128:(c + 1) * 128], identity=ident_b[0:bp, 0:bp]),
                         reads=[r_CQB, r_ident_b] if c == 0 else [], writes=[pr2] if c == 0 else [], inc=(c == 3))
                P.op("act", lambda e, pv=pv, b=b: e.activation(out=CQT[:, :, b * bp:(b + 1) * bp], in_=pv[:, 0:512].rearrange("p (c n) -> p c n", n=128)[:, :, 0:bp], func=AF.Copy),
                     reads=[pr2, r_CQT], writes=[r_CQT])
            load_cs(t)
            for b in range(nb):
                for g in range(4):
                    pt, pr, _ = PS.next()
                    for k in range(4):
                        P.op("pe", lambda e, pt=pt, k=k, b=b, g=g: e.matmul(pt[0:bp, 0:384], CQT[:, k, b * bp:(b + 1) * bp], wUQ[:, k, g * 384:(g + 1) * 384], start=(k == 0), stop=(k == 3)),
                             reads=[r_wUQ, r_CQT] if k == 0 else [], writes=[pr] if k == 0 else [], inc=(k == 3))
                    P.op("act", lambda e, pt=pt, g=g: e.activation(out=QS[0:bp, g * 384:(g + 1) * 384], in_=pt[0:bp, 0:384], func=AF.Copy), reads=[pr, r_QS], writes=[r_QS])
                q3 = QS[0:bp, :].rearrange("p (h d) -> p h d", d=192)
                s3 = SQT[0:bp, :].rearrange("p (h d) -> p h d", d=192)
                P.op("dve", lambda e: e.tensor_tensor(out=SQT[0:bp, :], in0=QS[0:bp, :], in1=QS[0:bp, :], op=ALU.mult), reads=[r_QS], writes=[r_SQT])
                P.op("dve", lambda e, s3=s3: e.tensor_reduce(out=SS[0:bp, 0:8], in_=s3[:, :, 0:128], axis=AX.X, op=ALU.add), reads=[r_SQT], writes=[r_SS])
                P.op("dve", lambda e, s3=s3: e.tensor_reduce(out=SS[0:bp, 8:16], in_=s3[:, :, 128:192], axis=AX.X, op=ALU.add), reads=[r_SQT, r_SS], writes=[r_SS])
                rstd_from_ss(SS[0:bp, 0:8], RS[0:bp, 0:8], 128.0, [r_SS], r_RS, bp)
                rstd_from_ss(SS[0:bp, 8:16], RS[0:bp, 8:16], 64.0, [r_SS, r_RS], r_RS, bp)
                tn = TMP[0:bp, :].rearrange("p (h d) -> p h d", d=128)
                P.op("dve", lambda e, tn=tn, q3=q3: e.tensor_tensor(out=tn, in0=q3[:, :, 0:128], in1=RS[0:bp, 0:8].unsqueeze(2).to_broadcast([bp, 8, 128]), op=ALU.mult),
                     reads=[r_QS, r_RS], writes=[r_TMP])
                P.op("dve", lambda e, tn=tn: e.tensor_tensor(out=QNB[0:bp, :].rearrange("p (h d) -> p h d", d=128), in0=tn, in1=qnn_bc[0:bp, :].unsqueeze(1).to_broadcast([bp, 8, 128]), op=ALU.mult),
                     reads=[r_TMP, r_bc], writes=[r_QNB])
                tp = TMP[0:bp, 0:512].rearrange("p (h d) -> p h d", d=64)
                P.op("dve", lambda e, tp=tp, q3=q3: e.tensor_tensor(out=tp, in0=q3[:, :, 128:192], in1=RS[0:bp, 8:16].unsqueeze(2).to_broadcast([bp, 8, 64]), op=ALU.mult),
                     reads=[r_QS, r_RS], writes=[r_TMP])
                qpf = QPF[0:bp, :].rearrange("p (h d) -> p h d", d=64)
                P.op("dve", lambda e, tp=tp, qpf=qpf: e.tensor_tensor(out=qpf, in0=tp, in1=qpn_bc[0:bp, :].unsqueeze(1).to_broadcast([bp, 8, 64]), op=ALU.mult),
                     reads=[r_TMP, r_bc], writes=[r_QPF])
                rope_tm(qpf, QPB[0:bp, :].rearrange("p (h d) -> p h d", d=64), 8, CS[0:bp, b, :], bp, [r_QPF, r_CS], [r_QPB], TMP2[0:bp, :].rearrange("p (h d) -> p h d", d=64), r_TMP2)
                pt2, pr2, _ = PS.next()
                pv = psb(pt2)
                for h in range(8):
                    P.op("pe", lambda e, pv=pv, h=h: e.transpose(out=pv[:, h * 128:h * 128 + bp], in_=QNB[0:bp, h * 128:(h + 1) * 128], identity=ident_b[0:bp, 0:bp]),
                         reads=[r_QNB, r_ident_b] if h == 0 else [], writes=[pr2] if h == 0 else [], inc=(h == 7))
                if t < 4:
                    P.op("act", lambda e, pv=pv, b=b: e.activation(out=QTN[:, :, b * 128:(b + 1) * 128], in_=pv[:, :].rearrange("p (h n) -> p h n", n=128), func=AF.Copy),
                         reads=[pr2, r_QTN], writes=[r_QTN])
                else:
                    P.op("act", lambda e, pv=pv: e.activation(out=QNS[:, :, :], in_=pv[:, :].rearrange("p (h n) -> p h n", n=128)[:, :, 0:64], func=AF.Copy),
                         reads=[pr2], writes=[r_QNS])
                pt3, pr3, _ = PS.next()
                pv3 = psb(pt3)
                if t < 4:
                    for c in range(4):
                        P.op("pe", lambda e, pv3=pv3, c=c: e.transpose(out=pv3[:, c * 128:(c + 1) * 128], in_=QPB[:, c * 128:(c + 1) * 128], identity=ident_b[:, :]),
                             reads=[r_QPB, r_ident_b] if c == 0 else [], writes=[pr3] if c == 0 else [], inc=(c == 3))
                    P.op("act", lambda e, pv3=pv3, b=b: e.activation(out=QTP[:, :, b * 128:(b + 1) * 128], in_=pv3[:, 0:512].rearrange("p (c n) -> p c n", n=128), func=AF.Copy),
                         reads=[pr3, r_QTP], writes=[r_QTP])
                else:
                    for h in range(8):
                        P.op("pe", lambda e, pv3=pv3, h=h: e.transpose(out=pv3[0:64, h * 64:(h + 1) * 64], in_=QPB[0:64, h * 64:(h + 1) * 64], identity=ident_b[0:64, 0:64]),
                             reads=[r_QPB, r_ident_b] if h == 0 else [], writes=[pr3] if h == 0 else [], inc=(h == 7))
                    P.op("act", lambda e, pv3=pv3: e.activation(out=qpT_s[:, :, :], in_=pv3[0:64, 0:512].rearrange("p (h n) -> p h n", n=64), func=AF.Copy),
                         reads=[pr3], writes=[r_qpT_s])
            if t < 4:
                P.dma("sp", lambda e: e.dma_start(out=qsc_n.rearrange("h p n -> p h n")[:, :, t * 512:(t + 1) * 512], in_=QTN[:, :, :]), s_QTN, reads=[r_QTN], writes=[r_qsc_n[t]])
                P.dma("sp", lambda e: e.dma_start(out=qsc_p.rearrange("h p n -> p h n")[:, :, t * 512:(t + 1) * 512], in_=QTP[:, :, :]), s_QTP, reads=[r_QTP], writes=[r_qsc_p[t]])

        dbg_y = None
        for t in range(NT):
            N = 512 if t < 4 else 64
            halo = t < 4
            load_x(t)
            if stop == "A_load": break
            rmsnorm_fm(C, N, halo, GV["nf1"][0])
            if stop == "A_norm": break
            ffn(C, N, halo, w_ffn1_gu[0], w_ffn1_down[0])
            if stop == "A_ffn1": break
            conv_mixer(t)
            if stop == "A_conv": break
            rmsnorm_fm(C, N, False, GV["nf2"][0])
            ffn(C, N, False, w_ffn2_gu[0], w_ffn2_down[0])
            shared_kv(t)
            if stop == "A_kv": break
            rmsnorm_fm(C, N, False, GV["nf1"][1])
            ffn(C, N, False, w_ffn1_gu[1], w_ffn1_down[1])
            if stop == "A_x1": break
            P.dma("sp", lambda e, t=t, N=N: e.dma_start(out=xsc[t].rearrange("p (k n) -> p k n", n=512)[:, :, 0:N], in_=X[:, :, 0:N]), s_XS, reads=r_X, writes=[r_xsc[t]])
            q_proj(t)
            if stop == "A_q": break
        if stop.startswith("A_"):
            tokd = P.dma("sp", lambda e: e.dma_start(out=y_p[0].rearrange("(p a) d -> p (a d)", p=128), in_=X[:, :, :].rearrange("p k n -> p (k n)")), s_XS, reads=r_X, writes=[r_out])
            P.wait_all("sp", out_tokens + [tokd])
            P.emit()
            return nc
        s_cc = P.dma_sem("cc")
        for c in range(3):
            P.dma("pool", lambda e, c=c: e.collective_compute("AllGather", ALU.bypass, replica_groups=[[0, 1, 2, 3], [4, 5, 6, 7]],
                                                              ins=[sendb[c].ap().opt()], outs=[recvb[c].ap().opt()]), s_cc, reads=[r_sendb], writes=[r_recvb], inc=1)
        if stop == "A":
            P.wait_all("sp", out_tokens)
        P.emit()
    if stop == "A":
        return nc

    oT = sb("oT", [128, 8, 2112], BF16)
    with ExitStack() as es:
        def sbx(name, shape, dt=F32):
            return es.enter_context(nc.sbuf_tensor(name, list(shape), dt))
        latT = sbx("latT", [128, 2, 8192], BF16); r_latT = Res("latT"); s_latT = P.dma_sem("latT")
        kpeT = sbx("kpeT", [128, 8192], BF16); r_kpeT = Res("kpeT")
        WUK = sbx("WUK", [128, 2, D], BF16); r_WUK = Res("WUK"); s_W = P.dma_sem("WUKa"); s_W2 = P.dma_sem("WUVa")
        WUV = sbx("WUV", [128, 2, D], BF16); r_WUV = Res("WUV")
        MASK = sbx("MASK", [128, 16 * 512], BF16); r_MASK = Res("MASK"); s_MASK = P.dma_sem("MASK")
        KN = Ring(P, nc, "KN", [128, 8192], BF16, 2, alloc=sbx)
        VV = Ring(P, nc, "VV", [128, 64, 130], BF16, 2, alloc=sbx)
        SQK = Ring(P, nc, "SQK", [128, 512], BF16, 2, alloc=sbx)
        RK = Ring(P, nc, "RK", [128, 512], F32, 2, alloc=sbx)
        QN = Ring(P, nc, "QN", [128, 512], BF16, 2, dma=True, alloc=sbx)
        QP = Ring(P, nc, "QP", [128, 512], BF16, 2, dma=True, alloc=sbx)
        PT = Ring(P, nc, "PT", [128, 512], BF16, 3, alloc=sbx)
        RC = Ring(P, nc, "RC", [128, 1], F32, 4, alloc=sbx)
        ON = Ring(P, nc, "ON", [128, 128], BF16, 2, alloc=sbx)
        PSO = PsRing([0, 1, 2, 3])
        PSS = PsRing([4, 5])
        PSX = PsRing([6, 7])

        for r in range(4):
            for c in range(3):
                dst = (latT[:, c, :] if c < 2 else kpeT[:, :]).rearrange("p (s r2 i) -> p s r2 i", s=4, r2=4, i=512)[:, :, r, :]
                src = recvb[c].ap().bitcast(BF16)[r * 128:(r + 1) * 128, :].rearrange("p (s i) -> p s i", i=512)
                P.dma("sp", lambda e, dst=dst, src=src: e.dma_start(out=dst, in_=src), s_latT, reads=[r_recvb], writes=[r_latT if c < 2 else r_kpeT])
        r_latT.w = r_kpeT.w = (s_latT.sem, s_latT.val)
        P.dma("pool", lambda e: e.dma_start(out=WUK[:, :, :], in_=w_uk.rearrange("(c p) m -> p c m", p=128)), s_W, writes=[r_WUK])
        P.dma("pool", lambda e: e.dma_start(out=WUV[:, :, :], in_=w_uv.rearrange("(c p) m -> p c m", p=128)), s_W2, writes=[r_WUV])
        P.dma("sp", lambda e: e.dma_start(out=MASK[:, :], in_=mask_p_d), s_MASK, writes=[r_MASK])
        kcol = GV["kn"]
        for (vt, vr, _) in VV.items:
            P.op("pool", lambda e, vt=vt: e.memset(vt[:, :, 128:130], 1.0), writes=[vr])
        nheads = 8
        for h in range(nheads):
            kn_t, kn_r, _ = KN.next()
            v_t, v_r, _ = VV.next()
            for kt in range(16):
                pk, prk, _ = PSX.next()
                for c in range(2):
                    P.op("pe", lambda e, pk=pk, c=c, h=h, kt=kt: e.matmul(pk[:, :], WUK[:, c, h * 128:(h + 1) * 128], latT[:, c, kt * 512:(kt + 1) * 512], start=(c == 0), stop=(c == 1)),
                         reads=[r_WUK, r_latT] if c == 0 else [], writes=[prk] if c == 0 else [], inc=(c == 1))
                sq, sqr, _ = SQK.next()
                P.op("act", lambda e, sq=sq, pk=pk: e.activation(out=sq[:, :], in_=pk[:, :], func=AF.Square), reads=[prk], writes=[sqr])
                p2, pr2, _ = PSX.next()
                P.op("pe", lambda e, p2=p2, sq=sq: e.matmul(p2[:, :], ones_b[:], sq[:, :], start=True, stop=True), reads=[sqr, r_ones], writes=[pr2])
                rk, rkr, _ = RK.next()
                rstd_from_ss(p2[:, :], rk[:, :], 128.0, [pr2], rkr)
                P.op("dve", lambda e, pk=pk, rk=rk, kn_t=kn_t, kt=kt: e.scalar_tensor_tensor(out=kn_t[:, kt * 512:(kt + 1) * 512], in0=pk[:, :], scalar=gv[:, kcol:kcol + 1], in1=rk[:, :],
                                                                                           op0=ALU.mult, op1=ALU.mult), reads=[prk, rkr, r_gv], writes=[kn_r])
            for kb4 in range(16):
                pvv, prv, _ = PSX.next()
                for q4 in range(4):
                    kb = kb4 * 4 + q4
                    for c in range(2):
                        P.op("pe", lambda e, pvv=pvv, q4=q4, kb=kb, c=c, h=h: e.matmul(pvv[:, q4 * 128:(q4 + 1) * 128], latT[:, c, kb * 128:(kb + 1) * 128], WUV[:, c, h * 128:(h + 1) * 128],
                                                                                    start=(c == 0), stop=(c == 1)),
                             reads=[r_WUV, r_latT] if (q4 == 0 and c == 0) else [], writes=[prv] if (q4 == 0 and c == 0) else [], inc=(q4 == 3 and c == 1))
                P.op("act", lambda e, pvv=pvv, v_t=v_t, kb4=kb4: e.activation(out=v_t[:, kb4 * 4:(kb4 + 1) * 4, 0:128], in_=pvv[:, :].rearrange("p (q d) -> p q d", d=128), func=AF.Copy),
                     reads=[prv, v_r], writes=[v_r])
            base = 64 * (h % 2)
            for s in range(4):
                qn_t, qn_r, qn_s = QN.next()
                qp_t, qp_r, qp_s = QP.next()
                P.dma("sp", lambda e, qn_t=qn_t, h=h, s=s: e.dma_start(out=qn_t[:, :], in_=qsc_n[h, :, s * 512:(s + 1) * 512]), qn_s, reads=[r_qsc_n[s]], writes=[qn_r])
                P.dma("sp", lambda e, qp_t=qp_t, h=h, s=s: e.dma_start(out=qp_t[:, :], in_=qsc_p[h // 2, :, s * 512:(s + 1) * 512]), qp_s, reads=[r_qsc_p[s]], writes=[qp_r])
                accs = [PSO.next() for _ in range(4)]
                nkb = 16 * (s + 1)
                for j in range(nkb):
                    sc, scr, _ = PSS.next()
                    P.op("pe", lambda e, sc=sc, kn_t=kn_t, qn_t=qn_t, j=j: e.matmul(sc[:, :], kn_t[:, j * 128:(j + 1) * 128], qn_t[:, :], start=True, stop=False),
                         reads=[kn_r, qn_r], writes=[scr], inc=False)
                    P.op("pe", lambda e, sc=sc, qp_t=qp_t, j=j, base=base: e.matmul(sc[:, :], kpeT[base:base + 64, j * 128:(j + 1) * 128], qp_t[base:base + 64, :], start=False, stop=True),
                         reads=[r_kpeT, qp_r], touch=[scr])
                    pt_, ptr, _ = PT.next()
                    P.op("act", lambda e, pt_=pt_, sc=sc: e.activation(out=pt_[:, :], in_=sc[:, :], func=AF.Exp, scale=SCALE), reads=[scr], writes=[ptr])
                    if j >= 16 * s:
                        jj = j - 16 * s
                        P.op("pool", lambda e, pt_=pt_, jj=jj: e.tensor_tensor(out=pt_[:, :], in0=pt_[:, :], in1=MASK[:, jj * 512:(jj + 1) * 512], op=ALU.mult),
                             reads=[ptr, r_MASK], writes=[ptr])
                    for i in range(4):
                        at, ar, _ = accs[i]
                        P.op("pe", lambda e, at=at, pt_=pt_, v_t=v_t, i=i, j=j, nkb=nkb: e.matmul(at[:, 0:129], pt_[:, i * 128:(i + 1) * 128], v_t[:, j, 0:129], start=(j == 0), stop=(j == nkb - 1)),
                             reads=[ptr, v_r] if i == 0 else [], writes=[ar] if j == 0 else [], touch=[] if j == 0 else [ar], inc=(i == 3))
                for i in range(4):
                    at, ar, _ = accs[i]
                    rc, rcr, _ = RC.next()
                    P.op("dve", lambda e, rc=rc, at=at: e.reciprocal(out=rc[:, :], in_=at[:, 128:129]), reads=[ar], writes=[rcr])
                    on, onr, _ = ON.next()
                    P.op("dve", lambda e, on=on, at=at, rc=rc: e.tensor_scalar(out=on[:, :], in0=at[:, 0:128], scalar1=rc[:, 0:1], scalar2=None, op0=ALU.mult),
                         reads=[ar, rcr], writes=[onr])
                    px, pxr, _ = PSX.next()
                    pxv = psb(px)
                    P.op("pe", lambda e, pxv=pxv, on=on: e.transpose(out=pxv[:, 0:128], in_=on[:, :], identity=ident_b[:, :]), reads=[onr, r_ident_b], writes=[pxr])
                    P.op("act", lambda e, pxv=pxv, h=h, s=s, i=i: e.activation(out=oT[:, h, s * 512 + i * 128:s * 512 + (i + 1) * 128], in_=pxv[:, 0:128], func=AF.Copy),
                         reads=[pxr, r_oT[s]], writes=[r_oT[s]])
        if stop == "AT":
            P.wait_all("sp", out_tokens)
        P.emit()
    if stop == "AT":
        return nc

    with ExitStack() as es:
        def sbx(name, shape, dt=F32):
            return es.enter_context(nc.sbuf_tensor(name, list(shape), dt))
        WUK = sbx("WUKs", [128, 2, D], BF16); r_WUK = Res("WUK"); s_W = P.dma_sem("WUKs"); s_W2 = P.dma_sem("WUVs")
        WUV = sbx("WUVs", [128, 2, D], BF16); r_WUV = Res("WUV")
        WUKT = sbx("WUKT", [128, 8, 256], BF16); r_WUKT = Res("WUKT")
        QABS = sbx("QABS", [128, 2, 8, 64], BF16); r_QABS = Res("QABS")
        MN = sbx("MN", [64, 512], BF16); r_MN = Res("MN"); s_MN = P.dma_sem("MN")
        LP = Ring(P, nc, "LP", [128, 257], BF16, 4, dma=True, alloc=sbx)
        KP = Ring(P, nc, "KP", [128, 64], BF16, 4, dma=True, alloc=sbx)
        IDX = Ring(P, nc, "IDX", [128, 1], I32, 4, alloc=sbx)
        IDX1 = Ring(P, nc, "IDX1", [128, 1], I32, 4, alloc=sbx)
        LTP = Ring(P, nc, "LTP", [128, 384], BF16, 2, alloc=sbx)
        SQP = Ring(P, nc, "SQP", [128, 1024], F32, 2, alloc=sbx)
        SSP = Ring(P, nc, "SSP", [128, 8], F32, 2, alloc=sbx)
        RSP = Ring(P, nc, "RSP", [128, 8], F32, 2, alloc=sbx)
        T1 = Ring(P, nc, "T1", [128, 32], F32, 2, alloc=sbx)
        PTS = Ring(P, nc, "PTS", [128, 32], BF16, 2, alloc=sbx)
        SNW = sbx("SNW", [64, 512]); r_SNW = Res("SNW")
        PNW = sbx("PNW", [64, 512], BF16); r_PNW = Res("PNW")
        PN2 = sbx("PN2", [64, 16, 32], BF16); r_PN2 = Res("PN2")
        RCs = Ring(P, nc, "RCs", [32, 1], F32, 2, alloc=sbx)
        OLN = Ring(P, nc, "OLN", [32, 256], BF16, 2, alloc=sbx)
        OLT = sbx("OLT", [128, 2, 8, 64], BF16); r_OLT = Res("OLT")
        PSK = PsRing([0, 1, 2, 3])
        PSX = PsRing([4, 5])
        PSO = PsRing([6, 7])
        kcol = GV["kn"]

        P.dma("pool", lambda e: e.dma_start(out=WUK[:, :, :], in_=w_uk.rearrange("(c p) m -> p c m", p=128)), s_W, writes=[r_WUK])
        P.dma("pool", lambda e: e.dma_start(out=WUV[:, :, :], in_=w_uv.rearrange("(c p) m -> p c m", p=128)), s_W2, writes=[r_WUV])
        P.dma("sp", lambda e: e.dma_start(out=MN[:, :], in_=mask_n_d), s_MN, writes=[r_MN])
        for (lt, lr, _) in LP.items:
            P.op("dve", lambda e, lt=lt: e.memset(lt[:, 256:257], 1.0), writes=[lr])
        for h in range(8):
            px, pxr, _ = PSX.next()
            pxv = psb(px)
            for c in range(2):
                P.op("pe", lambda e, pxv=pxv, c=c, h=h: e.transpose(out=pxv[:, c * 128:(c + 1) * 128], in_=WUK[:, c, h * 128:(h + 1) * 128], identity=ident_b[:, :]),
                     reads=[r_WUK, r_ident_b] if c == 0 else [], writes=[pxr] if c == 0 else [], inc=(c == 1))
            P.op("dve", lambda e, pxv=pxv, h=h: e.tensor_scalar(out=WUKT[:, h, :], in0=pxv[:, 0:256], scalar1=gv[:, kcol:kcol + 1], scalar2=None, op0=ALU.mult),
                 reads=[pxr, r_gv, r_WUKT], writes=[r_WUKT])
        for h in range(8):
            for c in range(2):
                px, pxr, _ = PSX.next()
                P.op("pe", lambda e, px=px, c=c, h=h: e.matmul(px[:, 0:64], WUKT[:, h, c * 128:(c + 1) * 128], QNS[:, h, :], start=True, stop=True),
                     reads=[r_WUKT, r_QNS], writes=[pxr])
                P.op("act", lambda e, px=px, c=c, h=h: e.activation(out=QABS[:, c, h, :], in_=px[:, 0:64], func=AF.Copy), reads=[pxr, r_QABS], writes=[r_QABS])

        def key_block(lat_bf, kpe_bf, nk, qabs_rhs, qp_rhs, ncol):
            px, pxr, _ = PSX.next()
            pxv = psb(px)
            return px, pxr, pxv

        _breg = {}

        def breg(e):
            if "r" not in _breg:
                _breg["r"] = e.to_reg(HALF - 1)
            return _breg["r"]

        nseq_run = 16
        for b in range(nseq_run):
            ol, olr, _ = PSO.next()
            for pg in range(NPAGE):
                col = b * NPAGE + pg
                ix, ixr, _ = IDX.next()
                P.op("dve", lambda e, ix=ix, col=col: e.scalar_tensor_tensor(out=ix[:, :], in0=ptab_t[:, col:col + 1], scalar=128, in1=iota_t[:, :], op0=ALU.mult, op1=ALU.add),
                     reads=[r_ptab, r_iota], writes=[ixr])
                ix1, ixr1, _ = IDX1.next()
                P.op("dve", lambda e, ix=ix, ix1=ix1: e.tensor_scalar(out=ix1[:, :], in0=ix[:, :], scalar1=-HALF, scalar2=None, op0=ALU.add), reads=[ixr], writes=[ixr1])
                lp, lpr, lps = LP.next()
                kp, kpr, kps = KP.next()
                for hf, (ixx, ixxr) in enumerate(((ix, ixr), (ix1, ixr1))):
                    P.dma("pool", lambda e, lp=lp, ixx=ixx, hf=hf: e.indirect_dma_start(out=lp[:, 0:256], out_offset=None, in_=cache_lat[hf],
                                                                                       in_offset=bass.IndirectOffsetOnAxis(ap=ixx[:, :], axis=0),
                                                                                       bounds_check=breg(e), oob_is_err=False),
                          lps, reads=[ixxr], writes=[lpr] if hf == 0 else [], touch=[lpr] if hf == 1 else [])
                    P.dma("pool", lambda e, kp=kp, ixx=ixx, hf=hf: e.indirect_dma_start(out=kp[:, :], out_offset=None, in_=cache_kpe[hf],
                                                                                       in_offset=bass.IndirectOffsetOnAxis(ap=ixx[:, :], axis=0),
                                                                                       bounds_check=breg(e), oob_is_err=False),
                          kps, reads=[ixxr], writes=[kpr] if hf == 0 else [], touch=[kpr] if hf == 1 else [])
                px, pxr, _ = PSX.next()
                pxv = psb(px)
                for c in range(2):
                    P.op("pe", lambda e, pxv=pxv, lp=lp, c=c: e.transpose(out=pxv[:, c * 128:(c + 1) * 128], in_=lp[:, c * 128:(c + 1) * 128], identity=ident_b[:, :]),
                         reads=[lpr, r_ident_b] if c == 0 else [], writes=[pxr] if c == 0 else [], inc=False)
                P.op("pe", lambda e, pxv=pxv, kp=kp: e.transpose(out=pxv[0:64, 256:384], in_=kp[:, :], identity=ident_b[:, :]), reads=[kpr], touch=[pxr])
                lt, ltr, _ = LTP.next()
                P.op("act", lambda e, lt=lt, pxv=pxv: e.activation(out=lt[:, 0:256], in_=pxv[:, 0:256], func=AF.Copy), reads=[pxr], writes=[ltr])
                P.op("act", lambda e, lt=lt, pxv=pxv: e.activation(out=lt[0:64, 256:384], in_=pxv[0:64, 256:384], func=AF.Copy), reads=[pxr, ltr], writes=[ltr])
                pk0, pkr0, _ = PSK.next()
                pk1, pkr1, _ = PSK.next()
                for hf, (pk, pkr) in enumerate(((pk0, pkr0), (pk1, pkr1))):
                    for c in range(2):
                        P.op("pe", lambda e, pk=pk, lt=lt, c=c, hf=hf: e.matmul(pk[:, :], lt[:, c * 128:(c + 1) * 128], WUK[:, c, hf * 512:(hf + 1) * 512], start=(c == 0), stop=(c == 1)),
                             reads=[ltr, r_WUK] if c == 0 else [], writes=[pkr] if c == 0 else [], inc=(c == 1))
                sq, sqr, _ = SQP.next()
                P.op("act", lambda e, sq=sq, pk0=pk0: e.activation(out=sq[:, 0:512], in_=pk0[:, :], func=AF.Square), reads=[pkr0], writes=[sqr])
                P.op("act", lambda e, sq=sq, pk1=pk1: e.activation(out=sq[:, 512:1024], in_=pk1[:, :], func=AF.Square), reads=[pkr1, sqr], writes=[sqr])
                ss, ssr, _ = SSP.next()
                P.op("dve", lambda e, ss=ss, sq=sq: e.tensor_reduce(out=ss[:, :], in_=sq[:, :].rearrange("p (h d) -> p h d", d=128), axis=AX.X, op=ALU.add), reads=[sqr], writes=[ssr])
                rs, rsr, _ = RSP.next()
                rstd_from_ss(ss[:, :], rs[:, :], 128.0, [ssr], rsr)
                ps_, psr, _ = PSX.next()
                for c in range(2):
                    P.op("pe", lambda e, ps_=ps_, lt=lt, c=c, b=b: e.matmul(ps_[:, 0:32], lt[:, c * 128:(c + 1) * 128], QABS[:, c, :, b * 4:(b + 1) * 4], start=(c == 0), stop=(c == 1)),
                         reads=[ltr, r_QABS] if c == 0 else [], writes=[psr] if c == 0 else [], inc=False)
                P.op("pe", lambda e, ps_=ps_, lt=lt, b=b: e.matmul(ps_[:, 32:64], lt[0:64, 256:384], qpT_s[:, :, b * 4:(b + 1) * 4], start=True, stop=True),
                     reads=[r_qpT_s], touch=[psr])
                t1, t1r, _ = T1.next()
                P.op("dve", lambda e, t1=t1, ps_=ps_, rs=rs: e.tensor_tensor(out=t1[:, :].rearrange("p (h t) -> p h t", t=4), in0=ps_[:, 0:32].rearrange("p (h t) -> p h t", t=4),
                                                                            in1=rs[:, :].unsqueeze(2).to_broadcast([128, 8, 4]), op=ALU.mult), reads=[psr, rsr], writes=[t1r])
                P.op("dve", lambda e, t1=t1, ps_=ps_: e.tensor_tensor(out=t1[:, :], in0=t1[:, :], in1=ps_[:, 32:64], op=ALU.add), reads=[psr, t1r], writes=[t1r])
                pts, ptsr, _ = PTS.next()
                P.op("act", lambda e, pts=pts, t1=t1: e.activation(out=pts[:, :], in_=t1[:, :], func=AF.Exp, scale=SCALE), reads=[t1r], writes=[ptsr])
                P.op("pe", lambda e, ol=ol, pts=pts, lp=lp, pg=pg: e.matmul(ol[0:32, 0:257], pts[:, :], lp[:, 0:257], start=(pg == 0), stop=False),
                     reads=[ptsr, lpr], writes=[olr] if pg == 0 else [], touch=[] if pg == 0 else [olr])
            if b == 0:
                pk0, pkr0, _ = PSK.next()
                pk1, pkr1, _ = PSK.next()
                for hf, (pk, pkr) in enumerate(((pk0, pkr0), (pk1, pkr1))):
                    for c in range(2):
                        P.op("pe", lambda e, pk=pk, c=c, hf=hf: e.matmul(pk[0:64, :], latnT[:, c, :], WUK[:, c, hf * 512:(hf + 1) * 512], start=(c == 0), stop=(c == 1)),
                             reads=[r_latnT, r_WUK] if c == 0 else [], writes=[pkr] if c == 0 else [], inc=(c == 1))
                sq, sqr, _ = SQP.next()
                P.op("act", lambda e, sq=sq, pk0=pk0: e.activation(out=sq[0:64, 0:512], in_=pk0[0:64, :], func=AF.Square), reads=[pkr0], writes=[sqr])
                P.op("act", lambda e, sq=sq, pk1=pk1: e.activation(out=sq[0:64, 512:1024], in_=pk1[0:64, :], func=AF.Square), reads=[pkr1, sqr], writes=[sqr])
                ss, ssr, _ = SSP.next()
                P.op("dve", lambda e, ss=ss, sq=sq: e.tensor_reduce(out=ss[0:64, :], in_=sq[0:64, :].rearrange("p (h d) -> p h d", d=128), axis=AX.X, op=ALU.add), reads=[sqr], writes=[ssr])
                rsn, rsnr, _ = RSP.next()
                rstd_from_ss(ss[0:64, :], rsn[0:64, :], 128.0, [ssr], rsnr, 64)
                pa, par, _ = PSK.next()
                pb_, pbr, _ = PSK.next()
                for c in range(2):
                    P.op("pe", lambda e, pa=pa, c=c: e.matmul(pa[0:64, :], latnT[:, c, :], QABS[:, c, :, :], start=(c == 0), stop=(c == 1)),
                         reads=[r_latnT, r_QABS] if c == 0 else [], writes=[par] if c == 0 else [], inc=(c == 1))
                P.op("pe", lambda e, pb_=pb_: e.matmul(pb_[0:64, :], latnT[0:64, 2, :], qpT_s[:, :, :], start=True, stop=True), reads=[r_latnT, r_qpT_s], writes=[pbr])
                P.op("dve", lambda e, pa=pa, rsn=rsn: e.tensor_tensor(out=SNW[:, :].rearrange("p (h n) -> p h n", n=64), in0=pa[0:64, :].rearrange("p (h n) -> p h n", n=64),
                                                                     in1=rsn[0:64, :].unsqueeze(2).to_broadcast([64, 8, 64]), op=ALU.mult), reads=[par, rsnr], writes=[r_SNW])
                P.op("dve", lambda e, pb_=pb_: e.tensor_tensor(out=SNW[:, :], in0=SNW[:, :], in1=pb_[0:64, :], op=ALU.add), reads=[pbr, r_SNW], writes=[r_SNW])
                P.op("act", lambda e: e.activation(out=PNW[:, :], in_=SNW[:, :], func=AF.Exp, scale=SCALE), reads=[r_SNW], writes=[r_PNW])
                P.op("dve", lambda e: e.tensor_tensor(out=PNW[:, :], in0=PNW[:, :], in1=MN[:, :], op=ALU.mult), reads=[r_PNW, r_MN], writes=[r_PNW])
                P.op("dve", lambda e: e.tensor_copy(out=PN2[:, :, :].rearrange("p b (h t) -> p b h t", t=4), in_=PNW[:, :].rearrange("p (h b t) -> p b h t", h=8, b=16, t=4)),
                     reads=[r_PNW], writes=[r_PN2])
            P.op("pe", lambda e, ol=ol, b=b: e.matmul(ol[0:32, 0:257], PN2[:, b, :], latn_aug[:, 0:257], start=False, stop=True), reads=[r_PN2, r_latn], touch=[olr])
            rc, rcr, _ = RCs.next()
            P.op("dve", lambda e, rc=rc, ol=ol: e.reciprocal(out=rc[:, :], in_=ol[0:32, 256:257]), reads=[olr], writes=[rcr])
            on, onr, _ = OLN.next()
            P.op("dve", lambda e, on=on, ol=ol, rc=rc: e.tensor_scalar(out=on[:, :], in0=ol[0:32, 0:256], scalar1=rc[:, 0:1], scalar2=None, op0=ALU.mult), reads=[olr, rcr], writes=[onr])
            px, pxr, _ = PSX.next()
            pxv = psb(px)
            for c in range(2):
                P.op("pe", lambda e, pxv=pxv, on=on, c=c: e.transpose(out=pxv[:, c * 32:(c + 1) * 32], in_=on[:, c * 128:(c + 1) * 128], identity=ident_b[0:32, 0:32]),
                     reads=[onr, r_ident_b] if c == 0 else [], writes=[pxr] if c == 0 else [], inc=(c == 1))
            P.op("act", lambda e, pxv=pxv, b=b: e.activation(out=OLT[:, :, :, b * 4:(b + 1) * 4], in_=pxv[:, 0:64].rearrange("p (c h t) -> p c h t", c=2, h=8, t=4), func=AF.Copy),
                 reads=[pxr, r_OLT], writes=[r_OLT])
        for h in range(8):
            px, pxr, _ = PSX.next()
            for c in range(2):
                P.op("pe", lambda e, px=px, c=c, h=h: e.matmul(px[:, 0:64], WUV[:, c, h * 128:(h + 1) * 128], OLT[:, c, h, :], start=(c == 0), stop=(c == 1)),
                     reads=[r_WUV, r_OLT] if c == 0 else [], writes=[pxr] if c == 0 else [], inc=(c == 1))
            P.op("act", lambda e, px=px, h=h: e.activation(out=oT[:, h, 2048:2112], in_=px[:, 0:64], func=AF.Copy), reads=[pxr, r_oT[4]], writes=[r_oT[4]])
        if stop == "S":
            P.wait_all("sp", out_tokens)
        P.emit()
    if stop == "S":
        return nc

    with ExitStack() as es:
        C = make_common(es)
        sbx = C["sbx"]
        X, XN, Hh = C["X"], C["XN"], C["Hh"]
        r_X, r_XN, r_H = C["r_X"], C["r_XN"], C["r_H"]
        YT = Ring(P, nc, "YT", [128, D], F32, 2, dma=True, alloc=sbx)
        s_XL = P.dma_sem("XL")
        for t in range(NT):
            N = 512 if t < 4 else 64
            nb, bp = (4, 128) if t < 4 else (1, 64)
            col0 = t * 512
            P.dma("sp", lambda e, t=t, N=N: e.dma_start(out=X[:, :, 0:N], in_=xsc[t].rearrange("p (k n) -> p k n", n=512)[:, :, 0:N]), s_XL, reads=[r_xsc[t]], writes=r_X)
            Wo = w_o[0]
            for mb in range(0, 8, 4):
                tw, rw = load_wA(C, wview(Wo, mb * 128, 512))
                for mm in range(4):
                    m = mb + mm
                    pt, pr, _ = PS.next()
                    for k in range(8):
                        P.op("pe", lambda e, pt=pt, tw=tw, k=k, mm=mm, N=N, col0=col0: e.matmul(pt[:, 0:N], tw[:, k, mm * 128:(mm + 1) * 128], oT[:, k, col0:col0 + N], start=(k == 0), stop=(k == 7)),
                             reads=[rw, r_oT[t]] if k == 0 else [], writes=[pr] if k == 0 else [], inc=(k == 7))
                    P.op("dve", lambda e, pt=pt, m=m, N=N: e.tensor_tensor(out=X[:, m, 0:N], in0=pt[:, 0:N], in1=X[:, m, 0:N], op=ALU.add), reads=[pr, r_X[m]], writes=[r_X[m]])
            rmsnorm_fm(C, N, False, GV["nf2"][1])
            ffn(C, N, False, w_ffn2_gu[1], w_ffn2_down[1])
            for b in range(nb):
                yt, yr, yts = YT.next()
                for half in range(2):
                    pt, pr, _ = PS.next()
                    for kk in range(4):
                        k = half * 4 + kk
                        P.op("pe", lambda e, pt=pt, k=k, kk=kk, b=b, bp=bp: e.transpose(out=pt[0:bp, kk * 128:(kk + 1) * 128], in_=X[:, k, b * bp:(b + 1) * bp], identity=ident_f[:, :]),
                             reads=[r_X[k], r_ident_f], writes=[pr] if kk == 0 else [], touch=[] if kk == 0 else [pr], inc=(kk == 3))
                    P.op("act", lambda e, pt=pt, yt=yt, half=half, bp=bp: e.activation(out=yt[0:bp, half * 512:(half + 1) * 512], in_=pt[0:bp, :], func=AF.Copy), reads=[pr, yr], writes=[yr])
                if t < 4:
                    out_tokens.append(P.dma("sp", lambda e, yt=yt, t=t, b=b: e.dma_start(out=y_p[t, b * 128:(b + 1) * 128, :], in_=yt[:, :]), yts, reads=[yr], writes=[r_out]))
                else:
                    out_tokens.append(P.dma("sp", lambda e, yt=yt: e.dma_start(out=y_s, in_=yt[0:64, :]), yts, reads=[yr], writes=[r_out]))
        P.wait_all("sp", out_tokens)
        P.emit()
    _NC_CACHE["cnt"] = dict(P.cnt)
    return nc


DEBUG_STOP = ""


def _get_nc(n_phys=10240, stop=""):
    key = ("nc", n_phys, stop)
    if key not in _NC_CACHE:
        _NC_CACHE[key] = build_program(n_phys, stop)
    return _NC_CACHE[key]


def _host_layout(inp):
    f32 = np.float32
    g = {k: np.asarray(v) for k, v in inp.items()}
    x_prompt, x_sample = g["x_prompt"], g["x_sample"]
    cache_lat = np.ascontiguousarray(g["cache_latent"]).reshape(-1, 256)
    cache_kpe = np.ascontiguousarray(g["cache_kpe"]).reshape(-1, 64)
    rows = [g["norm_ffn1"][0], g["norm_ffn1"][1], g["norm_mix"][0], g["norm_mix"][1], g["norm_ffn2"][0], g["norm_ffn2"][1],
            g["norm_kv_in"], g["conv_w"][0, 0], g["conv_w"][0, 1], g["conv_w"][0, 2]]
    gvec = np.zeros((88, 128), f32)
    gvec[0:80] = np.concatenate([r.reshape(8, 128) for r in rows], 0)
    gvec[80] = g["k_nope_norm"]
    freqs = (np.float32(10000.0) ** (-np.arange(32, dtype=f32) / np.float32(32))).astype(f32)

    def cs_table(pos):
        ang = (pos.astype(f32)[:, None] * freqs[None, :]).astype(f32)
        return np.concatenate([np.cos(ang), np.sin(ang)], 1).astype(f32)

    ident = np.eye(128, dtype=f32)
    iota = np.arange(128, dtype=np.int32)[:, None]
    kk = np.arange(128)[:, None, None]
    jj = np.arange(16)[None, :, None]
    qq = np.arange(512)[None, None, :]
    bp_, tp_ = np.divmod(np.arange(64), 4)
    mn = np.zeros((64, 8, 16, 4), f32)
    for b in range(16):
        for t in range(4):
            mn[:, :, b, t] = ((bp_ == b) & (tp_ <= t))[:, None]
    mask_n = mn.reshape(64, 512).astype(ml_dtypes.bfloat16)
    cs_s = cs_table(8192 + (np.arange(64) % 4))
    hr = cache_lat.shape[0] // 2
    shared = dict(cache_lat0=cache_lat[:hr], cache_lat1=cache_lat[hr:], cache_kpe0=cache_kpe[:hr], cache_kpe1=cache_kpe[hr:], gvec=gvec, ident=ident, iota=iota, mask_n=mask_n, cs_s=cs_s,
                  q_norm=g["q_norm"], q_nope_norm=g["q_nope_norm"], q_pe_norm=g["q_pe_norm"], kv_norm=g["kv_norm"], k_pe_norm=g["k_pe_norm"])
    for k in ("w_ffn1_gu", "w_ffn1_down", "w_ffn2_gu", "w_ffn2_down", "w_conv_in", "w_conv_out", "w_dq", "w_uq", "w_o", "w_dkv", "w_uk", "w_uv"):
        shared[k] = g[k]
    in_maps = []
    for c in range(8):
        q, cp = divmod(c, 4)
        xp = np.zeros((4, 514, D), f32)
        cs_p = np.zeros((4, 512, 64), f32)
        for s in range(4):
            st = (4 * s + cp) * 512
            xp[s, 2:] = x_prompt[q, st:st + 512]
            if st > 0:
                xp[s, 0:2] = x_prompt[q, st - 2:st]
            cs_p[s] = cs_table(st + np.arange(512))
        mask_p = (128 * jj + kk <= cp * 512 + qq).astype(f32).reshape(128, 16 * 512).astype(ml_dtypes.bfloat16)
        m = dict(shared)
        m.update(xp=xp, xs=np.ascontiguousarray(x_sample[16 * c:16 * c + 16]).reshape(64, D),
                 sconv=np.ascontiguousarray(g["state_conv"][0, 16 * c:16 * c + 16]).reshape(32, D),
                 ptab=np.ascontiguousarray(g["page_table"][16 * c:16 * c + 16]).reshape(1, 16 * NPAGE).astype(np.int32),
                 cs_p=cs_p, mask_p=mask_p)
        in_maps.append(m)
    return in_maps


def kernel(_stop="", _trace=False, **inputs):
    nc = _get_nc(int(np.asarray(inputs["cache_latent"]).shape[0]), _stop)
    in_maps = _host_layout(inputs)
    res = run_bass_kernel_spmd(nc, in_maps, core_ids=list(range(8)), **({"trace": True} if _trace else {}))
    if _trace:
        print("exec_time_ns", res.exec_time_ns)
    R = res.results
    f32 = np.float32
    y_prompt = np.zeros((2, 8192, D), f32)
    y_sample = np.zeros((128, 4, D), f32)
    conv_p = np.zeros((1, 2, 2, D), f32)
    conv_s = np.zeros((1, 128, 2, D), f32)
    lat_p = np.zeros((2, 8192, 256), f32)
    kpe_p = np.zeros((2, 8192, 64), f32)
    lat_s = np.zeros((128, 4, 256), f32)
    kpe_s = np.zeros((128, 4, 64), f32)
    for c in range(8):
        q, cp = divmod(c, 4)
        r = R[c]
        for s in range(4):
            st = (4 * s + cp) * 512
            y_prompt[q, st:st + 512] = r["y_p"][s]
            lat_p[q, st:st + 512] = r["lat_p"][s]
            kpe_p[q, st:st + 512] = r["kpe_p"][s]
        if cp == 3:
            conv_p[0, q] = r["cso_p"][3]
        y_sample[16 * c:16 * c + 16] = r["y_s"].reshape(16, 4, D)
        conv_s[0, 16 * c:16 * c + 16] = r["cso_s"].reshape(16, 2, D)
        lat_s[16 * c:16 * c + 16] = r["lat_s"].reshape(16, 4, 256)
        kpe_s[16 * c:16 * c + 16] = r["kpe_s"].reshape(16, 4, 64)
    return (y_prompt, y_sample, conv_p, conv_s, lat_p, kpe_p, lat_s, kpe_s)
```

```python
import numpy as np
from contextlib import ExitStack
import ml_dtypes
import concourse.bass as bass
import concourse.mybir as mybir
from concourse.bass_utils import run_bass_kernel_spmd

F32 = mybir.dt.float32
BF16 = mybir.dt.bfloat16
I32 = mybir.dt.int32
AF = mybir.ActivationFunctionType
ALU = mybir.AluOpType
AX = mybir.AxisListType

D = 1024
DFF = 2816
NJ = DFF // 128
EPS = 1e-6
SCALE = 192.0 ** -0.5
NPAGE = 64
NT = 5


class Res:
    __slots__ = ("name", "w", "r")

    def __init__(self, name=""):
        self.name = name
        self.w = None
        self.r = []


class DmaSem:
    def __init__(self, sem):
        self.sem = sem
        self.val = 0


class Prog:
    ENGS = ("pe", "act", "dve", "pool", "sp")

    def __init__(self, nc):
        self.nc = nc
        self.streams = {e: [] for e in self.ENGS}
        self.sem = {}
        self.cnt = {e: 0 for e in self.ENGS}
        self.waited = {e: {} for e in self.ENGS}
        for e in self.ENGS:
            self.sem[e] = nc.alloc_semaphore(name=f"s_{e}")

    def dma_sem(self, name=None):
        self._n = getattr(self, "_n", 0) + 1
        return DmaSem(self.nc.alloc_semaphore(name=f"{name}_{self._n}"))

    def _need(self, eng, tokens):
        wd = self.waited[eng]
        best = {}
        for t in tokens:
            if t is None:
                continue
            sem, val = t
            k = id(sem)
            if wd.get(k, 0) >= val:
                continue
            if k not in best or best[k][1] < val:
                best[k] = (sem, val)
        out = []
        for k, (sem, val) in best.items():
            wd[k] = val
            out.append((sem, val))
        return out

    @staticmethod
    def _deps(reads, writes):
        toks = []
        for r in reads:
            toks.append(r.w)
        for w in writes:
            toks.append(w.w)
            toks.extend(w.r)
        return toks

    def op(self, eng, fn, reads=(), writes=(), inc=True, touch=()):
        waits = self._need(eng, self._deps(reads, writes))
        tok = (self.sem[eng], self.cnt[eng] + 1)
        if inc:
            self.cnt[eng] += 1
        self.streams[eng].append((waits, fn, (self.sem[eng], 1) if inc else None))
        for r in reads:
            r.r.append(tok)
        for w in writes:
            w.w = tok
            w.r = []
        for w in touch:
            w.w = tok
        return tok

    def dma(self, q, fn, dsem, reads=(), writes=(), inc=16, touch=()):
        waits = self._need(q, self._deps(reads, writes))
        dsem.val += inc
        tok = (dsem.sem, dsem.val)
        self.streams[q].append((waits, fn, (dsem.sem, inc)))
        for r in reads:
            r.r.append(tok)
        for w in writes:
            w.w = tok
            w.r = []
        for w in touch:
            w.w = tok
        return tok

    def wait_all(self, eng, tokens):
        waits = self._need(eng, tokens)
        if waits:
            self.streams[eng].append((waits, None, None))

    def emit(self):
        nc = self.nc
        streams = self.streams
        self.streams = {e: [] for e in self.ENGS}

        def run(engobj, lst):
            for waits, fn, inc in lst:
                for sem, val in waits:
                    engobj.wait_ge(sem, val)
                if fn is not None:
                    inst = fn(engobj)
                    if inc is not None:
                        inst.then_inc(inc[0], inc[1])

        with nc.Block() as block:
            @block.tensor
            def _(e):
                run(e, streams["pe"])

            @block.scalar
            def _(e):
                run(e, streams["act"])

            @block.vector
            def _(e):
                run(e, streams["dve"])

            @block.gpsimd
            def _(e):
                run(e, streams["pool"])

            @block.sync
            def _(e):
                run(e, streams["sp"])


class Ring:
    def __init__(self, P, nc, name, shape, dt, n, psum=False, dma=False, alloc=None):
        self.items = []
        for i in range(n):
            if alloc is not None:
                t = alloc(f"{name}{i}", shape, dt)
            else:
                t = (nc.alloc_psum_tensor if psum else nc.alloc_sbuf_tensor)(f"{name}{i}", shape, dt)
            self.items.append((t, Res(f"{name}{i}"), P.dma_sem(f"ds_{name}{i}") if dma else None))
        self.i = 0

    def next(self):
        it = self.items[self.i % len(self.items)]
        self.i += 1
        return it


_NC_CACHE = {}


def build_program(n_phys=10240, stop=""):
    nc = bass.Bass("TRN2", target_bir_lowering=False)
    P = Prog(nc)

    def din(name, shape, dt=F32):
        return nc.dram_tensor(name, list(shape), dt, kind="ExternalInput").ap()

    def dout(name, shape, dt=F32):
        return nc.dram_tensor(name, list(shape), dt, kind="ExternalOutput").ap()

    xp = din("xp", [4, 514, D])
    xs = din("xs", [64, D])
    sconv = din("sconv", [32, D])
    ptab = din("ptab", [1, 16 * NPAGE], I32)
    HALFQ = (n_phys // 2) * 32
    cache_q = [din(f"cache_q{i}", [HALFQ, 4 * 320]) for i in range(2)]
    sel_d = din("sel", [128, 4])
    iota32_d = din("iota32", [128, 1])
    w_ffn1_gu = din("w_ffn1_gu", [2, D, 2 * DFF])
    w_ffn1_down = din("w_ffn1_down", [2, DFF, D])
    w_ffn2_gu = din("w_ffn2_gu", [2, D, 2 * DFF])
    w_ffn2_down = din("w_ffn2_down", [2, DFF, D])
    w_conv_in = din("w_conv_in", [1, D, 3 * D])
    w_conv_out = din("w_conv_out", [1, D, D])
    w_dq = din("w_dq", [1, D, 512])
    w_uq = din("w_uq", [1, 512, 1536])
    w_o = din("w_o", [1, D, D])
    w_dkv = din("w_dkv", [D, 320])
    w_uk = din("w_uk", [256, D])
    w_uv = din("w_uv", [256, D])
    gvec_d = din("gvec", [88, 128])
    q_norm = din("q_norm", [1, 512])
    q_nope_norm = din("q_nope_norm", [1, 128])
    q_pe_norm = din("q_pe_norm", [1, 64])
    kv_norm = din("kv_norm", [256])
    k_pe_norm = din("k_pe_norm", [64])
    ident_d = din("ident", [128, 128])
    iota_d = din("iota", [128, 1], I32)
    cs_p_d = din("cs_p", [4, 512, 64])
    cs_s_d = din("cs_s", [64, 64])
    mask_p_d = din("mask_p", [128, 16 * 512], BF16)
    mask_n_d = din("mask_n", [64, 512], BF16)

    y_p = dout("y_p", [4, 512, D])
    y_s = dout("y_s", [64, D])
    cso_p = dout("cso_p", [4, 2, D])
    cso_s = dout("cso_s", [32, D])
    lat_p = dout("lat_p", [4, 512, 256])
    kpe_p = dout("kpe_p", [4, 512, 64])
    lat_s = dout("lat_s", [64, 256])
    kpe_s = dout("kpe_s", [64, 64])

    xsc = nc.dram_tensor("xsc", [NT, 128, 8 * 512], F32).ap()
    sendb = [nc.dram_tensor(f"sendb{c}", [128, 1024], F32) for c in range(3)]
    recvb = [nc.dram_tensor(f"recvb{c}", [512, 1024], F32) for c in range(3)]
    r_xsc = [Res(f"xsc{t}") for t in range(NT)]
    r_sendb = Res("sendb")
    r_recvb = Res("recvb")
    r_out = Res("out")
    out_tokens = []

    def sb(name, shape, dt=F32):
        return nc.alloc_sbuf_tensor(name, list(shape), dt)

    ident_f = sb("ident_f", [128, 128]); r_ident_f = Res()
    ident_b = sb("ident_b", [128, 128], BF16); r_ident_b = Res()
    ones_b = sb("ones_b", [128, 128], BF16); r_ones = Res()
    eps_t = sb("eps_t", [128, 1]); r_eps = Res()
    iota_t = sb("iota_t", [128, 1], I32); r_iota = Res()
    gv = sb("gv", [128, 88]); r_gv = Res()
    qn_bc = sb("qn_bc", [128, 512]); qnn_bc = sb("qnn_bc", [128, 128]); qpn_bc = sb("qpn_bc", [128, 64])
    kvn_bc = sb("kvn_bc", [128, 256]); kpn_bc = sb("kpn_bc", [128, 64]); r_bc = Res()
    r_oT = [Res(f"oT{t}") for t in range(NT)]
    QNS = sb("QNS", [128, 8, 64], BF16); r_QNS = Res("QNS")
    qsc_n = nc.dram_tensor("qsc_n", [8, 128, 2048], BF16).ap(); r_qsc_n = [Res(f"qscn{t}") for t in range(4)]
    qsc_p = nc.dram_tensor("qsc_p", [4, 128, 2048], BF16).ap(); r_qsc_p = [Res(f"qscp{t}") for t in range(4)]
    qpT_s = sb("qpT_s", [64, 8, 64], BF16); r_qpT_s = Res()
    latn_aug = sb("latn_aug", [64, 257], BF16); r_latn = Res()
    latnT = sb("latnT", [128, 3, 64], BF16); r_latnT = Res()
    ptab_t = sb("ptab_t", [128, 16 * NPAGE], I32); r_ptab = Res()

    PSALL = [nc.alloc_psum_tensor(f"ps{i}", [128, 512], F32) for i in range(8)]
    PSRES = [Res(f"ps{i}") for i in range(8)]

    class PsRing:
        def __init__(self, idxs):
            self.idxs = idxs
            self.i = 0

        def next(self):
            k = self.idxs[self.i % len(self.idxs)]
            self.i += 1
            return PSALL[k], PSRES[k], None

    PS = PsRing(list(range(8)))
    DS = [P.dma_sem(f"dso{i}") for i in range(6)]
    dsi = [0]

    def ods():
        dsi[0] += 1
        return DS[dsi[0] % len(DS)]

    s0 = P.dma_sem("setup")
    P.dma("sp", lambda e: e.dma_start(out=ident_f[:], in_=ident_d), s0, writes=[r_ident_f])
    P.dma("sp", lambda e: e.dma_start(out=iota_t[:], in_=iota_d), s0, writes=[r_iota])
    P.dma("sp", lambda e: e.dma_start(out=qn_bc[:], in_=q_norm[0].partition_broadcast(128)), s0, writes=[r_bc])
    P.dma("sp", lambda e: e.dma_start(out=qnn_bc[:], in_=q_nope_norm[0].partition_broadcast(128)), s0, writes=[r_bc])
    P.dma("sp", lambda e: e.dma_start(out=qpn_bc[:], in_=q_pe_norm[0].partition_broadcast(128)), s0, writes=[r_bc])
    P.dma("sp", lambda e: e.dma_start(out=kvn_bc[:], in_=kv_norm.partition_broadcast(128)), s0, writes=[r_bc])
    P.dma("sp", lambda e: e.dma_start(out=kpn_bc[:], in_=k_pe_norm.partition_broadcast(128)), s0, writes=[r_bc])
    P.dma("sp", lambda e: e.dma_start(out=ptab_t[:], in_=ptab[0].partition_broadcast(128)), s0, writes=[r_ptab])
    gv_in = sb("gv_in", [88, 128]); r_gv_in = Res()
    P.dma("sp", lambda e: e.dma_start(out=gv_in[:], in_=gvec_d), s0, writes=[r_gv_in])
    for _r in (r_ident_f, r_iota, r_bc, r_ptab, r_gv_in):
        _r.w = (s0.sem, s0.val)
    P.op("dve", lambda e: e.tensor_copy(out=ident_b[:], in_=ident_f[:]), reads=[r_ident_f], writes=[r_ident_b])
    P.op("dve", lambda e: e.memset(ones_b[:], 1.0), writes=[r_ones])
    P.op("dve", lambda e: e.memset(eps_t[:], EPS), writes=[r_eps])
    P.op("dve", lambda e: e.memset(latn_aug[:, 256:257], 1.0), writes=[r_latn])
    pst, pr, _ = PS.next()
    P.op("pe", lambda e: e.transpose(out=pst[:, 0:88], in_=gv_in[:], identity=ident_f[0:88, 0:88]),
         reads=[r_gv_in, r_ident_f], writes=[pr])
    P.op("dve", lambda e: e.tensor_copy(out=gv[:], in_=pst[:, 0:88]), reads=[pr], writes=[r_gv])
    GV = dict(nf1=(0, 8), nm=(16, 24), nf2=(32, 40), nkv=48, cw=(56, 64, 72), kn=80)

    def psb(pt):
        return pt[:, :].bitcast(BF16)

    def rstd_from_ss(ss_ap, out_ap, n_over, res_in, res_out, np_=128):
        P.op("act", lambda e: e.activation(out=out_ap, in_=ss_ap, func=AF.Sqrt, bias=eps_t[0:np_, 0:1], scale=1.0 / n_over),
             reads=res_in + [r_eps], writes=[res_out])
        P.op("dve", lambda e: e.reciprocal(out=out_ap, in_=out_ap), reads=[res_out], writes=[res_out])

    uniq = [0]

    def make_common(es):
        uniq[0] += 1
        sfx = f"_{uniq[0]}"

        def sbx(name, shape, dt=F32):
            return es.enter_context(nc.sbuf_tensor(name + sfx, list(shape), dt))
        C = {}
        C["sbx"] = sbx
        C["wA"] = Ring(P, nc, "wA", [128, 8, 512], BF16, 4, dma=True, alloc=sbx)
        C["X"] = sbx("X", [128, 8, 512]); C["r_X"] = [Res(f"X{k}") for k in range(8)]
        C["XH"] = sbx("XH", [128, 8, 2]); C["r_XH"] = Res("XH")
        C["XN"] = sbx("XN", [128, 8, 514], BF16); C["r_XN"] = Res("XN")
        C["RSTD"] = sbx("RSTD", [128, 514]); C["r_RSTD"] = Res("RSTD")
        C["Hh"] = sbx("Hh", [128, NJ, 514], BF16); C["r_H"] = [Res(f"H{j}") for j in range(NJ)]
        C["SG"] = Ring(P, nc, "SG", [128, 514], F32, 2, alloc=sbx)
        return C

    def load_wA(C, src_ap, ncols=512, nk=8):
        t, r, s = C["wA"].next()
        P.dma("pool", lambda e: e.dma_start(out=t[:, 0:nk, 0:ncols], in_=src_ap), s, writes=[r])
        return t, r

    def wview(w2d, c0, ncols):
        return w2d.rearrange("(k p) c -> p k c", p=128)[:, :, c0:c0 + ncols]

    def segs(N, halo):
        s = [(0, N)]
        if halo:
            s.append((512, 2))
        return s

    def rmsnorm_fm(C, N, halo, gcol):
        X, XH, XN, RSTD, Hh = C["X"], C["XH"], C["XN"], C["RSTD"], C["Hh"]
        r_X, r_XH, r_XN, r_RSTD, r_H = C["r_X"], C["r_XH"], C["r_XN"], C["r_RSTD"], C["r_H"]
        for (c0, n) in segs(N, halo):
            src = (lambda k, n=n: X[:, k, 0:n]) if c0 == 0 else (lambda k, n=n: XH[:, k, 0:n])
            rsrc = r_X if c0 == 0 else [r_XH] * 8
            for k in range(8):
                P.op("dve", lambda e, k=k, src=src, c0=c0, n=n: e.tensor_tensor(out=Hh[:, k, c0:c0 + n], in0=src(k), in1=src(k), op=ALU.mult),
                     reads=[rsrc[k]], writes=[r_H[k]])
            pt, pr, _ = PS.next()
            for k in range(8):
                P.op("pe", lambda e, k=k, pt=pt, c0=c0, n=n: e.matmul(pt[:, 0:n], ones_b[:], Hh[:, k, c0:c0 + n], start=(k == 0), stop=(k == 7)),
                     reads=r_H[0:8] + [r_ones] if k == 0 else [], writes=[pr] if k == 0 else [], inc=(k == 7))
            rstd_from_ss(pt[:, 0:n], RSTD[:, c0:c0 + n], float(D), [pr], r_RSTD)
            for k in range(8):
                P.op("dve", lambda e, k=k, src=src, c0=c0, n=n: e.scalar_tensor_tensor(out=XN[:, k, c0:c0 + n], in0=src(k), scalar=gv[:, gcol + k:gcol + k + 1],
                                                                                         in1=RSTD[:, c0:c0 + n], op0=ALU.mult, op1=ALU.mult),
                     reads=[rsrc[k], r_gv, r_RSTD], writes=[r_XN])

    def ffn(C, N, halo, w_gu, w_down):
        X, XH, XN, Hh, SG = C["X"], C["XH"], C["XN"], C["Hh"], C["SG"]
        r_X, r_XH, r_XN, r_H = C["r_X"], C["r_XH"], C["r_XN"], C["r_H"]
        sg = segs(N, halo)
        for jb in range(0, NJ, 4):
            nj = min(4, NJ - jb)
            tg, rg = load_wA(C, wview(w_gu, jb * 128, nj * 128), nj * 128)
            tu, ru = load_wA(C, wview(w_gu, DFF + jb * 128, nj * 128), nj * 128)
            for jj in range(nj):
                j = jb + jj
                for (c0, n) in sg:
                    pg, prg, _ = PS.next()
                    pu, pru, _ = PS.next()
                    for (pt, pr, tw, rw) in ((pg, prg, tg, rg), (pu, pru, tu, ru)):
                        for k in range(8):
                            P.op("pe", lambda e, pt=pt, tw=tw, k=k, jj=jj, c0=c0, n=n: e.matmul(pt[:, 0:n], tw[:, k, jj * 128:(jj + 1) * 128], XN[:, k, c0:c0 + n],
                                                                                                  start=(k == 0), stop=(k == 7)),
                                 reads=[rw, r_XN] if k == 0 else [], writes=[pr] if k == 0 else [], inc=(k == 7))
                    st, sr, _ = SG.next()
                    P.op("act", lambda e, st=st, pg=pg, n=n: e.activation(out=st[:, 0:n], in_=pg[:, 0:n], func=AF.Silu), reads=[prg], writes=[sr])
                    P.op("dve", lambda e, st=st, pu=pu, j=j, c0=c0, n=n: e.tensor_tensor(out=Hh[:, j, c0:c0 + n], in0=st[:, 0:n], in1=pu[:, 0:n], op=ALU.mult),
                         reads=[sr, pru], writes=[r_H[j]])
        for mh in range(2):
            accs = [[PS.next() for _ in sg] for _ in range(4)]
            for jb in range(0, NJ, 4):
                nj = min(4, NJ - jb)
                src = w_down[jb * 128:(jb + nj) * 128, mh * 512:(mh + 1) * 512].rearrange("(j p) m -> p j m", p=128)
                tw, rw = load_wA(C, src, 512, nj)
                for jj in range(nj):
                    j = jb + jj
                    for mm in range(4):
                        for si, (c0, n) in enumerate(sg):
                            pt, pr, _ = accs[mm][si]
                            first = (j == 0)
                            last = (j == NJ - 1)
                            P.op("pe", lambda e, pt=pt, tw=tw, jj=jj, mm=mm, j=j, c0=c0, n=n, first=first, last=last:
                                 e.matmul(pt[:, 0:n], tw[:, jj, mm * 128:(mm + 1) * 128], Hh[:, j, c0:c0 + n], start=first, stop=last),
                                 reads=([rw] if jj == 0 else []) + [r_H[j]], writes=[pr] if first else [], touch=[] if first else [pr],
                                 inc=(last or (jj == nj - 1 and mm == 3 and si == len(sg) - 1)))
            for mm in range(4):
                m = mh * 4 + mm
                for si, (c0, n) in enumerate(sg):
                    pt, pr, _ = accs[mm][si]
                    if c0 == 0:
                        P.op("dve", lambda e, pt=pt, m=m, n=n: e.scalar_tensor_tensor(out=X[:, m, 0:n], in0=pt[:, 0:n], scalar=0.5, in1=X[:, m, 0:n],
                                                                                       op0=ALU.mult, op1=ALU.add), reads=[pr, r_X[m]], writes=[r_X[m]])
                    else:
                        P.op("dve", lambda e, pt=pt, m=m, n=n: e.scalar_tensor_tensor(out=XH[:, m, 0:n], in0=pt[:, 0:n], scalar=0.5, in1=XH[:, m, 0:n],
                                                                                       op0=ALU.mult, op1=ALU.add), reads=[pr, r_XH], writes=[r_XH])

    def rope_tm(src_ap, dst_ap, nh, csv, np_, rd, wr, tmp_ap, r_tmp):
        cosb = csv[:, 0:32].unsqueeze(1).to_broadcast([np_, nh, 32])
        sinb = csv[:, 32:64].unsqueeze(1).to_broadcast([np_, nh, 32])
        x1, x2 = src_ap[:, :, 0:32], src_ap[:, :, 32:64]
        o1, o2 = dst_ap[:, :, 0:32], dst_ap[:, :, 32:64]
        t1, t2 = tmp_ap[:, :, 0:32], tmp_ap[:, :, 32:64]
        P.op("dve", lambda e: e.tensor_tensor(out=t1, in0=x2, in1=sinb, op=ALU.mult), reads=rd, writes=[r_tmp])
        P.op("dve", lambda e: e.tensor_tensor(out=t2, in0=x1, in1=sinb, op=ALU.mult), reads=rd + [r_tmp], writes=[r_tmp])
        P.op("dve", lambda e: e.tensor_tensor(out=t1, in0=x1, in1=cosb, op=ALU.mult) if False else e.tensor_tensor(out=o1, in0=x1, in1=cosb, op=ALU.mult), reads=rd, writes=wr)
        P.op("dve", lambda e: e.tensor_tensor(out=o2, in0=x2, in1=cosb, op=ALU.mult), reads=rd + wr, writes=wr)
        P.op("dve", lambda e: e.tensor_tensor(out=o1, in0=o1, in1=t1, op=ALU.subtract), reads=[r_tmp] + wr, writes=wr)
        P.op("dve", lambda e: e.tensor_tensor(out=o2, in0=o2, in1=t2, op=ALU.add), reads=[r_tmp] + wr, writes=wr)

    with ExitStack() as es:
        C = make_common(es)
        sbx = C["sbx"]
        X, XH, XN, Hh = C["X"], C["XH"], C["XN"], C["Hh"]
        r_X, r_XH, r_XN, r_H = C["r_X"], C["r_XH"], C["r_XN"], C["r_H"]
        wUQ = sbx("wUQ", [128, 4, 1536], BF16); r_wUQ = Res(); s_wUQ = P.dma_sem("wUQ")
        XINr = Ring(P, nc, "XIN", [128, D], F32, 2, dma=True, alloc=sbx)
        XHI = sbx("XHI", [32, D]); r_XHI = Res("XHI"); s_XHI = P.dma_sem("XHI")
        UEXT = sbx("UEXT", [128, 8, 516]); r_UEXT = [Res(f"U{k}") for k in range(8)]
        HS = Ring(P, nc, "HS", [128, 514], F32, 2, alloc=sbx)
        YC = Ring(P, nc, "YC", [128, 512], F32, 2, alloc=sbx)
        CSO = sbx("CSO", [128, 8, 32]); r_CSO = Res("CSO")
        CSO2 = sbx("CSO2", [32, D]); r_CSO2 = Res("CSO2"); s_CSO2 = P.dma_sem("CSO2")
        CKV = sbx("CKV", [128, 320]); r_CKV = Res("CKV")
        SQT = sbx("SQT", [128, 1536]); r_SQT = Res("SQT")
        SS = sbx("SS", [128, 16]); r_SS = Res("SS")
        RS = sbx("RS", [128, 16]); r_RS = Res("RS")
        LATO = Ring(P, nc, "LATO", [128, 320], F32, 2, dma=True, alloc=sbx)
        TMP = sbx("TMP", [128, 1024]); r_TMP = Res("TMP")
        TMP2 = sbx("TMP2", [128, 512]); r_TMP2 = Res("TMP2")
        LB = sbx("LB", [128, 384], BF16); r_LB = Res("LB")
        ST = sbx("ST", [128, 3, 512], BF16); r_ST = Res("ST"); s_ST = P.dma_sem("ST")
        CS = sbx("CS", [128, 4, 64]); r_CS = Res("CS"); s_CS = P.dma_sem("CS")
        CQ = sbx("CQ", [128, 512]); r_CQ = Res("CQ")
        CQB = sbx("CQB", [128, 512], BF16); r_CQB = Res("CQB")
        CQT = sbx("CQT", [128, 4, 512], BF16); r_CQT = Res("CQT")
        QS = sbx("QS", [128, 1536]); r_QS = Res("QS")
        QNB = sbx("QNB", [128, 1024], BF16); r_QNB = Res("QNB")
        QPF = sbx("QPF", [128, 512]); r_QPF = Res("QPF")
        QPB = sbx("QPB", [128, 512], BF16); r_QPB = Res("QPB")
        QTN = sbx("QTN", [128, 8, 512], BF16); r_QTN = Res("QTN"); s_QTN = P.dma_sem("QTN")
        QTP = sbx("QTP", [128, 4, 512], BF16); r_QTP = Res("QTP"); s_QTP = P.dma_sem("QTP"); s_XS = P.dma_sem("XS")

        def ZB(k, N):
            return Hh[:, 8 + k, 0:N]

        def load_x(t):
            N = 512 if t < 4 else 64
            nb, bp = (4, 128) if t < 4 else (1, 64)
            blocks = []
            for b in range(nb):
                xt, xr, xsem = XINr.next()
                if t < 4:
                    P.dma("sp", lambda e, xt=xt, b=b: e.dma_start(out=xt[:, :], in_=xp[t, 2 + b * 128:2 + (b + 1) * 128, :]), xsem, writes=[xr])
                else:
                    P.dma("sp", lambda e, xt=xt: e.dma_start(out=xt[0:64, :], in_=xs), xsem, writes=[xr])
                for half in range(2):
                    pt, pr, _ = PS.next()
                    for kk in range(4):
                        k = half * 4 + kk
                        P.op("pe", lambda e, pt=pt, xt=xt, k=k, kk=kk: e.transpose(out=pt[:, kk * 128:kk * 128 + bp], in_=xt[0:bp, k * 128:(k + 1) * 128], identity=ident_f[0:bp, 0:bp]),
                             reads=[xr, r_ident_f] if kk == 0 else [], writes=[pr] if kk == 0 else [], inc=(kk == 3))
                    P.op("act", lambda e, pt=pt, half=half, b=b: e.activation(out=X[:, half * 4:half * 4 + 4, b * bp:(b + 1) * bp],
                                                                             in_=pt[:, :].rearrange("p (k n) -> p k n", n=128)[:, :, 0:bp], func=AF.Copy),
                         reads=[pr], writes=r_X[half * 4:half * 4 + 4])
            if t < 4:
                P.dma("sp", lambda e: e.dma_start(out=XHI[0:2, :], in_=xp[t, 0:2, :]), s_XHI, writes=[r_XHI])
                pt, pr, _ = PS.next()
                for k in range(8):
                    P.op("pe", lambda e, pt=pt, k=k: e.transpose(out=pt[:, 2 * k:2 * k + 2], in_=XHI[0:2, k * 128:(k + 1) * 128], identity=ident_f[0:2, 0:2]),
                         reads=[r_XHI, r_ident_f] if k == 0 else [], writes=[pr] if k == 0 else [], inc=(k == 7))
                P.op("act", lambda e, pt=pt: e.activation(out=XH[:, :, :], in_=pt[:, 0:16].rearrange("p (k c) -> p k c", c=2), func=AF.Copy), reads=[pr], writes=[r_XH])
            else:
                P.dma("sp", lambda e: e.dma_start(out=XHI[0:32, :], in_=sconv), s_XHI, writes=[r_XHI])
                for k in range(8):
                    pt, pr, _ = PS.next()
                    P.op("pe", lambda e, pt=pt, k=k: e.transpose(out=pt[:, 0:32], in_=XHI[0:32, k * 128:(k + 1) * 128], identity=ident_f[0:32, 0:32]),
                         reads=[r_XHI, r_ident_f], writes=[pr])
                    P.op("act", lambda e, pt=pt, k=k: e.activation(out=UEXT[:, k, 0:96].rearrange("p (b j) -> p b j", j=6)[:, :, 0:2],
                                                                   in_=pt[:, 0:32].rearrange("p (b j) -> p b j", j=2), func=AF.Copy), reads=[pr], writes=[r_UEXT[k]])

        def conv_mixer(t):
            N = 512 if t < 4 else 64
            halo = t < 4
            nseq, L = (1, 512) if t < 4 else (16, 4)
            W = w_conv_in[0].rearrange("(k p) c -> p k c", p=128)
            rmsnorm_fm(C, N, halo, GV["nm"][0])

            def uview(k, lo, hi):
                return UEXT[:, k, 0:nseq * (L + 2)].rearrange("p (b j) -> p b j", j=L + 2)[:, :, lo:hi]

            def nview(ap2d):
                return ap2d.rearrange("p (b j) -> p b j", j=L)

            cw = GV["cw"]
            for m in range(8):
                tw, rw = load_wA(C, W[:, :, m * 128:(m + 1) * 128], 128)
                tw2, rw2 = load_wA(C, W[:, :, D + m * 128:D + (m + 1) * 128], 128)
                tw3, rw3 = load_wA(C, W[:, :, 2 * D + m * 128:2 * D + (m + 1) * 128], 128)
                pb, prb, _ = PS.next()
                pc, prc, _ = PS.next()
                ph, prh, _ = PS.next()
                for (pt, pr, twx, rwx) in ((pb, prb, tw, rw), (pc, prc, tw2, rw2), (ph, prh, tw3, rw3)):
                    for k in range(8):
                        P.op("pe", lambda e, pt=pt, twx=twx, k=k: e.matmul(pt[:, 0:N], twx[:, k, 0:128], XN[:, k, 0:N], start=(k == 0), stop=(k == 7)),
                             reads=[rwx, r_XN] if k == 0 else [], writes=[pr] if k == 0 else [], inc=(k == 7))
                hs, hr, _ = HS.next()
                P.op("act", lambda e, hs=hs, ph=ph: e.activation(out=hs[:, 0:N], in_=ph[:, 0:N], func=AF.Copy), reads=[prh], writes=[hr])
                P.op("dve", lambda e, hs=hs, pc=pc, m=m: e.tensor_tensor(out=uview(m, 2, L + 2), in0=nview(pc[:, 0:N]), in1=nview(hs[:, 0:N]), op=ALU.mult),
                     reads=[prc, hr], writes=[r_UEXT[m]])
                if halo:
                    pc2, prc2, _ = PS.next()
                    ph2, prh2, _ = PS.next()
                    for (pt, pr, twx, rwx) in ((pc2, prc2, tw2, rw2), (ph2, prh2, tw3, rw3)):
                        for k in range(8):
                            P.op("pe", lambda e, pt=pt, twx=twx, k=k: e.matmul(pt[:, 0:2], twx[:, k, 0:128], XN[:, k, 512:514], start=(k == 0), stop=(k == 7)),
                                 reads=[rwx, r_XN] if k == 0 else [], writes=[pr] if k == 0 else [], inc=(k == 7))
                    hs2, hr2, _ = HS.next()
                    P.op("act", lambda e, hs2=hs2, ph2=ph2: e.activation(out=hs2[:, 0:2], in_=ph2[:, 0:2], func=AF.Copy), reads=[prh2], writes=[hr2])
                    P.op("dve", lambda e, hs2=hs2, pc2=pc2, m=m: e.tensor_tensor(out=UEXT[:, m, 0:2], in0=pc2[:, 0:2], in1=hs2[:, 0:2], op=ALU.mult),
                         reads=[prc2, hr2, r_UEXT[m]], writes=[r_UEXT[m]])
                P.op("act", lambda e, m=m: e.activation(out=CSO[:, m, 0:2 * nseq].rearrange("p (b j) -> p b j", j=2), in_=uview(m, L, L + 2), func=AF.Copy),
                     reads=[r_UEXT[m]], writes=[r_CSO])
                yc, yr, _ = YC.next()
                P.op("dve", lambda e, yc=yc, m=m: e.tensor_scalar(out=nview(yc[:, 0:N]), in0=uview(m, 0, L), scalar1=gv[:, cw[0] + m:cw[0] + m + 1], scalar2=None, op0=ALU.mult),
                     reads=[r_UEXT[m], r_gv], writes=[yr])
                for jx in (1, 2):
                    P.op("dve", lambda e, yc=yc, m=m, jx=jx: e.scalar_tensor_tensor(out=nview(yc[:, 0:N]), in0=uview(m, jx, L + jx), scalar=gv[:, cw[jx] + m:cw[jx] + m + 1],
                                                                                     in1=nview(yc[:, 0:N]), op0=ALU.mult, op1=ALU.add),
                         reads=[r_UEXT[m], r_gv, yr], writes=[yr])
                P.op("dve", lambda e, yc=yc, pb=pb, m=m: e.tensor_tensor(out=ZB(m, N), in0=yc[:, 0:N], in1=pb[:, 0:N], op=ALU.mult),
                     reads=[yr, prb], writes=[r_H[8 + m]])
            n = 2 * nseq
            for half in range(2):
                pt, pr, _ = PS.next()
                for kk in range(4):
                    k = half * 4 + kk
                    P.op("pe", lambda e, pt=pt, k=k, kk=kk: e.transpose(out=pt[0:n, kk * 128:(kk + 1) * 128], in_=CSO[:, k, 0:n], identity=ident_f[:, :]),
                         reads=[r_CSO, r_ident_f] if kk == 0 else [], writes=[pr] if kk == 0 else [], inc=(kk == 3))
                P.op("act", lambda e, pt=pt, half=half: e.activation(out=CSO2[0:n, half * 512:(half + 1) * 512], in_=pt[0:n, :], func=AF.Copy), reads=[pr], writes=[r_CSO2])
            dst = cso_p[t] if t < 4 else cso_s
            out_tokens.append(P.dma("sp", lambda e: e.dma_start(out=dst, in_=CSO2[0:n, :]), s_CSO2, reads=[r_CSO2], writes=[r_out]))
            Wo = w_conv_out[0]
            for mb in range(0, 8, 4):
                tw, rw = load_wA(C, wview(Wo, mb * 128, 512))
                for mm in range(4):
                    m = mb + mm
                    pt, pr, _ = PS.next()
                    for k in range(8):
                        P.op("pe", lambda e, pt=pt, tw=tw, k=k, mm=mm: e.matmul(pt[:, 0:N], tw[:, k, mm * 128:(mm + 1) * 128], ZB(k, N), start=(k == 0), stop=(k == 7)),
                             reads=[rw] + r_H[8:16] if k == 0 else [], writes=[pr] if k == 0 else [], inc=(k == 7))
                    P.op("dve", lambda e, pt=pt, m=m: e.tensor_tensor(out=X[:, m, 0:N], in0=pt[:, 0:N], in1=X[:, m, 0:N], op=ALU.add), reads=[pr, r_X[m]], writes=[r_X[m]])

        def load_cs(t):
            if t < 4:
                P.dma("sp", lambda e: e.dma_start(out=CS[:, :, :], in_=cs_p_d[t].rearrange("(b p) c -> p b c", p=128)), s_CS, writes=[r_CS])
            else:
                P.dma("sp", lambda e: e.dma_start(out=CS[0:64, 0, :], in_=cs_s_d), s_CS, writes=[r_CS])

        def shared_kv(t):
            N = 512 if t < 4 else 64
            nb, bp = (4, 128) if t < 4 else (1, 64)
            rmsnorm_fm(C, N, False, GV["nkv"])
            tw, rw = load_wA(C, w_dkv.rearrange("(k p) c -> p k c", p=128), 320)
            load_cs(t)
            for b in range(nb):
                pt, pr, _ = PS.next()
                for k in range(8):
                    P.op("pe", lambda e, pt=pt, k=k, b=b: e.matmul(pt[0:bp, 0:320], XN[:, k, b * bp:(b + 1) * bp], tw[:, k, 0:320], start=(k == 0), stop=(k == 7)),
                         reads=[rw, r_XN] if k == 0 else [], writes=[pr] if k == 0 else [], inc=(k == 7))
                P.op("act", lambda e, pt=pt: e.activation(out=CKV[0:bp, :], in_=pt[0:bp, 0:320], func=AF.Copy), reads=[pr], writes=[r_CKV])
                P.op("dve", lambda e: e.tensor_tensor(out=SQT[0:bp, 0:320], in0=CKV[0:bp, :], in1=CKV[0:bp, :], op=ALU.mult), reads=[r_CKV], writes=[r_SQT])
                P.op("dve", lambda e: e.tensor_reduce(out=SS[0:bp, 0:1], in_=SQT[0:bp, 0:256], axis=AX.X, op=ALU.add), reads=[r_SQT], writes=[r_SS])
                P.op("dve", lambda e: e.tensor_reduce(out=SS[0:bp, 1:2], in_=SQT[0:bp, 256:320], axis=AX.X, op=ALU.add), reads=[r_SQT, r_SS], writes=[r_SS])
                rstd_from_ss(SS[0:bp, 0:1], RS[0:bp, 0:1], 256.0, [r_SS], r_RS, bp)
                rstd_from_ss(SS[0:bp, 1:2], RS[0:bp, 1:2], 64.0, [r_SS, r_RS], r_RS, bp)
                lo, lr, los = LATO.next()
                P.op("dve", lambda e, lo=lo: e.scalar_tensor_tensor(out=lo[0:bp, 0:256], in0=CKV[0:bp, 0:256], scalar=RS[0:bp, 0:1], in1=kvn_bc[0:bp, :],
                                                                  op0=ALU.mult, op1=ALU.mult), reads=[r_CKV, r_RS, r_bc], writes=[lr])
                P.op("dve", lambda e: e.scalar_tensor_tensor(out=TMP[0:bp, 0:64], in0=CKV[0:bp, 256:320], scalar=RS[0:bp, 1:2], in1=kpn_bc[0:bp, :],
                                                           op0=ALU.mult, op1=ALU.mult), reads=[r_CKV, r_RS, r_bc], writes=[r_TMP])
                rope_tm(TMP[0:bp, 0:64].unsqueeze(1), lo[0:bp, 256:320].unsqueeze(1), 1, CS[0:bp, b, :], bp, [r_TMP, r_CS], [lr], TMP2[0:bp, 0:64].unsqueeze(1), r_TMP2)
                if t < 4:
                    out_tokens.append(P.dma("sp", lambda e, lo=lo, b=b: e.dma_start(out=lat_p[t, b * 128:(b + 1) * 128, :], in_=lo[:, 0:256]), los, reads=[lr], writes=[r_out]))
                    out_tokens.append(P.dma("sp", lambda e, lo=lo, b=b: e.dma_start(out=kpe_p[t, b * 128:(b + 1) * 128, :], in_=lo[:, 256:320]), los, reads=[lr], writes=[r_out]))
                else:
                    out_tokens.append(P.dma("sp", lambda e, lo=lo: e.dma_start(out=lat_s, in_=lo[0:64, 0:256]), los, reads=[lr], writes=[r_out]))
                    out_tokens.append(P.dma("sp", lambda e, lo=lo: e.dma_start(out=kpe_s, in_=lo[0:64, 256:320]), los, reads=[lr], writes=[r_out]))
                    P.op("act", lambda e, lo=lo: e.activation(out=latn_aug[0:64, 0:256], in_=lo[0:64, 0:256], func=AF.Copy), reads=[lr, r_latn], writes=[r_latn])
                P.op("act", lambda e, lo=lo: e.activation(out=LB[0:bp, 0:320], in_=lo[0:bp, 0:320], func=AF.Copy), reads=[lr], writes=[r_LB])
                P.op("act", lambda e, lo=lo: e.activation(out=LB[0:bp, 320:384], in_=lo[0:bp, 256:320], func=AF.Copy), reads=[lr, r_LB], writes=[r_LB])
                pt2, pr2, _ = PS.next()
                pv = psb(pt2)
                for c in range(3):
                    P.op("pe", lambda e, pv=pv, c=c: e.transpose(out=pv[:, c * 128:c * 128 + bp], in_=LB[0:bp, c * 128:(c + 1) * 128], identity=ident_b[0:bp, 0:bp]),
                         reads=[r_LB, r_ident_b] if c == 0 else [], writes=[pr2] if c == 0 else [], inc=(c == 2))
                if t < 4:
                    P.op("act", lambda e, pv=pv, b=b: e.activation(out=ST[:, :, b * 128:(b + 1) * 128], in_=pv[:, 0:384].rearrange("p (c n) -> p c n", n=128), func=AF.Copy),
                         reads=[pr2], writes=[r_ST])
                else:
                    P.op("act", lambda e, pv=pv: e.activation(out=latnT[:, :, :], in_=pv[:, 0:384].rearrange("p (c n) -> p c n", n=128)[:, :, 0:64], func=AF.Copy),
                         reads=[pr2], writes=[r_latnT])
            if t < 4:
                for c in range(3):
                    P.dma("sp", lambda e, c=c: e.dma_start(out=sendb[c].ap().bitcast(BF16)[:, t * 512:(t + 1) * 512], in_=ST[:, c, :]), s_ST,
                          reads=[r_ST], writes=[r_sendb])

        def q_proj(t):
            N = 512 if t < 4 else 64
            nb, bp = (4, 128) if t < 4 else (1, 64)
            rmsnorm_fm(C, N, False, GV["nm"][1])
            tw, rw = load_wA(C, w_dq[0].rearrange("(k p) c -> p k c", p=128), 512)
            P.dma("pool", lambda e: e.dma_start(out=wUQ[:, :, :], in_=w_uq[0].rearrange("(k p) c -> p k c", p=128)), s_wUQ, writes=[r_wUQ])
            for b in range(nb):
                pt, pr, _ = PS.next()
                for k in range(8):
                    P.op("pe", lambda e, pt=pt, k=k, b=b: e.matmul(pt[0:bp, 0:512], XN[:, k, b * bp:(b + 1) * bp], tw[:, k, 0:512], start=(k == 0), stop=(k == 7)),
                         reads=[rw, r_XN] if k == 0 else [], writes=[pr] if k == 0 else [], inc=(k == 7))
                P.op("act", lambda e, pt=pt: e.activation(out=CQ[0:bp, :], in_=pt[0:bp, 0:512], func=AF.Copy), reads=[pr], writes=[r_CQ])
                P.op("dve", lambda e: e.tensor_tensor(out=SQT[0:bp, 0:512], in0=CQ[0:bp, :], in1=CQ[0:bp, :], op=ALU.mult), reads=[r_CQ], writes=[r_SQT])
                P.op("dve", lambda e: e.tensor_reduce(out=SS[0:bp, 0:1], in_=SQT[0:bp, 0:512], axis=AX.X, op=ALU.add), reads=[r_SQT], writes=[r_SS])
                rstd_from_ss(SS[0:bp, 0:1], RS[0:bp, 0:1], 512.0, [r_SS], r_RS, bp)
                P.op("dve", lambda e: e.scalar_tensor_tensor(out=CQB[0:bp, :], in0=CQ[0:bp, :], scalar=RS[0:bp, 0:1], in1=qn_bc[0:bp, :], op0=ALU.mult, op1=ALU.mult),
                     reads=[r_CQ, r_RS, r_bc], writes=[r_CQB])
                pt2, pr2, _ = PS.next()
                pv = psb(pt2)
                for c in range(4):
                    P.op("pe", lambda e, pv=pv, c=c: e.transpose(out=pv[:, c * 128:c * 128 + bp], in_=CQB[0:bp, c * 128:(c + 1) * 128], identity=ident_b[0:bp, 0:bp]),
                         reads=[r_CQB, r_ident_b] if c == 0 else [], writes=[pr2] if c == 0 else [], inc=(c == 3))
                P.op("act", lambda e, pv=pv, b=b: e.activation(out=CQT[:, :, b * bp:(b + 1) * bp], in_=pv[:, 0:512].rearrange("p (c n) -> p c n", n=128)[:, :, 0:bp], func=AF.Copy),
                     reads=[pr2, r_CQT], writes=[r_CQT])
            load_cs(t)
            for b in range(nb):
                for g in range(4):
                    pt, pr, _ = PS.next()
                    for k in range(4):
                        P.op("pe", lambda e, pt=pt, k=k, b=b, g=g: e.matmul(pt[0:bp, 0:384], CQT[:, k, b * bp:(b + 1) * bp], wUQ[:, k, g * 384:(g + 1) * 384], start=(k == 0), stop=(k == 3)),
                             reads=[r_wUQ, r_CQT] if k == 0 else [], writes=[pr] if k == 0 else [], inc=(k == 3))
                    P.op("act", lambda e, pt=pt, g=g: e.activation(out=QS[0:bp, g * 384:(g + 1) * 384], in_=pt[0:bp, 0:384], func=AF.Copy), reads=[pr, r_QS], writes=[r_QS])
                q3 = QS[0:bp, :].rearrange("p (h d) -> p h d", d=192)
                s3 = SQT[0:bp, :].rearrange("p (h d) -> p h d", d=192)
                P.op("dve", lambda e: e.tensor_tensor(out=SQT[0:bp, :], in0=QS[0:bp, :], in1=QS[0:bp, :], op=ALU.mult), reads=[r_QS], writes=[r_SQT])
                P.op("dve", lambda e, s3=s3: e.tensor_reduce(out=SS[0:bp, 0:8], in_=s3[:, :, 0:128], axis=AX.X, op=ALU.add), reads=[r_SQT], writes=[r_SS])
                P.op("dve", lambda e, s3=s3: e.tensor_reduce(out=SS[0:bp, 8:16], in_=s3[:, :, 128:192], axis=AX.X, op=ALU.add), reads=[r_SQT, r_SS], writes=[r_SS])
                rstd_from_ss(SS[0:bp, 0:8], RS[0:bp, 0:8], 128.0, [r_SS], r_RS, bp)
                rstd_from_ss(SS[0:bp, 8:16], RS[0:bp, 8:16], 64.0, [r_SS, r_RS], r_RS, bp)
                tn = TMP[0:bp, :].rearrange("p (h d) -> p h d", d=128)
                P.op("dve", lambda e, tn=tn, q3=q3: e.tensor_tensor(out=tn, in0=q3[:, :, 0:128], in1=RS[0:bp, 0:8].unsqueeze(2).to_broadcast([bp, 8, 128]), op=ALU.mult),
                     reads=[r_QS, r_RS], writes=[r_TMP])
                P.op("dve", lambda e, tn=tn: e.tensor_tensor(out=QNB[0:bp, :].rearrange("p (h d) -> p h d", d=128), in0=tn, in1=qnn_bc[0:bp, :].unsqueeze(1).to_broadcast([bp, 8, 128]), op=ALU.mult),
                     reads=[r_TMP, r_bc], writes=[r_QNB])
                tp = TMP[0:bp, 0:512].rearrange("p (h d) -> p h d", d=64)
                P.op("dve", lambda e, tp=tp, q3=q3: e.tensor_tensor(out=tp, in0=q3[:, :, 128:192], in1=RS[0:bp, 8:16].unsqueeze(2).to_broadcast([bp, 8, 64]), op=ALU.mult),
                     reads=[r_QS, r_RS], writes=[r_TMP])
                qpf = QPF[0:bp, :].rearrange("p (h d) -> p h d", d=64)
                P.op("dve", lambda e, tp=tp, qpf=qpf: e.tensor_tensor(out=qpf, in0=tp, in1=qpn_bc[0:bp, :].unsqueeze(1).to_broadcast([bp, 8, 64]), op=ALU.mult),
                     reads=[r_TMP, r_bc], writes=[r_QPF])
                rope_tm(qpf, QPB[0:bp, :].rearrange("p (h d) -> p h d", d=64), 8, CS[0:bp, b, :], bp, [r_QPF, r_CS], [r_QPB], TMP2[0:bp, :].rearrange("p (h d) -> p h d", d=64), r_TMP2)
                pt2, pr2, _ = PS.next()
                pv = psb(pt2)
                for h in range(8):
                    P.op("pe", lambda e, pv=pv, h=h: e.transpose(out=pv[:, h * 128:h * 128 + bp], in_=QNB[0:bp, h * 128:(h + 1) * 128], identity=ident_b[0:bp, 0:bp]),
                         reads=[r_QNB, r_ident_b] if h == 0 else [], writes=[pr2] if h == 0 else [], inc=(h == 7))
                if t < 4:
                    P.op("act", lambda e, pv=pv, b=b: e.activation(out=QTN[:, :, b * 128:(b + 1) * 128], in_=pv[:, :].rearrange("p (h n) -> p h n", n=128), func=AF.Copy),
                         reads=[pr2, r_QTN], writes=[r_QTN])
                else:
                    P.op("act", lambda e, pv=pv: e.activation(out=QNS[:, :, :], in_=pv[:, :].rearrange("p (h n) -> p h n", n=128)[:, :, 0:64], func=AF.Copy),
                         reads=[pr2], writes=[r_QNS])
                pt3, pr3, _ = PS.next()
                pv3 = psb(pt3)
                if t < 4:
                    for c in range(4):
                        P.op("pe", lambda e, pv3=pv3, c=c: e.transpose(out=pv3[:, c * 128:(c + 1) * 128], in_=QPB[:, c * 128:(c + 1) * 128], identity=ident_b[:, :]),
                             reads=[r_QPB, r_ident_b] if c == 0 else [], writes=[pr3] if c == 0 else [], inc=(c == 3))
                    P.op("act", lambda e, pv3=pv3, b=b: e.activation(out=QTP[:, :, b * 128:(b + 1) * 128], in_=pv3[:, 0:512].rearrange("p (c n) -> p c n", n=128), func=AF.Copy),
                         reads=[pr3, r_QTP], writes=[r_QTP])
                else:
                    for h in range(8):
                        P.op("pe", lambda e, pv3=pv3, h=h: e.transpose(out=pv3[0:64, h * 64:(h + 1) * 64], in_=QPB[0:64, h * 64:(h + 1) * 64], identity=ident_b[0:64, 0:64]),
                             reads=[r_QPB, r_ident_b] if h == 0 else [], writes=[pr3] if h == 0 else [], inc=(h == 7))
                    P.op("act", lambda e, pv3=pv3: e.activation(out=qpT_s[:, :, :], in_=pv3[0:64, 0:512].rearrange("p (h n) -> p h n", n=64), func=AF.Copy),
                         reads=[pr3], writes=[r_qpT_s])
            if t < 4:
                P.dma("sp", lambda e: e.dma_start(out=qsc_n.rearrange("h p n -> p h n")[:, :, t * 512:(t + 1) * 512], in_=QTN[:, :, :]), s_QTN, reads=[r_QTN], writes=[r_qsc_n[t]])
                P.dma("sp", lambda e: e.dma_start(out=qsc_p.rearrange("h p n -> p h n")[:, :, t * 512:(t + 1) * 512], in_=QTP[:, :, :]), s_QTP, reads=[r_QTP], writes=[r_qsc_p[t]])

        dbg_y = None
        for t in range(NT):
            N = 512 if t < 4 else 64
            halo = t < 4
            load_x(t)
            if stop == "A_load": break
            rmsnorm_fm(C, N, halo, GV["nf1"][0])
            if stop == "A_norm": break
            ffn(C, N, halo, w_ffn1_gu[0], w_ffn1_down[0])
            if stop == "A_ffn1": break
            conv_mixer(t)
            if stop == "A_conv": break
            rmsnorm_fm(C, N, False, GV["nf2"][0])
            ffn(C, N, False, w_ffn2_gu[0], w_ffn2_down[0])
            shared_kv(t)
            if stop == "A_kv": break
            rmsnorm_fm(C, N, False, GV["nf1"][1])
            ffn(C, N, False, w_ffn1_gu[1], w_ffn1_down[1])
            if stop == "A_x1": break
            P.dma("sp", lambda e, t=t, N=N: e.dma_start(out=xsc[t].rearrange("p (k n) -> p k n", n=512)[:, :, 0:N], in_=X[:, :, 0:N]), s_XS, reads=r_X, writes=[r_xsc[t]])
            q_proj(t)
            if stop == "A_q": break
        if stop.startswith("A_"):
            tokd = P.dma("sp", lambda e: e.dma_start(out=y_p[0].rearrange("(p a) d -> p (a d)", p=128), in_=X[:, :, :].rearrange("p k n -> p (k n)")), s_XS, reads=r_X, writes=[r_out])
            P.wait_all("sp", out_tokens + [tokd])
            P.emit()
            return nc
        s_cc = P.dma_sem("cc")
        for c in range(3):
            P.dma("pool", lambda e, c=c: e.collective_compute("AllGather", ALU.bypass, replica_groups=[[0, 1, 2, 3], [4, 5, 6, 7]],
                                                              ins=[sendb[c].ap().opt()], outs=[recvb[c].ap().opt()]), s_cc, reads=[r_sendb], writes=[r_recvb], inc=1)
        if stop == "A":
            P.wait_all("sp", out_tokens)
        P.emit()
    if stop == "A":
        return nc

    oT = sb("oT", [128, 8, 2112], BF16)
    with ExitStack() as es:
        def sbx(name, shape, dt=F32):
            return es.enter_context(nc.sbuf_tensor(name, list(shape), dt))
        latT = sbx("latT", [128, 2, 8192], BF16); r_latT = Res("latT"); s_latT = P.dma_sem("latT")
        kpeT = sbx("kpeT", [128, 8192], BF16); r_kpeT = Res("kpeT")
        WUK = sbx("WUK", [128, 2, D], BF16); r_WUK = Res("WUK"); s_W = P.dma_sem("WUKa"); s_W2 = P.dma_sem("WUVa")
        WUV = sbx("WUV", [128, 2, D], BF16); r_WUV = Res("WUV")
        MASK = sbx("MASK", [128, 16 * 512], BF16); r_MASK = Res("MASK"); s_MASK = P.dma_sem("MASK")
        KN = Ring(P, nc, "KN", [128, 8192], BF16, 2, alloc=sbx)
        VV = Ring(P, nc, "VV", [128, 64, 130], BF16, 2, alloc=sbx)
        SQK = Ring(P, nc, "SQK", [128, 512], BF16, 2, alloc=sbx)
        RK = Ring(P, nc, "RK", [128, 512], F32, 2, alloc=sbx)
        QN = Ring(P, nc, "QN", [128, 512], BF16, 2, dma=True, alloc=sbx)
        QP = Ring(P, nc, "QP", [128, 512], BF16, 2, dma=True, alloc=sbx)
        PT = Ring(P, nc, "PT", [128, 512], BF16, 3, alloc=sbx)
        RC = Ring(P, nc, "RC", [128, 1], F32, 4, alloc=sbx)
        ON = Ring(P, nc, "ON", [128, 128], BF16, 2, alloc=sbx)
        PSO = PsRing([0, 1, 2, 3])
        PSS = PsRing([4, 5])
        PSX = PsRing([6, 7])

        for r in range(4):
            for c in range(3):
                dst = (latT[:, c, :] if c < 2 else kpeT[:, :]).rearrange("p (s r2 i) -> p s r2 i", s=4, r2=4, i=512)[:, :, r, :]
                src = recvb[c].ap().bitcast(BF16)[r * 128:(r + 1) * 128, :].rearrange("p (s i) -> p s i", i=512)
                P.dma("sp", lambda e, dst=dst, src=src: e.dma_start(out=dst, in_=src), s_latT, reads=[r_recvb], writes=[r_latT if c < 2 else r_kpeT])
        r_latT.w = r_kpeT.w = (s_latT.sem, s_latT.val)
        P.dma("pool", lambda e: e.dma_start(out=WUK[:, :, :], in_=w_uk.rearrange("(c p) m -> p c m", p=128)), s_W, writes=[r_WUK])
        P.dma("pool", lambda e: e.dma_start(out=WUV[:, :, :], in_=w_uv.rearrange("(c p) m -> p c m", p=128)), s_W2, writes=[r_WUV])
        P.dma("sp", lambda e: e.dma_start(out=MASK[:, :], in_=mask_p_d), s_MASK, writes=[r_MASK])
        kcol = GV["kn"]
        for (vt, vr, _) in VV.items:
            P.op("pool", lambda e, vt=vt: e.memset(vt[:, :, 128:130], 1.0), writes=[vr])
        nheads = 8
        for h in range(nheads):
            kn_t, kn_r, _ = KN.next()
            v_t, v_r, _ = VV.next()
            for kt in range(16):
                pk, prk, _ = PSX.next()
                for c in range(2):
                    P.op("pe", lambda e, pk=pk, c=c, h=h, kt=kt: e.matmul(pk[:, :], WUK[:, c, h * 128:(h + 1) * 128], latT[:, c, kt * 512:(kt + 1) * 512], start=(c == 0), stop=(c == 1)),
                         reads=[r_WUK, r_latT] if c == 0 else [], writes=[prk] if c == 0 else [], inc=(c == 1))
                sq, sqr, _ = SQK.next()
                P.op("act", lambda e, sq=sq, pk=pk: e.activation(out=sq[:, :], in_=pk[:, :], func=AF.Square), reads=[prk], writes=[sqr])
                p2, pr2, _ = PSX.next()
                P.op("pe", lambda e, p2=p2, sq=sq: e.matmul(p2[:, :], ones_b[:], sq[:, :], start=True, stop=True), reads=[sqr, r_ones], writes=[pr2])
                rk, rkr, _ = RK.next()
                rstd_from_ss(p2[:, :], rk[:, :], 128.0, [pr2], rkr)
                P.op("dve", lambda e, pk=pk, rk=rk, kn_t=kn_t, kt=kt: e.scalar_tensor_tensor(out=kn_t[:, kt * 512:(kt + 1) * 512], in0=pk[:, :], scalar=gv[:, kcol:kcol + 1], in1=rk[:, :],
                                                                                           op0=ALU.mult, op1=ALU.mult), reads=[prk, rkr, r_gv], writes=[kn_r])
            for kb4 in range(16):
                pvv, prv, _ = PSX.next()
                for q4 in range(4):
                    kb = kb4 * 4 + q4
                    for c in range(2):
                        P.op("pe", lambda e, pvv=pvv, q4=q4, kb=kb, c=c, h=h: e.matmul(pvv[:, q4 * 128:(q4 + 1) * 128], latT[:, c, kb * 128:(kb + 1) * 128], WUV[:, c, h * 128:(h + 1) * 128],
                                                                                    start=(c == 0), stop=(c == 1)),
                             reads=[r_WUV, r_latT] if (q4 == 0 and c == 0) else [], writes=[prv] if (q4 == 0 and c == 0) else [], inc=(q4 == 3 and c == 1))
                P.op("act", lambda e, pvv=pvv, v_t=v_t, kb4=kb4: e.activation(out=v_t[:, kb4 * 4:(kb4 + 1) * 4, 0:128], in_=pvv[:, :].rearrange("p (q d) -> p q d", d=128), func=AF.Copy),
                     reads=[prv, v_r], writes=[v_r])
            base = 64 * (h % 2)
            for s in range(4):
                qn_t, qn_r, qn_s = QN.next()
                qp_t, qp_r, qp_s = QP.next()
                P.dma("sp", lambda e, qn_t=qn_t, h=h, s=s: e.dma_start(out=qn_t[:, :], in_=qsc_n[h, :, s * 512:(s + 1) * 512]), qn_s, reads=[r_qsc_n[s]], writes=[qn_r])
                P.dma("sp", lambda e, qp_t=qp_t, h=h, s=s: e.dma_start(out=qp_t[:, :], in_=qsc_p[h // 2, :, s * 512:(s + 1) * 512]), qp_s, reads=[r_qsc_p[s]], writes=[qp_r])
                accs = [PSO.next() for _ in range(4)]
                nkb = 16 * (s + 1)
                def front(j):
                    sc, scr, _ = PSS.next()
                    P.op("pe", lambda e, sc=sc, kn_t=kn_t, qn_t=qn_t, j=j: e.matmul(sc[:, :], kn_t[:, j * 128:(j + 1) * 128], qn_t[:, :], start=True, stop=False),
                         reads=[kn_r, qn_r], writes=[scr], inc=False)
                    P.op("pe", lambda e, sc=sc, qp_t=qp_t, j=j, base=base: e.matmul(sc[:, :], kpeT[base:base + 64, j * 128:(j + 1) * 128], qp_t[base:base + 64, :], start=False, stop=True),
                         reads=[r_kpeT, qp_r], touch=[scr])
                    pt_, ptr, _ = PT.next()
                    P.op("act", lambda e, pt_=pt_, sc=sc: e.activation(out=pt_[:, :], in_=sc[:, :], func=AF.Exp, scale=SCALE), reads=[scr], writes=[ptr])
                    if j >= 16 * s:
                        jj = j - 16 * s
                        P.op("pool", lambda e, pt_=pt_, jj=jj: e.tensor_tensor(out=pt_[:, :], in0=pt_[:, :], in1=MASK[:, jj * 512:(jj + 1) * 512], op=ALU.mult),
                             reads=[ptr, r_MASK], writes=[ptr])
                    return pt_, ptr

                def back(j, pt_, ptr):
                    for i in range(4):
                        at, ar, _ = accs[i]
                        P.op("pe", lambda e, at=at, pt_=pt_, v_t=v_t, i=i, j=j, nkb=nkb: e.matmul(at[:, 0:129], pt_[:, i * 128:(i + 1) * 128], v_t[:, j, 0:129], start=(j == 0), stop=(j == nkb - 1)),
                             reads=[ptr, v_r] if i == 0 else [], writes=[ar] if j == 0 else [], touch=[] if j == 0 else [ar], inc=(i == 3))

                pend = front(0)
                for j in range(nkb):
                    nxt = front(j + 1) if j + 1 < nkb else None
                    back(j, *pend)
                    pend = nxt
                for i in range(4):
                    at, ar, _ = accs[i]
                    rc, rcr, _ = RC.next()
                    P.op("dve", lambda e, rc=rc, at=at: e.reciprocal(out=rc[:, :], in_=at[:, 128:129]), reads=[ar], writes=[rcr])
                    on, onr, _ = ON.next()
                    P.op("dve", lambda e, on=on, at=at, rc=rc: e.tensor_scalar(out=on[:, :], in0=at[:, 0:128], scalar1=rc[:, 0:1], scalar2=None, op0=ALU.mult),
                         reads=[ar, rcr], writes=[onr])
                    px, pxr, _ = PSX.next()
                    pxv = psb(px)
                    P.op("pe", lambda e, pxv=pxv, on=on: e.transpose(out=pxv[:, 0:128], in_=on[:, :], identity=ident_b[:, :]), reads=[onr, r_ident_b], writes=[pxr])
                    P.op("act", lambda e, pxv=pxv, h=h, s=s, i=i: e.activation(out=oT[:, h, s * 512 + i * 128:s * 512 + (i + 1) * 128], in_=pxv[:, 0:128], func=AF.Copy),
                         reads=[pxr, r_oT[s]], writes=[r_oT[s]])
        if stop == "AT":
            P.wait_all("sp", out_tokens)
        P.emit()
    if stop == "AT":
        return nc

    with ExitStack() as es:
        def sbx(name, shape, dt=F32):
            return es.enter_context(nc.sbuf_tensor(name, list(shape), dt))
        WUK = sbx("WUKs", [128, 2, D], BF16); r_WUK = Res("WUK"); s_W = P.dma_sem("WUKs"); s_W2 = P.dma_sem("WUVs")
        WUV = sbx("WUVs", [128, 2, D], BF16); r_WUV = Res("WUV")
        WUKT = sbx("WUKT", [128, 8, 256], BF16); r_WUKT = Res("WUKT")
        QABS = sbx("QABS", [128, 2, 8, 64], BF16); r_QABS = Res("QABS")
        MN = sbx("MN", [64, 512], BF16); r_MN = Res("MN"); s_MN = P.dma_sem("MN")
        LPK = Ring(P, nc, "LPK", [128, 1280], BF16, 3, dma=True, alloc=sbx)
        IDX = Ring(P, nc, "IDX", [128, 1], I32, 3, alloc=sbx)
        IDX1 = Ring(P, nc, "IDX1", [128, 1], I32, 3, alloc=sbx)
        I4 = Ring(P, nc, "I4", [128, 4], F32, 2, alloc=sbx)
        I1 = Ring(P, nc, "I1", [128, 1], F32, 2, alloc=sbx)
        sel_t = sbx("sel_t", [128, 4]); iota32_t = sbx("iota32_t", [128, 1]); r_sel = Res("sel"); s_sel = P.dma_sem("sel")
        P.dma("sp", lambda e: e.dma_start(out=sel_t[:, :], in_=sel_d), s_sel, writes=[r_sel])
        P.dma("sp", lambda e: e.dma_start(out=iota32_t[:, :], in_=iota32_d), s_sel, writes=[r_sel])
        r_sel.w = (s_sel.sem, s_sel.val)
        LTP = Ring(P, nc, "LTP", [128, 384], BF16, 2, alloc=sbx)
        SQP = Ring(P, nc, "SQP", [128, 1024], F32, 2, alloc=sbx)
        SSP = Ring(P, nc, "SSP", [128, 8], F32, 2, alloc=sbx)
        RSP = Ring(P, nc, "RSP", [128, 8], F32, 2, alloc=sbx)
        T1 = Ring(P, nc, "T1", [128, 32], F32, 2, alloc=sbx)
        PTS = Ring(P, nc, "PTS", [128, 32], BF16, 2, alloc=sbx)
        SNW = sbx("SNW", [64, 512]); r_SNW = Res("SNW")
        PNW = sbx("PNW", [64, 512], BF16); r_PNW = Res("PNW")
        PN2 = sbx("PN2", [64, 16, 32], BF16); r_PN2 = Res("PN2")
        RCs = Ring(P, nc, "RCs", [32, 1], F32, 2, alloc=sbx)
        OLN = Ring(P, nc, "OLN", [32, 256], BF16, 2, alloc=sbx)
        OLT = sbx("OLT", [128, 2, 8, 64], BF16); r_OLT = Res("OLT")
        PSK = PsRing([0, 1, 2, 3])
        PSX = PsRing([4, 5])
        PSO = PsRing([6, 7])
        kcol = GV["kn"]

        P.dma("pool", lambda e: e.dma_start(out=WUK[:, :, :], in_=w_uk.rearrange("(c p) m -> p c m", p=128)), s_W, writes=[r_WUK])
        P.dma("pool", lambda e: e.dma_start(out=WUV[:, :, :], in_=w_uv.rearrange("(c p) m -> p c m", p=128)), s_W2, writes=[r_WUV])
        P.dma("sp", lambda e: e.dma_start(out=MN[:, :], in_=mask_n_d), s_MN, writes=[r_MN])
        for h in range(8):
            px, pxr, _ = PSX.next()
            pxv = psb(px)
            for c in range(2):
                P.op("pe", lambda e, pxv=pxv, c=c, h=h: e.transpose(out=pxv[:, c * 128:(c + 1) * 128], in_=WUK[:, c, h * 128:(h + 1) * 128], identity=ident_b[:, :]),
                     reads=[r_WUK, r_ident_b] if c == 0 else [], writes=[pxr] if c == 0 else [], inc=(c == 1))
            P.op("dve", lambda e, pxv=pxv, h=h: e.tensor_scalar(out=WUKT[:, h, :], in0=pxv[:, 0:256], scalar1=gv[:, kcol:kcol + 1], scalar2=None, op0=ALU.mult),
                 reads=[pxr, r_gv, r_WUKT], writes=[r_WUKT])
        for h in range(8):
            for c in range(2):
                px, pxr, _ = PSX.next()
                P.op("pe", lambda e, px=px, c=c, h=h: e.matmul(px[:, 0:64], WUKT[:, h, c * 128:(c + 1) * 128], QNS[:, h, :], start=True, stop=True),
                     reads=[r_WUKT, r_QNS], writes=[pxr])
                P.op("act", lambda e, px=px, c=c, h=h: e.activation(out=QABS[:, c, h, :], in_=px[:, 0:64], func=AF.Copy), reads=[pxr, r_QABS], writes=[r_QABS])

        def key_block(lat_bf, kpe_bf, nk, qabs_rhs, qp_rhs, ncol):
            px, pxr, _ = PSX.next()
            pxv = psb(px)
            return px, pxr, pxv

        _breg = {}

        def breg(e):
            if "r" not in _breg:
                _breg["r"] = e.to_reg(HALFQ - 1)
            return _breg["r"]

        nseq_run = 16
        for b in range(nseq_run):
            ol, olr, _ = PSO.next()
            for g in range(NPAGE // 4):
                col = b * NPAGE + 4 * g
                i4, i4r, _ = I4.next()
                P.op("dve", lambda e, i4=i4, col=col: e.tensor_tensor(out=i4[:, :], in0=ptab_t[:, col:col + 4], in1=sel_t[:, :], op=ALU.mult), reads=[r_ptab, r_sel], writes=[i4r])
                i1, i1r, _ = I1.next()
                P.op("dve", lambda e, i4=i4, i1=i1: e.tensor_reduce(out=i1[:, :], in_=i4[:, :], axis=AX.X, op=ALU.add), reads=[i4r], writes=[i1r])
                ix, ixr, _ = IDX.next()
                P.op("dve", lambda e, ix=ix, i1=i1: e.scalar_tensor_tensor(out=ix[:, :], in0=i1[:, :], scalar=32.0, in1=iota32_t[:, :], op0=ALU.mult, op1=ALU.add),
                     reads=[i1r, r_sel], writes=[ixr])
                ix1, ixr1, _ = IDX1.next()
                P.op("dve", lambda e, ix=ix, ix1=ix1: e.tensor_scalar(out=ix1[:, :], in0=ix[:, :], scalar1=-HALFQ, scalar2=None, op0=ALU.add), reads=[ixr], writes=[ixr1])
                lpk, lpr, lps = LPK.next()
                for hf, (ixx, ixxr) in enumerate(((ix, ixr), (ix1, ixr1))):
                    P.dma("pool", lambda e, lpk=lpk, ixx=ixx, hf=hf: e.indirect_dma_start(out=lpk[:, :], out_offset=None, in_=cache_q[hf],
                                                                                         in_offset=bass.IndirectOffsetOnAxis(ap=ixx[:, :], axis=0),
                                                                                         bounds_check=breg(e), oob_is_err=False),
                          lps, reads=[ixxr], writes=[lpr] if hf == 0 else [], touch=[lpr] if hf == 1 else [])
                for j in range(4):
                    pg = 4 * g + j
                    px, pxr, _ = PSX.next()
                    pxv = psb(px)
                    for c in range(2):
                        P.op("pe", lambda e, pxv=pxv, lpk=lpk, c=c, j=j: e.transpose(out=pxv[:, c * 128:(c + 1) * 128], in_=lpk[:, j * 320 + c * 128:j * 320 + (c + 1) * 128], identity=ident_b[:, :]),
                             reads=[lpr, r_ident_b] if c == 0 else [], writes=[pxr] if c == 0 else [], inc=False)
                    P.op("pe", lambda e, pxv=pxv, lpk=lpk, j=j: e.transpose(out=pxv[0:64, 256:384], in_=lpk[:, j * 320 + 256:j * 320 + 320], identity=ident_b[:, :]), reads=[lpr], touch=[pxr])
                    lt, ltr, _ = LTP.next()
                    P.op("act", lambda e, lt=lt, pxv=pxv: e.activation(out=lt[:, 0:256], in_=pxv[:, 0:256], func=AF.Copy), reads=[pxr], writes=[ltr])
                    P.op("act", lambda e, lt=lt, pxv=pxv: e.activation(out=lt[0:64, 256:384], in_=pxv[0:64, 256:384], func=AF.Copy), reads=[pxr, ltr], writes=[ltr])
                    pk0, pkr0, _ = PSK.next()
                    pk1, pkr1, _ = PSK.next()
                    for hf, (pk, pkr) in enumerate(((pk0, pkr0), (pk1, pkr1))):
                        for c in range(2):
                            P.op("pe", lambda e, pk=pk, lt=lt, c=c, hf=hf: e.matmul(pk[:, :], lt[:, c * 128:(c + 1) * 128], WUK[:, c, hf * 512:(hf + 1) * 512], start=(c == 0), stop=(c == 1)),
                                 reads=[ltr, r_WUK] if c == 0 else [], writes=[pkr] if c == 0 else [], inc=(c == 1))
                    sq, sqr, _ = SQP.next()
                    P.op("act", lambda e, sq=sq, pk0=pk0: e.activation(out=sq[:, 0:512], in_=pk0[:, :], func=AF.Square), reads=[pkr0], writes=[sqr])
                    P.op("act", lambda e, sq=sq, pk1=pk1: e.activation(out=sq[:, 512:1024], in_=pk1[:, :], func=AF.Square), reads=[pkr1, sqr], writes=[sqr])
                    ss, ssr, _ = SSP.next()
                    P.op("dve", lambda e, ss=ss, sq=sq: e.tensor_reduce(out=ss[:, :], in_=sq[:, :].rearrange("p (h d) -> p h d", d=128), axis=AX.X, op=ALU.add), reads=[sqr], writes=[ssr])
                    rs, rsr, _ = RSP.next()
                    rstd_from_ss(ss[:, :], rs[:, :], 128.0, [ssr], rsr)
                    ps_, psr, _ = PSX.next()
                    for c in range(2):
                        P.op("pe", lambda e, ps_=ps_, lt=lt, c=c, b=b: e.matmul(ps_[:, 0:32], lt[:, c * 128:(c + 1) * 128], QABS[:, c, :, b * 4:(b + 1) * 4], start=(c == 0), stop=(c == 1)),
                             reads=[ltr, r_QABS] if c == 0 else [], writes=[psr] if c == 0 else [], inc=False)
                    P.op("pe", lambda e, ps_=ps_, lt=lt, b=b: e.matmul(ps_[:, 32:64], lt[0:64, 256:384], qpT_s[:, :, b * 4:(b + 1) * 4], start=True, stop=True),
                         reads=[r_qpT_s], touch=[psr])
                    t1, t1r, _ = T1.next()
                    P.op("dve", lambda e, t1=t1, ps_=ps_, rs=rs: e.tensor_tensor(out=t1[:, :].rearrange("p (h t) -> p h t", t=4), in0=ps_[:, 0:32].rearrange("p (h t) -> p h t", t=4),
                                                                                in1=rs[:, :].unsqueeze(2).to_broadcast([128, 8, 4]), op=ALU.mult), reads=[psr, rsr], writes=[t1r])
                    P.op("dve", lambda e, t1=t1, ps_=ps_: e.tensor_tensor(out=t1[:, :], in0=t1[:, :], in1=ps_[:, 32:64], op=ALU.add), reads=[psr, t1r], writes=[t1r])
                    pts, ptsr, _ = PTS.next()
                    P.op("act", lambda e, pts=pts, t1=t1: e.activation(out=pts[:, :], in_=t1[:, :], func=AF.Exp, scale=SCALE), reads=[t1r], writes=[ptsr])
                    P.op("pe", lambda e, ol=ol, pts=pts, lpk=lpk, j=j, pg=pg: e.matmul(ol[0:32, 0:256], pts[:, :], lpk[:, j * 320:j * 320 + 256], start=(pg == 0), stop=False),
                         reads=[ptsr, lpr], writes=[olr] if pg == 0 else [], touch=[] if pg == 0 else [olr], inc=False)
                    P.op("pe", lambda e, ol=ol, pts=pts: e.matmul(ol[0:32, 256:257], pts[:, :], ones_b[:, 0:1], start=False, stop=False, skip_group_check=True),
                         reads=[r_ones], touch=[olr])
            if b == 0:
                pk0, pkr0, _ = PSK.next()
                pk1, pkr1, _ = PSK.next()
                for hf, (pk, pkr) in enumerate(((pk0, pkr0), (pk1, pkr1))):
                    for c in range(2):
                        P.op("pe", lambda e, pk=pk, c=c, hf=hf: e.matmul(pk[0:64, :], latnT[:, c, :], WUK[:, c, hf * 512:(hf + 1) * 512], start=(c == 0), stop=(c == 1)),
                             reads=[r_latnT, r_WUK] if c == 0 else [], writes=[pkr] if c == 0 else [], inc=(c == 1))
                sq, sqr, _ = SQP.next()
                P.op("act", lambda e, sq=sq, pk0=pk0: e.activation(out=sq[0:64, 0:512], in_=pk0[0:64, :], func=AF.Square), reads=[pkr0], writes=[sqr])
                P.op("act", lambda e, sq=sq, pk1=pk1: e.activation(out=sq[0:64, 512:1024], in_=pk1[0:64, :], func=AF.Square), reads=[pkr1, sqr], writes=[sqr])
                ss, ssr, _ = SSP.next()
                P.op("dve", lambda e, ss=ss, sq=sq: e.tensor_reduce(out=ss[0:64, :], in_=sq[0:64, :].rearrange("p (h d) -> p h d", d=128), axis=AX.X, op=ALU.add), reads=[sqr], writes=[ssr])
                rsn, rsnr, _ = RSP.next()
                rstd_from_ss(ss[0:64, :], rsn[0:64, :], 128.0, [ssr], rsnr, 64)
                pa, par, _ = PSK.next()
                pb_, pbr, _ = PSK.next()
                for c in range(2):
                    P.op("pe", lambda e, pa=pa, c=c: e.matmul(pa[0:64, :], latnT[:, c, :], QABS[:, c, :, :], start=(c == 0), stop=(c == 1)),
                         reads=[r_latnT, r_QABS] if c == 0 else [], writes=[par] if c == 0 else [], inc=(c == 1))
                P.op("pe", lambda e, pb_=pb_: e.matmul(pb_[0:64, :], latnT[0:64, 2, :], qpT_s[:, :, :], start=True, stop=True), reads=[r_latnT, r_qpT_s], writes=[pbr])
                P.op("dve", lambda e, pa=pa, rsn=rsn: e.tensor_tensor(out=SNW[:, :].rearrange("p (h n) -> p h n", n=64), in0=pa[0:64, :].rearrange("p (h n) -> p h n", n=64),
                                                                     in1=rsn[0:64, :].unsqueeze(2).to_broadcast([64, 8, 64]), op=ALU.mult), reads=[par, rsnr], writes=[r_SNW])
                P.op("dve", lambda e, pb_=pb_: e.tensor_tensor(out=SNW[:, :], in0=SNW[:, :], in1=pb_[0:64, :], op=ALU.add), reads=[pbr, r_SNW], writes=[r_SNW])
                P.op("act", lambda e: e.activation(out=PNW[:, :], in_=SNW[:, :], func=AF.Exp, scale=SCALE), reads=[r_SNW], writes=[r_PNW])
                P.op("dve", lambda e: e.tensor_tensor(out=PNW[:, :], in0=PNW[:, :], in1=MN[:, :], op=ALU.mult), reads=[r_PNW, r_MN], writes=[r_PNW])
                P.op("dve", lambda e: e.tensor_copy(out=PN2[:, :, :].rearrange("p b (h t) -> p b h t", t=4), in_=PNW[:, :].rearrange("p (h b t) -> p b h t", h=8, b=16, t=4)),
                     reads=[r_PNW], writes=[r_PN2])
            P.op("pe", lambda e, ol=ol, b=b: e.matmul(ol[0:32, 0:257], PN2[:, b, :], latn_aug[:, 0:257], start=False, stop=True), reads=[r_PN2, r_latn], touch=[olr])
            rc, rcr, _ = RCs.next()
            P.op("dve", lambda e, rc=rc, ol=ol: e.reciprocal(out=rc[:, :], in_=ol[0:32, 256:257]), reads=[olr], writes=[rcr])
            on, onr, _ = OLN.next()
            P.op("dve", lambda e, on=on, ol=ol, rc=rc: e.tensor_scalar(out=on[:, :], in0=ol[0:32, 0:256], scalar1=rc[:, 0:1], scalar2=None, op0=ALU.mult), reads=[olr, rcr], writes=[onr])
            px, pxr, _ = PSX.next()
            pxv = psb(px)
            for c in range(2):
                P.op("pe", lambda e, pxv=pxv, on=on, c=c: e.transpose(out=pxv[:, c * 32:(c + 1) * 32], in_=on[:, c * 128:(c + 1) * 128], identity=ident_b[0:32, 0:32]),
                     reads=[onr, r_ident_b] if c == 0 else [], writes=[pxr] if c == 0 else [], inc=(c == 1))
            P.op("act", lambda e, pxv=pxv, b=b: e.activation(out=OLT[:, :, :, b * 4:(b + 1) * 4], in_=pxv[:, 0:64].rearrange("p (c h t) -> p c h t", c=2, h=8, t=4), func=AF.Copy),
                 reads=[pxr, r_OLT], writes=[r_OLT])
        for h in range(8):
            px, pxr, _ = PSX.next()
            for c in range(2):
                P.op("pe", lambda e, px=px, c=c, h=h: e.matmul(px[:, 0:64], WUV[:, c, h * 128:(h + 1) * 128], OLT[:, c, h, :], start=(c == 0), stop=(c == 1)),
                     reads=[r_WUV, r_OLT] if c == 0 else [], writes=[pxr] if c == 0 else [], inc=(c == 1))
            P.op("act", lambda e, px=px, h=h: e.activation(out=oT[:, h, 2048:2112], in_=px[:, 0:64], func=AF.Copy), reads=[pxr, r_oT[4]], writes=[r_oT[4]])
        if stop == "S":
            P.wait_all("sp", out_tokens)
        P.emit()
    if stop == "S":
        return nc

    with ExitStack() as es:
        C = make_common(es)
        sbx = C["sbx"]
        X, XN, Hh = C["X"], C["XN"], C["Hh"]
        r_X, r_XN, r_H = C["r_X"], C["r_XN"], C["r_H"]
        YT = Ring(P, nc, "YT", [128, D], F32, 2, dma=True, alloc=sbx)
        s_XL = P.dma_sem("XL")
        for t in range(NT):
            N = 512 if t < 4 else 64
            nb, bp = (4, 128) if t < 4 else (1, 64)
            col0 = t * 512
            P.dma("sp", lambda e, t=t, N=N: e.dma_start(out=X[:, :, 0:N], in_=xsc[t].rearrange("p (k n) -> p k n", n=512)[:, :, 0:N]), s_XL, reads=[r_xsc[t]], writes=r_X)
            Wo = w_o[0]
            for mb in range(0, 8, 4):
                tw, rw = load_wA(C, wview(Wo, mb * 128, 512))
                for mm in range(4):
                    m = mb + mm
                    pt, pr, _ = PS.next()
                    for k in range(8):
                        P.op("pe", lambda e, pt=pt, tw=tw, k=k, mm=mm, N=N, col0=col0: e.matmul(pt[:, 0:N], tw[:, k, mm * 128:(mm + 1) * 128], oT[:, k, col0:col0 + N], start=(k == 0), stop=(k == 7)),
                             reads=[rw, r_oT[t]] if k == 0 else [], writes=[pr] if k == 0 else [], inc=(k == 7))
                    P.op("dve", lambda e, pt=pt, m=m, N=N: e.tensor_tensor(out=X[:, m, 0:N], in0=pt[:, 0:N], in1=X[:, m, 0:N], op=ALU.add), reads=[pr, r_X[m]], writes=[r_X[m]])
            rmsnorm_fm(C, N, False, GV["nf2"][1])
            ffn(C, N, False, w_ffn2_gu[1], w_ffn2_down[1])
            for b in range(nb):
                yt, yr, yts = YT.next()
                for half in range(2):
                    pt, pr, _ = PS.next()
                    for kk in range(4):
                        k = half * 4 + kk
                        P.op("pe", lambda e, pt=pt, k=k, kk=kk, b=b, bp=bp: e.transpose(out=pt[0:bp, kk * 128:(kk + 1) * 128], in_=X[:, k, b * bp:(b + 1) * bp], identity=ident_f[:, :]),
                             reads=[r_X[k], r_ident_f], writes=[pr] if kk == 0 else [], touch=[] if kk == 0 else [pr], inc=(kk == 3))
                    P.op("act", lambda e, pt=pt, yt=yt, half=half, bp=bp: e.activation(out=yt[0:bp, half * 512:(half + 1) * 512], in_=pt[0:bp, :], func=AF.Copy), reads=[pr, yr], writes=[yr])
                if t < 4:
                    out_tokens.append(P.dma("sp", lambda e, yt=yt, t=t, b=b: e.dma_start(out=y_p[t, b * 128:(b + 1) * 128, :], in_=yt[:, :]), yts, reads=[yr], writes=[r_out]))
                else:
                    out_tokens.append(P.dma("sp", lambda e, yt=yt: e.dma_start(out=y_s, in_=yt[0:64, :]), yts, reads=[yr], writes=[r_out]))
        P.wait_all("sp", out_tokens)
        P.emit()
    _NC_CACHE["cnt"] = dict(P.cnt)
    return nc


DEBUG_STOP = ""


def _get_nc(n_phys=10240, stop=""):
    key = ("nc", n_phys, stop)
    if key not in _NC_CACHE:
        _NC_CACHE[key] = build_program(n_phys, stop)
    return _NC_CACHE[key]


def _host_layout(inp):
    f32 = np.float32
    g = {k: np.asarray(v) for k, v in inp.items()}
    x_prompt, x_sample = g["x_prompt"], g["x_sample"]
    cache_lat = np.ascontiguousarray(g["cache_latent"]).reshape(-1, 256)
    cache_kpe = np.ascontiguousarray(g["cache_kpe"]).reshape(-1, 64)
    rows = [g["norm_ffn1"][0], g["norm_ffn1"][1], g["norm_mix"][0], g["norm_mix"][1], g["norm_ffn2"][0], g["norm_ffn2"][1],
            g["norm_kv_in"], g["conv_w"][0, 0], g["conv_w"][0, 1], g["conv_w"][0, 2]]
    gvec = np.zeros((88, 128), f32)
    gvec[0:80] = np.concatenate([r.reshape(8, 128) for r in rows], 0)
    gvec[80] = g["k_nope_norm"]
    freqs = (np.float32(10000.0) ** (-np.arange(32, dtype=f32) / np.float32(32))).astype(f32)

    def cs_table(pos):
        ang = (pos.astype(f32)[:, None] * freqs[None, :]).astype(f32)
        return np.concatenate([np.cos(ang), np.sin(ang)], 1).astype(f32)

    ident = np.eye(128, dtype=f32)
    iota = np.arange(128, dtype=np.int32)[:, None]
    kk = np.arange(128)[:, None, None]
    jj = np.arange(16)[None, :, None]
    qq = np.arange(512)[None, None, :]
    bp_, tp_ = np.divmod(np.arange(64), 4)
    mn = np.zeros((64, 8, 16, 4), f32)
    for b in range(16):
        for t in range(4):
            mn[:, :, b, t] = ((bp_ == b) & (tp_ <= t))[:, None]
    mask_n = mn.reshape(64, 512).astype(ml_dtypes.bfloat16)
    cs_s = cs_table(8192 + (np.arange(64) % 4))
    hr = cache_lat.shape[0] // 2
    cache_cat = np.concatenate([cache_lat, cache_kpe], axis=1)
    sel = (np.arange(128)[:, None] // 32 == np.arange(4)[None, :]).astype(f32)
    iota32 = (np.arange(128) % 32).astype(f32)[:, None]
    shared = dict(cache_q0=cache_cat[:hr].reshape(hr // 4, 1280), cache_q1=cache_cat[hr:].reshape(hr // 4, 1280), sel=sel, iota32=iota32, gvec=gvec, ident=ident, iota=iota, mask_n=mask_n, cs_s=cs_s,
                  q_norm=g["q_norm"], q_nope_norm=g["q_nope_norm"], q_pe_norm=g["q_pe_norm"], kv_norm=g["kv_norm"], k_pe_norm=g["k_pe_norm"])
    for k in ("w_ffn1_gu", "w_ffn1_down", "w_ffn2_gu", "w_ffn2_down", "w_conv_in", "w_conv_out", "w_dq", "w_uq", "w_o", "w_dkv", "w_uk", "w_uv"):
        shared[k] = g[k]
    in_maps = []
    for c in range(8):
        q, cp = divmod(c, 4)
        xp = np.zeros((4, 514, D), f32)
        cs_p = np.zeros((4, 512, 64), f32)
        for s in range(4):
            st = (4 * s + cp) * 512
            xp[s, 2:] = x_prompt[q, st:st + 512]
            if st > 0:
                xp[s, 0:2] = x_prompt[q, st - 2:st]
            cs_p[s] = cs_table(st + np.arange(512))
        mask_p = (128 * jj + kk <= cp * 512 + qq).astype(f32).reshape(128, 16 * 512).astype(ml_dtypes.bfloat16)
        m = dict(shared)
        m.update(xp=xp, xs=np.ascontiguousarray(x_sample[16 * c:16 * c + 16]).reshape(64, D),
                 sconv=np.ascontiguousarray(g["state_conv"][0, 16 * c:16 * c + 16]).reshape(32, D),
                 ptab=np.ascontiguousarray(g["page_table"][16 * c:16 * c + 16]).reshape(1, 16 * NPAGE).astype(np.int32),
                 cs_p=cs_p, mask_p=mask_p)
        in_maps.append(m)
    return in_maps


def kernel(_stop="", _trace=False, **inputs):
    nc = _get_nc(int(np.asarray(inputs["cache_latent"]).shape[0]), _stop)
    in_maps = _host_layout(inputs)
    res = run_bass_kernel_spmd(nc, in_maps, core_ids=list(range(8)), **({"trace": True} if _trace else {}))
    if _trace:
        print("exec_time_ns", res.exec_time_ns)
    R = res.results
    f32 = np.float32
    y_prompt = np.zeros((2, 8192, D), f32)
    y_sample = np.zeros((128, 4, D), f32)
    conv_p = np.zeros((1, 2, 2, D), f32)
    conv_s = np.zeros((1, 128, 2, D), f32)
    lat_p = np.zeros((2, 8192, 256), f32)
    kpe_p = np.zeros((2, 8192, 64), f32)
    lat_s = np.zeros((128, 4, 256), f32)
    kpe_s = np.zeros((128, 4, 64), f32)
    for c in range(8):
        q, cp = divmod(c, 4)
        r = R[c]
        for s in range(4):
            st = (4 * s + cp) * 512
            y_prompt[q, st:st + 512] = r["y_p"][s]
            lat_p[q, st:st + 512] = r["lat_p"][s]
            kpe_p[q, st:st + 512] = r["kpe_p"][s]
        if cp == 3:
            conv_p[0, q] = r["cso_p"][3]
        y_sample[16 * c:16 * c + 16] = r["y_s"].reshape(16, 4, D)
        conv_s[0, 16 * c:16 * c + 16] = r["cso_s"].reshape(16, 2, D)
        lat_s[16 * c:16 * c + 16] = r["lat_s"].reshape(16, 4, 256)
        kpe_s[16 * c:16 * c + 16] = r["kpe_s"].reshape(16, 4, 64)
    return (y_prompt, y_sample, conv_p, conv_s, lat_p, kpe_p, lat_s, kpe_s)
```

```python
import numpy as np
from contextlib import ExitStack
import ml_dtypes
import concourse.bass as bass
import concourse.mybir as mybir
from concourse.bass_utils import run_bass_kernel_spmd

F32 = mybir.dt.float32
BF16 = mybir.dt.bfloat16
I32 = mybir.dt.int32
AF = mybir.ActivationFunctionType
ALU = mybir.AluOpType
AX = mybir.AxisListType

D = 1024
DFF = 2816
NJ = DFF // 128
EPS = 1e-6
SCALE = 192.0 ** -0.5
NPAGE = 64
NT = 5


class Res:
    __slots__ = ("name", "w", "r")

    def __init__(self, name=""):
        self.name = name
        self.w = None
        self.r = []


class DmaSem:
    def __init__(self, sem):
        self.sem = sem
        self.val = 0


class Prog:
    ENGS = ("pe", "act", "dve", "pool", "sp")

    def __init__(self, nc):
        self.nc = nc
        self.streams = {e: [] for e in self.ENGS}
        self.sem = {}
        self.cnt = {e: 0 for e in self.ENGS}
        self.waited = {e: {} for e in self.ENGS}
        for e in self.ENGS:
            self.sem[e] = nc.alloc_semaphore(name=f"s_{e}")

    def dma_sem(self, name=None):
        self._n = getattr(self, "_n", 0) + 1
        return DmaSem(self.nc.alloc_semaphore(name=f"{name}_{self._n}"))

    def _need(self, eng, tokens):
        wd = self.waited[eng]
        best = {}
        for t in tokens:
            if t is None:
                continue
            sem, val = t
            k = id(sem)
            if wd.get(k, 0) >= val:
                continue
            if k not in best or best[k][1] < val:
                best[k] = (sem, val)
        out = []
        for k, (sem, val) in best.items():
            wd[k] = val
            out.append((sem, val))
        return out

    @staticmethod
    def _deps(reads, writes):
        toks = []
        for r in reads:
            toks.append(r.w)
        for w in writes:
            toks.append(w.w)
            toks.extend(w.r)
        return toks

    def op(self, eng, fn, reads=(), writes=(), inc=True, touch=()):
        waits = self._need(eng, self._deps(reads, writes))
        tok = (self.sem[eng], self.cnt[eng] + 1)
        if inc:
            self.cnt[eng] += 1
        self.streams[eng].append((waits, fn, (self.sem[eng], 1) if inc else None))
        for r in reads:
            r.r.append(tok)
        for w in writes:
            w.w = tok
            w.r = []
        for w in touch:
            w.w = tok
        return tok

    def dma(self, q, fn, dsem, reads=(), writes=(), inc=16, touch=()):
        waits = self._need(q, self._deps(reads, writes))
        dsem.val += inc
        tok = (dsem.sem, dsem.val)
        self.streams[q].append((waits, fn, (dsem.sem, inc)))
        for r in reads:
            r.r.append(tok)
        for w in writes:
            w.w = tok
            w.r = []
        for w in touch:
            w.w = tok
        return tok

    def wait_all(self, eng, tokens):
        waits = self._need(eng, tokens)
        if waits:
            self.streams[eng].append((waits, None, None))

    def emit(self):
        nc = self.nc
        streams = self.streams
        self.streams = {e: [] for e in self.ENGS}

        def run(engobj, lst):
            for waits, fn, inc in lst:
                for sem, val in waits:
                    engobj.wait_ge(sem, val)
                if fn is not None:
                    inst = fn(engobj)
                    if inc is not None:
                        inst.then_inc(inc[0], inc[1])

        with nc.Block() as block:
            @block.tensor
            def _(e):
                run(e, streams["pe"])

            @block.scalar
            def _(e):
                run(e, streams["act"])

            @block.vector
            def _(e):
                run(e, streams["dve"])

            @block.gpsimd
            def _(e):
                run(e, streams["pool"])

            @block.sync
            def _(e):
                run(e, streams["sp"])


class Ring:
    def __init__(self, P, nc, name, shape, dt, n, psum=False, dma=False, alloc=None):
        self.items = []
        for i in range(n):
            if alloc is not None:
                t = alloc(f"{name}{i}", shape, dt)
            else:
                t = (nc.alloc_psum_tensor if psum else nc.alloc_sbuf_tensor)(f"{name}{i}", shape, dt)
            self.items.append((t, Res(f"{name}{i}"), P.dma_sem(f"ds_{name}{i}") if dma else None))
        self.i = 0

    def next(self):
        it = self.items[self.i % len(self.items)]
        self.i += 1
        return it


_NC_CACHE = {}


def build_program(n_phys=10240, stop=""):
    nc = bass.Bass("TRN2", target_bir_lowering=False)
    P = Prog(nc)

    def din(name, shape, dt=F32):
        return nc.dram_tensor(name, list(shape), dt, kind="ExternalInput").ap()

    def dout(name, shape, dt=F32):
        return nc.dram_tensor(name, list(shape), dt, kind="ExternalOutput").ap()

    xp = din("xp", [4, 514, D])
    xs = din("xs", [64, D])
    sconv = din("sconv", [32, D])
    ptab = din("ptab", [1, 16 * NPAGE], I32)
    HALFQ = (n_phys // 2) * 32
    cache_q = [din(f"cache_q{i}", [HALFQ, 4 * 320]) for i in range(2)]
    sel_d = din("sel", [128, 4])
    iota32_d = din("iota32", [128, 1])
    w_ffn1_gu = din("w_ffn1_gu", [2, D, 2 * DFF])
    w_ffn1_down = din("w_ffn1_down", [2, DFF, D])
    w_ffn2_gu = din("w_ffn2_gu", [2, D, 2 * DFF])
    w_ffn2_down = din("w_ffn2_down", [2, DFF, D])
    w_conv_in = din("w_conv_in", [1, D, 3 * D])
    w_conv_out = din("w_conv_out", [1, D, D])
    w_dq = din("w_dq", [1, D, 512])
    w_uq = din("w_uq", [1, 512, 1536])
    w_o = din("w_o", [1, D, D])
    w_dkv = din("w_dkv", [D, 320])
    w_uk = din("w_uk", [256, D])
    w_uv = din("w_uv", [256, D])
    gvec_d = din("gvec", [88, 128])
    q_norm = din("q_norm", [1, 512])
    q_nope_norm = din("q_nope_norm", [1, 128])
    q_pe_norm = din("q_pe_norm", [1, 64])
    kv_norm = din("kv_norm", [256])
    k_pe_norm = din("k_pe_norm", [64])
    ident_d = din("ident", [128, 128])
    iota_d = din("iota", [128, 1], I32)
    cs_p_d = din("cs_p", [4, 512, 64])
    cs_s_d = din("cs_s", [64, 64])
    mask_p_d = din("mask_p", [128, 16 * 512], BF16)
    mask_n_d = din("mask_n", [64, 512], BF16)

    y_p = dout("y_p", [4, 512, D])
    y_s = dout("y_s", [64, D])
    cso_p = dout("cso_p", [4, 2, D])
    cso_s = dout("cso_s", [32, D])
    lat_p = dout("lat_p", [4, 512, 256])
    kpe_p = dout("kpe_p", [4, 512, 64])
    lat_s = dout("lat_s", [64, 256])
    kpe_s = dout("kpe_s", [64, 64])

    xsc = nc.dram_tensor("xsc", [NT, 128, 8 * 512], F32).ap()
    sendb = [nc.dram_tensor(f"sendb{c}", [128, 1024], F32) for c in range(3)]
    recvb = [nc.dram_tensor(f"recvb{c}", [512, 1024], F32) for c in range(3)]
    r_xsc = [Res(f"xsc{t}") for t in range(NT)]
    r_sendb = Res("sendb")
    r_recvb = Res("recvb")
    r_out = Res("out")
    out_tokens = []

    def sb(name, shape, dt=F32):
        return nc.alloc_sbuf_tensor(name, list(shape), dt)

    ident_f = sb("ident_f", [128, 128]); r_ident_f = Res()
    ident_b = sb("ident_b", [128, 128], BF16); r_ident_b = Res()
    ones_b = sb("ones_b", [128, 128], BF16); r_ones = Res()
    eps_t = sb("eps_t", [128, 1]); r_eps = Res()
    iota_t = sb("iota_t", [128, 1], I32); r_iota = Res()
    gv = sb("gv", [128, 88]); r_gv = Res()
    qn_bc = sb("qn_bc", [128, 512]); qnn_bc = sb("qnn_bc", [128, 128]); qpn_bc = sb("qpn_bc", [128, 64])
    kvn_bc = sb("kvn_bc", [128, 256]); kpn_bc = sb("kpn_bc", [128, 64]); r_bc = Res()
    r_oT = [Res(f"oT{t}") for t in range(NT)]
    QNS = sb("QNS", [128, 8, 64], BF16); r_QNS = Res("QNS")
    qsc_n = nc.dram_tensor("qsc_n", [8, 128, 2048], BF16).ap(); r_qsc_n = [Res(f"qscn{t}") for t in range(4)]
    qsc_p = nc.dram_tensor("qsc_p", [4, 128, 2048], BF16).ap(); r_qsc_p = [Res(f"qscp{t}") for t in range(4)]
    qpT_s = sb("qpT_s", [64, 8, 64], BF16); r_qpT_s = Res()
    latn_aug = sb("latn_aug", [64, 257], BF16); r_latn = Res()
    latnT = sb("latnT", [128, 3, 64], BF16); r_latnT = Res()
    ptab_t = sb("ptab_t", [128, 16 * NPAGE], I32); r_ptab = Res()

    PSALL = [nc.alloc_psum_tensor(f"ps{i}", [128, 512], F32) for i in range(8)]
    PSRES = [Res(f"ps{i}") for i in range(8)]

    class PsRing:
        def __init__(self, idxs):
            self.idxs = idxs
            self.i = 0

        def next(self):
            k = self.idxs[self.i % len(self.idxs)]
            self.i += 1
            return PSALL[k], PSRES[k], None

    PS = PsRing(list(range(8)))
    DS = [P.dma_sem(f"dso{i}") for i in range(6)]
    dsi = [0]

    def ods():
        dsi[0] += 1
        return DS[dsi[0] % len(DS)]

    s0 = P.dma_sem("setup")
    P.dma("sp", lambda e: e.dma_start(out=ident_f[:], in_=ident_d), s0, writes=[r_ident_f])
    P.dma("sp", lambda e: e.dma_start(out=iota_t[:], in_=iota_d), s0, writes=[r_iota])
    P.dma("sp", lambda e: e.dma_start(out=qn_bc[:], in_=q_norm[0].partition_broadcast(128)), s0, writes=[r_bc])
    P.dma("sp", lambda e: e.dma_start(out=qnn_bc[:], in_=q_nope_norm[0].partition_broadcast(128)), s0, writes=[r_bc])
    P.dma("sp", lambda e: e.dma_start(out=qpn_bc[:], in_=q_pe_norm[0].partition_broadcast(128)), s0, writes=[r_bc])
    P.dma("sp", lambda e: e.dma_start(out=kvn_bc[:], in_=kv_norm.partition_broadcast(128)), s0, writes=[r_bc])
    P.dma("sp", lambda e: e.dma_start(out=kpn_bc[:], in_=k_pe_norm.partition_broadcast(128)), s0, writes=[r_bc])
    P.dma("sp", lambda e: e.dma_start(out=ptab_t[:], in_=ptab[0].partition_broadcast(128)), s0, writes=[r_ptab])
    gv_in = sb("gv_in", [88, 128]); r_gv_in = Res()
    P.dma("sp", lambda e: e.dma_start(out=gv_in[:], in_=gvec_d), s0, writes=[r_gv_in])
    for _r in (r_ident_f, r_iota, r_bc, r_ptab, r_gv_in):
        _r.w = (s0.sem, s0.val)
    P.op("dve", lambda e: e.tensor_copy(out=ident_b[:], in_=ident_f[:]), reads=[r_ident_f], writes=[r_ident_b])
    P.op("dve", lambda e: e.memset(ones_b[:], 1.0), writes=[r_ones])
    P.op("dve", lambda e: e.memset(eps_t[:], EPS), writes=[r_eps])
    P.op("dve", lambda e: e.memset(latn_aug[:, 256:257], 1.0), writes=[r_latn])
    pst, pr, _ = PS.next()
    P.op("pe", lambda e: e.transpose(out=pst[:, 0:88], in_=gv_in[:], identity=ident_f[0:88, 0:88]),
         reads=[r_gv_in, r_ident_f], writes=[pr])
    P.op("dve", lambda e: e.tensor_copy(out=gv[:], in_=pst[:, 0:88]), reads=[pr], writes=[r_gv])
    GV = dict(nf1=(0, 8), nm=(16, 24), nf2=(32, 40), nkv=48, cw=(56, 64, 72), kn=80)

    def psb(pt):
        return pt[:, :].bitcast(BF16)

    def rstd_from_ss(ss_ap, out_ap, n_over, res_in, res_out, np_=128):
        P.op("act", lambda e: e.activation(out=out_ap, in_=ss_ap, func=AF.Sqrt, bias=eps_t[0:np_, 0:1], scale=1.0 / n_over),
             reads=res_in + [r_eps], writes=[res_out])
        P.op("dve", lambda e: e.reciprocal(out=out_ap, in_=out_ap), reads=[res_out], writes=[res_out])

    uniq = [0]

    def make_common(es):
        uniq[0] += 1
        sfx = f"_{uniq[0]}"

        def sbx(name, shape, dt=F32):
            return es.enter_context(nc.sbuf_tensor(name + sfx, list(shape), dt))
        C = {}
        C["sbx"] = sbx
        C["wA"] = Ring(P, nc, "wA", [128, 8, 512], BF16, 4, dma=True, alloc=sbx)
        C["X"] = sbx("X", [128, 8, 512]); C["r_X"] = [Res(f"X{k}") for k in range(8)]
        C["XH"] = sbx("XH", [128, 8, 2]); C["r_XH"] = Res("XH")
        C["XN"] = sbx("XN", [128, 8, 514], BF16); C["r_XN"] = Res("XN")
        C["RSTD"] = sbx("RSTD", [128, 514]); C["r_RSTD"] = Res("RSTD")
        C["Hh"] = sbx("Hh", [128, NJ, 514], BF16); C["r_H"] = [Res(f"H{j}") for j in range(NJ)]
        C["SG"] = Ring(P, nc, "SG", [128, 514], F32, 2, alloc=sbx)
        return C

    def load_wA(C, src_ap, ncols=512, nk=8):
        t, r, s = C["wA"].next()
        P.dma("pool", lambda e: e.dma_start(out=t[:, 0:nk, 0:ncols], in_=src_ap), s, writes=[r])
        return t, r

    def wview(w2d, c0, ncols):
        return w2d.rearrange("(k p) c -> p k c", p=128)[:, :, c0:c0 + ncols]

    def segs(N, halo):
        s = [(0, N)]
        if halo:
            s.append((512, 2))
        return s

    def rmsnorm_fm(C, N, halo, gcol):
        X, XH, XN, RSTD, Hh = C["X"], C["XH"], C["XN"], C["RSTD"], C["Hh"]
        r_X, r_XH, r_XN, r_RSTD, r_H = C["r_X"], C["r_XH"], C["r_XN"], C["r_RSTD"], C["r_H"]
        for (c0, n) in segs(N, halo):
            src = (lambda k, n=n: X[:, k, 0:n]) if c0 == 0 else (lambda k, n=n: XH[:, k, 0:n])
            rsrc = r_X if c0 == 0 else [r_XH] * 8
            for k in range(8):
                P.op("dve", lambda e, k=k, src=src, c0=c0, n=n: e.tensor_tensor(out=Hh[:, k, c0:c0 + n], in0=src(k), in1=src(k), op=ALU.mult),
                     reads=[rsrc[k]], writes=[r_H[k]])
            pt, pr, _ = PS.next()
            for k in range(8):
                P.op("pe", lambda e, k=k, pt=pt, c0=c0, n=n: e.matmul(pt[:, 0:n], ones_b[:], Hh[:, k, c0:c0 + n], start=(k == 0), stop=(k == 7)),
                     reads=r_H[0:8] + [r_ones] if k == 0 else [], writes=[pr] if k == 0 else [], inc=(k == 7))
            rstd_from_ss(pt[:, 0:n], RSTD[:, c0:c0 + n], float(D), [pr], r_RSTD)
            for k in range(8):
                P.op("dve", lambda e, k=k, src=src, c0=c0, n=n: e.scalar_tensor_tensor(out=XN[:, k, c0:c0 + n], in0=src(k), scalar=gv[:, gcol + k:gcol + k + 1],
                                                                                         in1=RSTD[:, c0:c0 + n], op0=ALU.mult, op1=ALU.mult),
                     reads=[rsrc[k], r_gv, r_RSTD], writes=[r_XN])

    def ffn(C, N, halo, w_gu, w_down):
        X, XH, XN, Hh, SG = C["X"], C["XH"], C["XN"], C["Hh"], C["SG"]
        r_X, r_XH, r_XN, r_H = C["r_X"], C["r_XH"], C["r_XN"], C["r_H"]
        sg = segs(N, halo)
        for jb in range(0, NJ, 4):
            nj = min(4, NJ - jb)
            tg, rg = load_wA(C, wview(w_gu, jb * 128, nj * 128), nj * 128)
            tu, ru = load_wA(C, wview(w_gu, DFF + jb * 128, nj * 128), nj * 128)
            for jj in range(nj):
                j = jb + jj
                for (c0, n) in sg:
                    pg, prg, _ = PS.next()
                    pu, pru, _ = PS.next()
                    for (pt, pr, tw, rw) in ((pg, prg, tg, rg), (pu, pru, tu, ru)):
                        for k in range(8):
                            P.op("pe", lambda e, pt=pt, tw=tw, k=k, jj=jj, c0=c0, n=n: e.matmul(pt[:, 0:n], tw[:, k, jj * 128:(jj + 1) * 128], XN[:, k, c0:c0 + n],
                                                                                                  start=(k == 0), stop=(k == 7)),
                                 reads=[rw, r_XN] if k == 0 else [], writes=[pr] if k == 0 else [], inc=(k == 7))
                    st, sr, _ = SG.next()
                    P.op("act", lambda e, st=st, pg=pg, n=n: e.activation(out=st[:, 0:n], in_=pg[:, 0:n], func=AF.Silu), reads=[prg], writes=[sr])
                    P.op("dve", lambda e, st=st, pu=pu, j=j, c0=c0, n=n: e.tensor_tensor(out=Hh[:, j, c0:c0 + n], in0=st[:, 0:n], in1=pu[:, 0:n], op=ALU.mult),
                         reads=[sr, pru], writes=[r_H[j]])
        for mh in range(2):
            accs = [[PS.next() for _ in sg] for _ in range(4)]
            for jb in range(0, NJ, 4):
                nj = min(4, NJ - jb)
                src = w_down[jb * 128:(jb + nj) * 128, mh * 512:(mh + 1) * 512].rearrange("(j p) m -> p j m", p=128)
                tw, rw = load_wA(C, src, 512, nj)
                for jj in range(nj):
                    j = jb + jj
                    for mm in range(4):
                        for si, (c0, n) in enumerate(sg):
                            pt, pr, _ = accs[mm][si]
                            first = (j == 0)
                            last = (j == NJ - 1)
                            P.op("pe", lambda e, pt=pt, tw=tw, jj=jj, mm=mm, j=j, c0=c0, n=n, first=first, last=last:
                                 e.matmul(pt[:, 0:n], tw[:, jj, mm * 128:(mm + 1) * 128], Hh[:, j, c0:c0 + n], start=first, stop=last),
                                 reads=([rw] if jj == 0 else []) + [r_H[j]], writes=[pr] if first else [], touch=[] if first else [pr],
                                 inc=(last or (jj == nj - 1 and mm == 3 and si == len(sg) - 1)))
            for mm in range(4):
                m = mh * 4 + mm
                for si, (c0, n) in enumerate(sg):
                    pt, pr, _ = accs[mm][si]
                    if c0 == 0:
                        P.op("dve", lambda e, pt=pt, m=m, n=n: e.scalar_tensor_tensor(out=X[:, m, 0:n], in0=pt[:, 0:n], scalar=0.5, in1=X[:, m, 0:n],
                                                                                       op0=ALU.mult, op1=ALU.add), reads=[pr, r_X[m]], writes=[r_X[m]])
                    else:
                        P.op("dve", lambda e, pt=pt, m=m, n=n: e.scalar_tensor_tensor(out=XH[:, m, 0:n], in0=pt[:, 0:n], scalar=0.5, in1=XH[:, m, 0:n],
                                                                                       op0=ALU.mult, op1=ALU.add), reads=[pr, r_XH], writes=[r_XH])

    def rope_tm(src_ap, dst_ap, nh, csv, np_, rd, wr, tmp_ap, r_tmp):
        cosb = csv[:, 0:32].unsqueeze(1).to_broadcast([np_, nh, 32])
        sinb = csv[:, 32:64].unsqueeze(1).to_broadcast([np_, nh, 32])
        x1, x2 = src_ap[:, :, 0:32], src_ap[:, :, 32:64]
        o1, o2 = dst_ap[:, :, 0:32], dst_ap[:, :, 32:64]
        t1, t2 = tmp_ap[:, :, 0:32], tmp_ap[:, :, 32:64]
        P.op("dve", lambda e: e.tensor_tensor(out=t1, in0=x2, in1=sinb, op=ALU.mult), reads=rd, writes=[r_tmp])
        P.op("dve", lambda e: e.tensor_tensor(out=t2, in0=x1, in1=sinb, op=ALU.mult), reads=rd + [r_tmp], writes=[r_tmp])
        P.op("dve", lambda e: e.tensor_tensor(out=t1, in0=x1, in1=cosb, op=ALU.mult) if False else e.tensor_tensor(out=o1, in0=x1, in1=cosb, op=ALU.mult), reads=rd, writes=wr)
        P.op("dve", lambda e: e.tensor_tensor(out=o2, in0=x2, in1=cosb, op=ALU.mult), reads=rd + wr, writes=wr)
        P.op("dve", lambda e: e.tensor_tensor(out=o1, in0=o1, in1=t1, op=ALU.subtract), reads=[r_tmp] + wr, writes=wr)
        P.op("dve", lambda e: e.tensor_tensor(out=o2, in0=o2, in1=t2, op=ALU.add), reads=[r_tmp] + wr, writes=wr)

    with ExitStack() as es:
        C = make_common(es)
        sbx = C["sbx"]
        X, XH, XN, Hh = C["X"], C["XH"], C["XN"], C["Hh"]
        r_X, r_XH, r_XN, r_H = C["r_X"], C["r_XH"], C["r_XN"], C["r_H"]
        wUQ = sbx("wUQ", [128, 4, 1536], BF16); r_wUQ = Res(); s_wUQ = P.dma_sem("wUQ")
        XINr = Ring(P, nc, "XIN", [128, D], F32, 2, dma=True, alloc=sbx)
        XHI = sbx("XHI", [32, D]); r_XHI = Res("XHI"); s_XHI = P.dma_sem("XHI")
        UEXT = sbx("UEXT", [128, 8, 516]); r_UEXT = [Res(f"U{k}") for k in range(8)]
        HS = Ring(P, nc, "HS", [128, 514], F32, 2, alloc=sbx)
        YC = Ring(P, nc, "YC", [128, 512], F32, 2, alloc=sbx)
        CSO = sbx("CSO", [128, 8, 32]); r_CSO = Res("CSO")
        CSO2 = sbx("CSO2", [32, D]); r_CSO2 = Res("CSO2"); s_CSO2 = P.dma_sem("CSO2")
        CKV = sbx("CKV", [128, 320]); r_CKV = Res("CKV")
        SQT = sbx("SQT", [128, 1536]); r_SQT = Res("SQT")
        SS = sbx("SS", [128, 16]); r_SS = Res("SS")
        RS = sbx("RS", [128, 16]); r_RS = Res("RS")
        LATO = Ring(P, nc, "LATO", [128, 320], F32, 2, dma=True, alloc=sbx)
        TMP = sbx("TMP", [128, 1024]); r_TMP = Res("TMP")
        TMP2 = sbx("TMP2", [128, 512]); r_TMP2 = Res("TMP2")
        LB = sbx("LB", [128, 384], BF16); r_LB = Res("LB")
        ST = sbx("ST", [128, 3, 512], BF16); r_ST = Res("ST"); s_ST = P.dma_sem("ST")
        CS = sbx("CS", [128, 4, 64]); r_CS = Res("CS"); s_CS = P.dma_sem("CS")
        CQ = sbx("CQ", [128, 512]); r_CQ = Res("CQ")
        CQB = sbx("CQB", [128, 512], BF16); r_CQB = Res("CQB")
        CQT = sbx("CQT", [128, 4, 512], BF16); r_CQT = Res("CQT")
        QS = sbx("QS", [128, 1536]); r_QS = Res("QS")
        QNB = sbx("QNB", [128, 1024], BF16); r_QNB = Res("QNB")
        QPF = sbx("QPF", [128, 512]); r_QPF = Res("QPF")
        QPB = sbx("QPB", [128, 512], BF16); r_QPB = Res("QPB")
        QTN = sbx("QTN", [128, 8, 512], BF16); r_QTN = Res("QTN"); s_QTN = P.dma_sem("QTN")
        QTP = sbx("QTP", [128, 4, 512], BF16); r_QTP = Res("QTP"); s_QTP = P.dma_sem("QTP"); s_XS = P.dma_sem("XS")

        def ZB(k, N):
            return Hh[:, 8 + k, 0:N]

        def load_x(t):
            N = 512 if t < 4 else 64
            nb, bp = (4, 128) if t < 4 else (1, 64)
            blocks = []
            for b in range(nb):
                xt, xr, xsem = XINr.next()
                if t < 4:
                    P.dma("sp", lambda e, xt=xt, b=b: e.dma_start(out=xt[:, :], in_=xp[t, 2 + b * 128:2 + (b + 1) * 128, :]), xsem, writes=[xr])
                else:
                    P.dma("sp", lambda e, xt=xt: e.dma_start(out=xt[0:64, :], in_=xs), xsem, writes=[xr])
                for half in range(2):
                    pt, pr, _ = PS.next()
                    for kk in range(4):
                        k = half * 4 + kk
                        P.op("pe", lambda e, pt=pt, xt=xt, k=k, kk=kk: e.transpose(out=pt[:, kk * 128:kk * 128 + bp], in_=xt[0:bp, k * 128:(k + 1) * 128], identity=ident_f[0:bp, 0:bp]),
                             reads=[xr, r_ident_f] if kk == 0 else [], writes=[pr] if kk == 0 else [], inc=(kk == 3))
                    P.op("act", lambda e, pt=pt, half=half, b=b: e.activation(out=X[:, half * 4:half * 4 + 4, b * bp:(b + 1) * bp],
                                                                             in_=pt[:, :].rearrange("p (k n) -> p k n", n=128)[:, :, 0:bp], func=AF.Copy),
                         reads=[pr], writes=r_X[half * 4:half * 4 + 4])
            if t < 4:
                P.dma("sp", lambda e: e.dma_start(out=XHI[0:2, :], in_=xp[t, 0:2, :]), s_XHI, writes=[r_XHI])
                pt, pr, _ = PS.next()
                for k in range(8):
                    P.op("pe", lambda e, pt=pt, k=k: e.transpose(out=pt[:, 2 * k:2 * k + 2], in_=XHI[0:2, k * 128:(k + 1) * 128], identity=ident_f[0:2, 0:2]),
                         reads=[r_XHI, r_ident_f] if k == 0 else [], writes=[pr] if k == 0 else [], inc=(k == 7))
                P.op("act", lambda e, pt=pt: e.activation(out=XH[:, :, :], in_=pt[:, 0:16].rearrange("p (k c) -> p k c", c=2), func=AF.Copy), reads=[pr], writes=[r_XH])
            else:
                P.dma("sp", lambda e: e.dma_start(out=XHI[0:32, :], in_=sconv), s_XHI, writes=[r_XHI])
                for k in range(8):
                    pt, pr, _ = PS.next()
                    P.op("pe", lambda e, pt=pt, k=k: e.transpose(out=pt[:, 0:32], in_=XHI[0:32, k * 128:(k + 1) * 128], identity=ident_f[0:32, 0:32]),
                         reads=[r_XHI, r_ident_f], writes=[pr])
                    P.op("act", lambda e, pt=pt, k=k: e.activation(out=UEXT[:, k, 0:96].rearrange("p (b j) -> p b j", j=6)[:, :, 0:2],
                                                                   in_=pt[:, 0:32].rearrange("p (b j) -> p b j", j=2), func=AF.Copy), reads=[pr], writes=[r_UEXT[k]])

        def conv_mixer(t):
            N = 512 if t < 4 else 64
            halo = t < 4
            nseq, L = (1, 512) if t < 4 else (16, 4)
            W = w_conv_in[0].rearrange("(k p) c -> p k c", p=128)
            rmsnorm_fm(C, N, halo, GV["nm"][0])

            def uview(k, lo, hi):
                return UEXT[:, k, 0:nseq * (L + 2)].rearrange("p (b j) -> p b j", j=L + 2)[:, :, lo:hi]

            def nview(ap2d):
                return ap2d.rearrange("p (b j) -> p b j", j=L)

            cw = GV["cw"]
            for m in range(8):
                tw, rw = load_wA(C, W[:, :, m * 128:(m + 1) * 128], 128)
                tw2, rw2 = load_wA(C, W[:, :, D + m * 128:D + (m + 1) * 128], 128)
                tw3, rw3 = load_wA(C, W[:, :, 2 * D + m * 128:2 * D + (m + 1) * 128], 128)
                pb, prb, _ = PS.next()
                pc, prc, _ = PS.next()
                ph, prh, _ = PS.next()
                for (pt, pr, twx, rwx) in ((pb, prb, tw, rw), (pc, prc, tw2, rw2), (ph, prh, tw3, rw3)):
                    for k in range(8):
                        P.op("pe", lambda e, pt=pt, twx=twx, k=k: e.matmul(pt[:, 0:N], twx[:, k, 0:128], XN[:, k, 0:N], start=(k == 0), stop=(k == 7)),
                             reads=[rwx, r_XN] if k == 0 else [], writes=[pr] if k == 0 else [], inc=(k == 7))
                hs, hr, _ = HS.next()
                P.op("act", lambda e, hs=hs, ph=ph: e.activation(out=hs[:, 0:N], in_=ph[:, 0:N], func=AF.Copy), reads=[prh], writes=[hr])
                P.op("dve", lambda e, hs=hs, pc=pc, m=m: e.tensor_tensor(out=uview(m, 2, L + 2), in0=nview(pc[:, 0:N]), in1=nview(hs[:, 0:N]), op=ALU.mult),
                     reads=[prc, hr], writes=[r_UEXT[m]])
                if halo:
                    pc2, prc2, _ = PS.next()
                    ph2, prh2, _ = PS.next()
                    for (pt, pr, twx, rwx) in ((pc2, prc2, tw2, rw2), (ph2, prh2, tw3, rw3)):
                        for k in range(8):
                            P.op("pe", lambda e, pt=pt, twx=twx, k=k: e.matmul(pt[:, 0:2], twx[:, k, 0:128], XN[:, k, 512:514], start=(k == 0), stop=(k == 7)),
                                 reads=[rwx, r_XN] if k == 0 else [], writes=[pr] if k == 0 else [], inc=(k == 7))
                    hs2, hr2, _ = HS.next()
                    P.op("act", lambda e, hs2=hs2, ph2=ph2: e.activation(out=hs2[:, 0:2], in_=ph2[:, 0:2], func=AF.Copy), reads=[prh2], writes=[hr2])
                    P.op("dve", lambda e, hs2=hs2, pc2=pc2, m=m: e.tensor_tensor(out=UEXT[:, m, 0:2], in0=pc2[:, 0:2], in1=hs2[:, 0:2], op=ALU.mult),
                         reads=[prc2, hr2, r_UEXT[m]], writes=[r_UEXT[m]])
                P.op("act", lambda e, m=m: e.activation(out=CSO[:, m, 0:2 * nseq].rearrange("p (b j) -> p b j", j=2), in_=uview(m, L, L + 2), func=AF.Copy),
                     reads=[r_UEXT[m]], writes=[r_CSO])
                yc, yr, _ = YC.next()
                P.op("dve", lambda e, yc=yc, m=m: e.tensor_scalar(out=nview(yc[:, 0:N]), in0=uview(m, 0, L), scalar1=gv[:, cw[0] + m:cw[0] + m + 1], scalar2=None, op0=ALU.mult),
                     reads=[r_UEXT[m], r_gv], writes=[yr])
                for jx in (1, 2):
                    P.op("dve", lambda e, yc=yc, m=m, jx=jx: e.scalar_tensor_tensor(out=nview(yc[:, 0:N]), in0=uview(m, jx, L + jx), scalar=gv[:, cw[jx] + m:cw[jx] + m + 1],
                                                                                     in1=nview(yc[:, 0:N]), op0=ALU.mult, op1=ALU.add),
                         reads=[r_UEXT[m], r_gv, yr], writes=[yr])
                P.op("dve", lambda e, yc=yc, pb=pb, m=m: e.tensor_tensor(out=ZB(m, N), in0=yc[:, 0:N], in1=pb[:, 0:N], op=ALU.mult),
                     reads=[yr, prb], writes=[r_H[8 + m]])
            n = 2 * nseq
            for half in range(2):
                pt, pr, _ = PS.next()
                for kk in range(4):
                    k = half * 4 + kk
                    P.op("pe", lambda e, pt=pt, k=k, kk=kk: e.transpose(out=pt[0:n, kk * 128:(kk + 1) * 128], in_=CSO[:, k, 0:n], identity=ident_f[:, :]),
                         reads=[r_CSO, r_ident_f] if kk == 0 else [], writes=[pr] if kk == 0 else [], inc=(kk == 3))
                P.op("act", lambda e, pt=pt, half=half: e.activation(out=CSO2[0:n, half * 512:(half + 1) * 512], in_=pt[0:n, :], func=AF.Copy), reads=[pr], writes=[r_CSO2])
            dst = cso_p[t] if t < 4 else cso_s
            out_tokens.append(P.dma("sp", lambda e: e.dma_start(out=dst, in_=CSO2[0:n, :]), s_CSO2, reads=[r_CSO2], writes=[r_out]))
            Wo = w_conv_out[0]
            for mb in range(0, 8, 4):
                tw, rw = load_wA(C, wview(Wo, mb * 128, 512))
                for mm in range(4):
                    m = mb + mm
                    pt, pr, _ = PS.next()
                    for k in range(8):
                        P.op("pe", lambda e, pt=pt, tw=tw, k=k, mm=mm: e.matmul(pt[:, 0:N], tw[:, k, mm * 128:(mm + 1) * 128], ZB(k, N), start=(k == 0), stop=(k == 7)),
                             reads=[rw] + r_H[8:16] if k == 0 else [], writes=[pr] if k == 0 else [], inc=(k == 7))
                    P.op("dve", lambda e, pt=pt, m=m: e.tensor_tensor(out=X[:, m, 0:N], in0=pt[:, 0:N], in1=X[:, m, 0:N], op=ALU.add), reads=[pr, r_X[m]], writes=[r_X[m]])

        def load_cs(t):
            if t < 4:
                P.dma("sp", lambda e: e.dma_start(out=CS[:, :, :], in_=cs_p_d[t].rearrange("(b p) c -> p b c", p=128)), s_CS, writes=[r_CS])
            else:
                P.dma("sp", lambda e: e.dma_start(out=CS[0:64, 0, :], in_=cs_s_d), s_CS, writes=[r_CS])

        def shared_kv(t):
            N = 512 if t < 4 else 64
            nb, bp = (4, 128) if t < 4 else (1, 64)
            rmsnorm_fm(C, N, False, GV["nkv"])
            tw, rw = load_wA(C, w_dkv.rearrange("(k p) c -> p k c", p=128), 320)
            load_cs(t)
            for b in range(nb):
                pt, pr, _ = PS.next()
                for k in range(8):
                    P.op("pe", lambda e, pt=pt, k=k, b=b: e.matmul(pt[0:bp, 0:320], XN[:, k, b * bp:(b + 1) * bp], tw[:, k, 0:320], start=(k == 0), stop=(k == 7)),
                         reads=[rw, r_XN] if k == 0 else [], writes=[pr] if k == 0 else [], inc=(k == 7))
                P.op("act", lambda e, pt=pt: e.activation(out=CKV[0:bp, :], in_=pt[0:bp, 0:320], func=AF.Copy), reads=[pr], writes=[r_CKV])
                P.op("dve", lambda e: e.tensor_tensor(out=SQT[0:bp, 0:320], in0=CKV[0:bp, :], in1=CKV[0:bp, :], op=ALU.mult), reads=[r_CKV], writes=[r_SQT])
                P.op("dve", lambda e: e.tensor_reduce(out=SS[0:bp, 0:1], in_=SQT[0:bp, 0:256], axis=AX.X, op=ALU.add), reads=[r_SQT], writes=[r_SS])
                P.op("dve", lambda e: e.tensor_reduce(out=SS[0:bp, 1:2], in_=SQT[0:bp, 256:320], axis=AX.X, op=ALU.add), reads=[r_SQT, r_SS], writes=[r_SS])
                rstd_from_ss(SS[0:bp, 0:1], RS[0:bp, 0:1], 256.0, [r_SS], r_RS, bp)
                rstd_from_ss(SS[0:bp, 1:2], RS[0:bp, 1:2], 64.0, [r_SS, r_RS], r_RS, bp)
                lo, lr, los = LATO.next()
                P.op("dve", lambda e, lo=lo: e.scalar_tensor_tensor(out=lo[0:bp, 0:256], in0=CKV[0:bp, 0:256], scalar=RS[0:bp, 0:1], in1=kvn_bc[0:bp, :],
                                                                  op0=ALU.mult, op1=ALU.mult), reads=[r_CKV, r_RS, r_bc], writes=[lr])
                P.op("dve", lambda e: e.scalar_tensor_tensor(out=TMP[0:bp, 0:64], in0=CKV[0:bp, 256:320], scalar=RS[0:bp, 1:2], in1=kpn_bc[0:bp, :],
                                                           op0=ALU.mult, op1=ALU.mult), reads=[r_CKV, r_RS, r_bc], writes=[r_TMP])
                rope_tm(TMP[0:bp, 0:64].unsqueeze(1), lo[0:bp, 256:320].unsqueeze(1), 1, CS[0:bp, b, :], bp, [r_TMP, r_CS], [lr], TMP2[0:bp, 0:64].unsqueeze(1), r_TMP2)
                if t < 4:
                    out_tokens.append(P.dma("sp", lambda e, lo=lo, b=b: e.dma_start(out=lat_p[t, b * 128:(b + 1) * 128, :], in_=lo[:, 0:256]), los, reads=[lr], writes=[r_out]))
                    out_tokens.append(P.dma("sp", lambda e, lo=lo, b=b: e.dma_start(out=kpe_p[t, b * 128:(b + 1) * 128, :], in_=lo[:, 256:320]), los, reads=[lr], writes=[r_out]))
                else:
                    out_tokens.append(P.dma("sp", lambda e, lo=lo: e.dma_start(out=lat_s, in_=lo[0:64, 0:256]), los, reads=[lr], writes=[r_out]))
                    out_tokens.append(P.dma("sp", lambda e, lo=lo: e.dma_start(out=kpe_s, in_=lo[0:64, 256:320]), los, reads=[lr], writes=[r_out]))
                    P.op("act", lambda e, lo=lo: e.activation(out=latn_aug[0:64, 0:256], in_=lo[0:64, 0:256], func=AF.Copy), reads=[lr, r_latn], writes=[r_latn])
                P.op("act", lambda e, lo=lo: e.activation(out=LB[0:bp, 0:320], in_=lo[0:bp, 0:320], func=AF.Copy), reads=[lr], writes=[r_LB])
                P.op("act", lambda e, lo=lo: e.activation(out=LB[0:bp, 320:384], in_=lo[0:bp, 256:320], func=AF.Copy), reads=[lr, r_LB], writes=[r_LB])
                pt2, pr2, _ = PS.next()
                pv = psb(pt2)
                for c in range(3):
                    P.op("pe", lambda e, pv=pv, c=c: e.transpose(out=pv[:, c * 128:c * 128 + bp], in_=LB[0:bp, c * 128:(c + 1) * 128], identity=ident_b[0:bp, 0:bp]),
                         reads=[r_LB, r_ident_b] if c == 0 else [], writes=[pr2] if c == 0 else [], inc=(c == 2))
                if t < 4:
                    P.op("act", lambda e, pv=pv, b=b: e.activation(out=ST[:, :, b * 128:(b + 1) * 128], in_=pv[:, 0:384].rearrange("p (c n) -> p c n", n=128), func=AF.Copy),
                         reads=[pr2], writes=[r_ST])
                else:
                    P.op("act", lambda e, pv=pv: e.activation(out=latnT[:, :, :], in_=pv[:, 0:384].rearrange("p (c n) -> p c n", n=128)[:, :, 0:64], func=AF.Copy),
                         reads=[pr2], writes=[r_latnT])
            if t < 4:
                for c in range(3):
                    P.dma("sp", lambda e, c=c: e.dma_start(out=sendb[c].ap().bitcast(BF16)[:, t * 512:(t + 1) * 512], in_=ST[:, c, :]), s_ST,
                          reads=[r_ST], writes=[r_sendb])

        def q_proj(t):
            N = 512 if t < 4 else 64
            nb, bp = (4, 128) if t < 4 else (1, 64)
            rmsnorm_fm(C, N, False, GV["nm"][1])
            tw, rw = load_wA(C, w_dq[0].rearrange("(k p) c -> p k c", p=128), 512)
            P.dma("pool", lambda e: e.dma_start(out=wUQ[:, :, :], in_=w_uq[0].rearrange("(k p) c -> p k c", p=128)), s_wUQ, writes=[r_wUQ])
            for b in range(nb):
                pt, pr, _ = PS.next()
                for k in range(8):
                    P.op("pe", lambda e, pt=pt, k=k, b=b: e.matmul(pt[0:bp, 0:512], XN[:, k, b * bp:(b + 1) * bp], tw[:, k, 0:512], start=(k == 0), stop=(k == 7)),
                         reads=[rw, r_XN] if k == 0 else [], writes=[pr] if k == 0 else [], inc=(k == 7))
                P.op("act", lambda e, pt=pt: e.activation(out=CQ[0:bp, :], in_=pt[0:bp, 0:512], func=AF.Copy), reads=[pr], writes=[r_CQ])
                P.op("dve", lambda e: e.tensor_tensor(out=SQT[0:bp, 0:512], in0=CQ[0:bp, :], in1=CQ[0:bp, :], op=ALU.mult), reads=[r_CQ], writes=[r_SQT])
                P.op("dve", lambda e: e.tensor_reduce(out=SS[0:bp, 0:1], in_=SQT[0:bp, 0:512], axis=AX.X, op=ALU.add), reads=[r_SQT], writes=[r_SS])
                rstd_from_ss(SS[0:bp, 0:1], RS[0:bp, 0:1], 512.0, [r_SS], r_RS, bp)
                P.op("dve", lambda e: e.scalar_tensor_tensor(out=CQB[0:bp, :], in0=CQ[0:bp, :], scalar=RS[0:bp, 0:1], in1=qn_bc[0:bp, :], op0=ALU.mult, op1=ALU.mult),
                     reads=[r_CQ, r_RS, r_bc], writes=[r_CQB])
                pt2, pr2, _ = PS.next()
                pv = psb(pt2)
                for c in range(4):
                    P.op("pe", lambda e, pv=pv, c=c: e.transpose(out=pv[:, c * 128:c * 128 + bp], in_=CQB[0:bp, c * 128:(c + 1) * 128], identity=ident_b[0:bp, 0:bp]),
                         reads=[r_CQB, r_ident_b] if c == 0 else [], writes=[pr2] if c == 0 else [], inc=(c == 3))
                P.op("act", lambda e, pv=pv, b=b: e.activation(out=CQT[:, :, b * bp:(b + 1) * bp], in_=pv[:, 0:512].rearrange("p (c n) -> p c n", n=128)[:, :, 0:bp], func=AF.Copy),
                     reads=[pr2, r_CQT], writes=[r_CQT])
            load_cs(t)
            for b in range(nb):
                for g in range(4):
                    pt, pr, _ = PS.next()
                    for k in range(4):
                        P.op("pe", lambda e, pt=pt, k=k, b=b, g=g: e.matmul(pt[0:bp, 0:384], CQT[:, k, b * bp:(b + 1) * bp], wUQ[:, k, g * 384:(g + 1) * 384], start=(k == 0), stop=(k == 3)),
                             reads=[r_wUQ, r_CQT] if k == 0 else [], writes=[pr] if k == 0 else [], inc=(k == 3))
                    P.op("act", lambda e, pt=pt, g=g: e.activation(out=QS[0:bp, g * 384:(g + 1) * 384], in_=pt[0:bp, 0:384], func=AF.Copy), reads=[pr, r_QS], writes=[r_QS])
                q3 = QS[0:bp, :].rearrange("p (h d) -> p h d", d=192)
                s3 = SQT[0:bp, :].rearrange("p (h d) -> p h d", d=192)
                P.op("dve", lambda e: e.tensor_tensor(out=SQT[0:bp, :], in0=QS[0:bp, :], in1=QS[0:bp, :], op=ALU.mult), reads=[r_QS], writes=[r_SQT])
                P.op("dve", lambda e, s3=s3: e.tensor_reduce(out=SS[0:bp, 0:8], in_=s3[:, :, 0:128], axis=AX.X, op=ALU.add), reads=[r_SQT], writes=[r_SS])
                P.op("dve", lambda e, s3=s3: e.tensor_reduce(out=SS[0:bp, 8:16], in_=s3[:, :, 128:192], axis=AX.X, op=ALU.add), reads=[r_SQT, r_SS], writes=[r_SS])
                rstd_from_ss(SS[0:bp, 0:8], RS[0:bp, 0:8], 128.0, [r_SS], r_RS, bp)
                rstd_from_ss(SS[0:bp, 8:16], RS[0:bp, 8:16], 64.0, [r_SS, r_RS], r_RS, bp)
                tn = TMP[0:bp, :].rearrange("p (h d) -> p h d", d=128)
                P.op("dve", lambda e, tn=tn, q3=q3: e.tensor_tensor(out=tn, in0=q3[:, :, 0:128], in1=RS[0:bp, 0:8].unsqueeze(2).to_broadcast([bp, 8, 128]), op=ALU.mult),
                     reads=[r_QS, r_RS], writes=[r_TMP])
                P.op("dve", lambda e, tn=tn: e.tensor_tensor(out=QNB[0:bp, :].rearrange("p (h d) -> p h d", d=128), in0=tn, in1=qnn_bc[0:bp, :].unsqueeze(1).to_broadcast([bp, 8, 128]), op=ALU.mult),
                     reads=[r_TMP, r_bc], writes=[r_QNB])
                tp = TMP[0:bp, 0:512].rearrange("p (h d) -> p h d", d=64)
                P.op("dve", lambda e, tp=tp, q3=q3: e.tensor_tensor(out=tp, in0=q3[:, :, 128:192], in1=RS[0:bp, 8:16].unsqueeze(2).to_broadcast([bp, 8, 64]), op=ALU.mult),
                     reads=[r_QS, r_RS], writes=[r_TMP])
                qpf = QPF[0:bp, :].rearrange("p (h d) -> p h d", d=64)
                P.op("dve", lambda e, tp=tp, qpf=qpf: e.tensor_tensor(out=qpf, in0=tp, in1=qpn_bc[0:bp, :].unsqueeze(1).to_broadcast([bp, 8, 64]), op=ALU.mult),
                     reads=[r_TMP, r_bc], writes=[r_QPF])
                rope_tm(qpf, QPB[0:bp, :].rearrange("p (h d) -> p h d", d=64), 8, CS[0:bp, b, :], bp, [r_QPF, r_CS], [r_QPB], TMP2[0:bp, :].rearrange("p (h d) -> p h d", d=64), r_TMP2)
                pt2, pr2, _ = PS.next()
                pv = psb(pt2)
                for h in range(8):
                    P.op("pe", lambda e, pv=pv, h=h: e.transpose(out=pv[:, h * 128:h * 128 + bp], in_=QNB[0:bp, h * 128:(h + 1) * 128], identity=ident_b[0:bp, 0:bp]),
                         reads=[r_QNB, r_ident_b] if h == 0 else [], writes=[pr2] if h == 0 else [], inc=(h == 7))
                if t < 4:
                    P.op("act", lambda e, pv=pv, b=b: e.activation(out=QTN[:, :, b * 128:(b + 1) * 128], in_=pv[:, :].rearrange("p (h n) -> p h n", n=128), func=AF.Copy),
                         reads=[pr2, r_QTN], writes=[r_QTN])
                else:
                    P.op("act", lambda e, pv=pv: e.activation(out=QNS[:, :, :], in_=pv[:, :].rearrange("p (h n) -> p h n", n=128)[:, :, 0:64], func=AF.Copy),
                         reads=[pr2], writes=[r_QNS])
                pt3, pr3, _ = PS.next()
                pv3 = psb(pt3)
                if t < 4:
                    for c in range(4):
                        P.op("pe", lambda e, pv3=pv3, c=c: e.transpose(out=pv3[:, c * 128:(c + 1) * 128], in_=QPB[:, c * 128:(c + 1) * 128], identity=ident_b[:, :]),
                             reads=[r_QPB, r_ident_b] if c == 0 else [], writes=[pr3] if c == 0 else [], inc=(c == 3))
                    P.op("act", lambda e, pv3=pv3, b=b: e.activation(out=QTP[:, :, b * 128:(b + 1) * 128], in_=pv3[:, 0:512].rearrange("p (c n) -> p c n", n=128), func=AF.Copy),
                         reads=[pr3, r_QTP], writes=[r_QTP])
                else:
                    for h in range(8):
                        P.op("pe", lambda e, pv3=pv3, h=h: e.transpose(out=pv3[0:64, h * 64:(h + 1) * 64], in_=QPB[0:64, h * 64:(h + 1) * 64], identity=ident_b[0:64, 0:64]),
                             reads=[r_QPB, r_ident_b] if h == 0 else [], writes=[pr3] if h == 0 else [], inc=(h == 7))
                    P.op("act", lambda e, pv3=pv3: e.activation(out=qpT_s[:, :, :], in_=pv3[0:64, 0:512].rearrange("p (h n) -> p h n", n=64), func=AF.Copy),
                         reads=[pr3], writes=[r_qpT_s])
            if t < 4:
                P.dma("sp", lambda e: e.dma_start(out=qsc_n.rearrange("h p n -> p h n")[:, :, t * 512:(t + 1) * 512], in_=QTN[:, :, :]), s_QTN, reads=[r_QTN], writes=[r_qsc_n[t]])
                P.dma("sp", lambda e: e.dma_start(out=qsc_p.rearrange("h p n -> p h n")[:, :, t * 512:(t + 1) * 512], in_=QTP[:, :, :]), s_QTP, reads=[r_QTP], writes=[r_qsc_p[t]])

        dbg_y = None
        for t in range(NT):
            N = 512 if t < 4 else 64
            halo = t < 4
            load_x(t)
            if stop == "A_load": break
            rmsnorm_fm(C, N, halo, GV["nf1"][0])
            if stop == "A_norm": break
            ffn(C, N, halo, w_ffn1_gu[0], w_ffn1_down[0])
            if stop == "A_ffn1": break
            conv_mixer(t)
            if stop == "A_conv": break
            rmsnorm_fm(C, N, False, GV["nf2"][0])
            ffn(C, N, False, w_ffn2_gu[0], w_ffn2_down[0])
            shared_kv(t)
            if stop == "A_kv": break
            rmsnorm_fm(C, N, False, GV["nf1"][1])
            ffn(C, N, False, w_ffn1_gu[1], w_ffn1_down[1])
            if stop == "A_x1": break
            P.dma("sp", lambda e, t=t, N=N: e.dma_start(out=xsc[t].rearrange("p (k n) -> p k n", n=512)[:, :, 0:N], in_=X[:, :, 0:N]), s_XS, reads=r_X, writes=[r_xsc[t]])
            q_proj(t)
            if stop == "A_q": break
        if stop.startswith("A_"):
            tokd = P.dma("sp", lambda e: e.dma_start(out=y_p[0].rearrange("(p a) d -> p (a d)", p=128), in_=X[:, :, :].rearrange("p k n -> p (k n)")), s_XS, reads=r_X, writes=[r_out])
            P.wait_all("sp", out_tokens + [tokd])
            P.emit()
            return nc
        s_cc = P.dma_sem("cc")
        for c in range(3):
            P.dma("pool", lambda e, c=c: e.collective_compute("AllGather", ALU.bypass, replica_groups=[[0, 1, 2, 3], [4, 5, 6, 7]],
                                                              ins=[sendb[c].ap().opt()], outs=[recvb[c].ap().opt()]), s_cc, reads=[r_sendb], writes=[r_recvb], inc=1)
        if stop == "A":
            P.wait_all("sp", out_tokens)
        P.emit()
    if stop == "A":
        return nc

    oT = sb("oT", [128, 8, 2112], BF16)
    with ExitStack() as es:
        def sbx(name, shape, dt=F32):
            return es.enter_context(nc.sbuf_tensor(name, list(shape), dt))
        latT = sbx("latT", [128, 2, 8192], BF16); r_latT = Res("latT"); s_latT = P.dma_sem("latT")
        kpeT = sbx("kpeT", [128, 8192], BF16); r_kpeT = Res("kpeT")
        WUK = sbx("WUK", [128, 2, D], BF16); r_WUK = Res("WUK"); s_W = P.dma_sem("WUKa"); s_W2 = P.dma_sem("WUVa")
        WUV = sbx("WUV", [128, 2, D], BF16); r_WUV = Res("WUV")
        MASK = sbx("MASK", [128, 16 * 512], BF16); r_MASK = Res("MASK"); s_MASK = P.dma_sem("MASK")
        KN = Ring(P, nc, "KN", [128, 8192], BF16, 2, alloc=sbx)
        VV = Ring(P, nc, "VV", [128, 64, 130], BF16, 2, alloc=sbx)
        SQK = Ring(P, nc, "SQK", [128, 512], BF16, 2, alloc=sbx)
        RK = Ring(P, nc, "RK", [128, 512], F32, 2, alloc=sbx)
        QN = Ring(P, nc, "QN", [128, 512], BF16, 2, dma=True, alloc=sbx)
        QP = Ring(P, nc, "QP", [128, 512], BF16, 2, dma=True, alloc=sbx)
        PT = Ring(P, nc, "PT", [128, 512], BF16, 3, alloc=sbx)
        RC = Ring(P, nc, "RC", [128, 1], F32, 4, alloc=sbx)
        ON = Ring(P, nc, "ON", [128, 128], BF16, 2, alloc=sbx)
        PSO = PsRing([0, 1, 2, 3])
        PSS = PsRing([4, 5])
        PSX = PsRing([6, 7])

        for r in range(4):
            for c in range(3):
                dst = (latT[:, c, :] if c < 2 else kpeT[:, :]).rearrange("p (s r2 i) -> p s r2 i", s=4, r2=4, i=512)[:, :, r, :]
                src = recvb[c].ap().bitcast(BF16)[r * 128:(r + 1) * 128, :].rearrange("p (s i) -> p s i", i=512)
                P.dma("sp", lambda e, dst=dst, src=src: e.dma_start(out=dst, in_=src), s_latT, reads=[r_recvb], writes=[r_latT if c < 2 else r_kpeT])
        r_latT.w = r_kpeT.w = (s_latT.sem, s_latT.val)
        P.dma("pool", lambda e: e.dma_start(out=WUK[:, :, :], in_=w_uk.rearrange("(c p) m -> p c m", p=128)), s_W, writes=[r_WUK])
        P.dma("pool", lambda e: e.dma_start(out=WUV[:, :, :], in_=w_uv.rearrange("(c p) m -> p c m", p=128)), s_W2, writes=[r_WUV])
        P.dma("sp", lambda e: e.dma_start(out=MASK[:, :], in_=mask_p_d), s_MASK, writes=[r_MASK])
        kcol = GV["kn"]
        for (vt, vr, _) in VV.items:
            P.op("pool", lambda e, vt=vt: e.memset(vt[:, :, 128:130], 1.0), writes=[vr])
        nheads = 8
        for h in range(nheads):
            kn_t, kn_r, _ = KN.next()
            v_t, v_r, _ = VV.next()
            for kt in range(16):
                pk, prk, _ = PSX.next()
                for c in range(2):
                    P.op("pe", lambda e, pk=pk, c=c, h=h, kt=kt: e.matmul(pk[:, :], WUK[:, c, h * 128:(h + 1) * 128], latT[:, c, kt * 512:(kt + 1) * 512], start=(c == 0), stop=(c == 1)),
                         reads=[r_WUK, r_latT] if c == 0 else [], writes=[prk] if c == 0 else [], inc=(c == 1))
                sq, sqr, _ = SQK.next()
                P.op("act", lambda e, sq=sq, pk=pk: e.activation(out=sq[:, :], in_=pk[:, :], func=AF.Square), reads=[prk], writes=[sqr])
                p2, pr2, _ = PSX.next()
                P.op("pe", lambda e, p2=p2, sq=sq: e.matmul(p2[:, :], ones_b[:], sq[:, :], start=True, stop=True), reads=[sqr, r_ones], writes=[pr2])
                rk, rkr, _ = RK.next()
                rstd_from_ss(p2[:, :], rk[:, :], 128.0, [pr2], rkr)
                P.op("dve", lambda e, pk=pk, rk=rk, kn_t=kn_t, kt=kt: e.scalar_tensor_tensor(out=kn_t[:, kt * 512:(kt + 1) * 512], in0=pk[:, :], scalar=gv[:, kcol:kcol + 1], in1=rk[:, :],
                                                                                           op0=ALU.mult, op1=ALU.mult), reads=[prk, rkr, r_gv], writes=[kn_r])
            for kb4 in range(16):
                pvv, prv, _ = PSX.next()
                for q4 in range(4):
                    kb = kb4 * 4 + q4
                    for c in range(2):
                        P.op("pe", lambda e, pvv=pvv, q4=q4, kb=kb, c=c, h=h: e.matmul(pvv[:, q4 * 128:(q4 + 1) * 128], latT[:, c, kb * 128:(kb + 1) * 128], WUV[:, c, h * 128:(h + 1) * 128],
                                                                                    start=(c == 0), stop=(c == 1)),
                             reads=[r_WUV, r_latT] if (q4 == 0 and c == 0) else [], writes=[prv] if (q4 == 0 and c == 0) else [], inc=(q4 == 3 and c == 1))
                P.op("act", lambda e, pvv=pvv, v_t=v_t, kb4=kb4: e.activation(out=v_t[:, kb4 * 4:(kb4 + 1) * 4, 0:128], in_=pvv[:, :].rearrange("p (q d) -> p q d", d=128), func=AF.Copy),
                     reads=[prv, v_r], writes=[v_r])
            base = 64 * (h % 2)
            for s in range(4):
                qn_t, qn_r, qn_s = QN.next()
                qp_t, qp_r, qp_s = QP.next()
                P.dma("sp", lambda e, qn_t=qn_t, h=h, s=s: e.dma_start(out=qn_t[:, :], in_=qsc_n[h, :, s * 512:(s + 1) * 512]), qn_s, reads=[r_qsc_n[s]], writes=[qn_r])
                P.dma("sp", lambda e, qp_t=qp_t, h=h, s=s: e.dma_start(out=qp_t[:, :], in_=qsc_p[h // 2, :, s * 512:(s + 1) * 512]), qp_s, reads=[r_qsc_p[s]], writes=[qp_r])
                accs = [PSO.next() for _ in range(4)]
                nkb = 16 * (s + 1)
                def front(j):
                    sc, scr, _ = PSS.next()
                    P.op("pe", lambda e, sc=sc, kn_t=kn_t, qn_t=qn_t, j=j: e.matmul(sc[:, :], kn_t[:, j * 128:(j + 1) * 128], qn_t[:, :], start=True, stop=False),
                         reads=[kn_r, qn_r], writes=[scr], inc=False)
                    P.op("pe", lambda e, sc=sc, qp_t=qp_t, j=j, base=base: e.matmul(sc[:, :], kpeT[base:base + 64, j * 128:(j + 1) * 128], qp_t[base:base + 64, :], start=False, stop=True),
                         reads=[r_kpeT, qp_r], touch=[scr])
                    pt_, ptr, _ = PT.next()
                    P.op("act", lambda e, pt_=pt_, sc=sc: e.activation(out=pt_[:, :], in_=sc[:, :], func=AF.Exp, scale=SCALE), reads=[scr], writes=[ptr])
                    if j >= 16 * s:
                        jj = j - 16 * s
                        P.op("pool", lambda e, pt_=pt_, jj=jj: e.tensor_tensor(out=pt_[:, :], in0=pt_[:, :], in1=MASK[:, jj * 512:(jj + 1) * 512], op=ALU.mult),
                             reads=[ptr, r_MASK], writes=[ptr])
                    return pt_, ptr

                def back(j, pt_, ptr):
                    for i in range(4):
                        at, ar, _ = accs[i]
                        P.op("pe", lambda e, at=at, pt_=pt_, v_t=v_t, i=i, j=j, nkb=nkb: e.matmul(at[:, 0:129], pt_[:, i * 128:(i + 1) * 128], v_t[:, j, 0:129], start=(j == 0), stop=(j == nkb - 1)),
                             reads=[ptr, v_r] if i == 0 else [], writes=[ar] if j == 0 else [], touch=[] if j == 0 else [ar], inc=(i == 3))

                pend = front(0)
                for j in range(nkb):
                    nxt = front(j + 1) if j + 1 < nkb else None
                    back(j, *pend)
                    pend = nxt
                for i in range(4):
                    at, ar, _ = accs[i]
                    rc, rcr, _ = RC.next()
                    P.op("dve", lambda e, rc=rc, at=at: e.reciprocal(out=rc[:, :], in_=at[:, 128:129]), reads=[ar], writes=[rcr])
                    on, onr, _ = ON.next()
                    P.op("dve", lambda e, on=on, at=at, rc=rc: e.tensor_scalar(out=on[:, :], in0=at[:, 0:128], scalar1=rc[:, 0:1], scalar2=None, op0=ALU.mult),
                         reads=[ar, rcr], writes=[onr])
                    px, pxr, _ = PSX.next()
                    pxv = psb(px)
                    P.op("pe", lambda e, pxv=pxv, on=on: e.transpose(out=pxv[:, 0:128], in_=on[:, :], identity=ident_b[:, :]), reads=[onr, r_ident_b], writes=[pxr])
                    P.op("act", lambda e, pxv=pxv, h=h, s=s, i=i: e.activation(out=oT[:, h, s * 512 + i * 128:s * 512 + (i + 1) * 128], in_=pxv[:, 0:128], func=AF.Copy),
                         reads=[pxr, r_oT[s]], writes=[r_oT[s]])
        if stop == "AT":
            P.wait_all("sp", out_tokens)
        P.emit()
    if stop == "AT":
        return nc

    with ExitStack() as es:
        def sbx(name, shape, dt=F32):
            return es.enter_context(nc.sbuf_tensor(name, list(shape), dt))
        WUK = sbx("WUKs", [128, 2, D], BF16); r_WUK = Res("WUK"); s_W = P.dma_sem("WUKs"); s_W2 = P.dma_sem("WUVs")
        WUV = sbx("WUVs", [128, 2, D], BF16); r_WUV = Res("WUV")
        WUKT = sbx("WUKT", [128, 8, 256], BF16); r_WUKT = Res("WUKT")
        QABS = sbx("QABS", [128, 2, 8, 64], BF16); r_QABS = Res("QABS")
        MN = sbx("MN", [64, 512], BF16); r_MN = Res("MN"); s_MN = P.dma_sem("MN")
        LPK = Ring(P, nc, "LPK", [128, 1280], BF16, 3, dma=True, alloc=sbx)
        IDX = Ring(P, nc, "IDX", [128, 1], I32, 3, alloc=sbx)
        IDX1 = Ring(P, nc, "IDX1", [128, 1], I32, 3, alloc=sbx)
        I4 = Ring(P, nc, "I4", [128, 4], F32, 2, alloc=sbx)
        I1 = Ring(P, nc, "I1", [128, 1], F32, 2, alloc=sbx)
        sel_t = sbx("sel_t", [128, 4]); iota32_t = sbx("iota32_t", [128, 1]); r_sel = Res("sel"); s_sel = P.dma_sem("sel")
        P.dma("sp", lambda e: e.dma_start(out=sel_t[:, :], in_=sel_d), s_sel, writes=[r_sel])
        P.dma("sp", lambda e: e.dma_start(out=iota32_t[:, :], in_=iota32_d), s_sel, writes=[r_sel])
        r_sel.w = (s_sel.sem, s_sel.val)
        LTP = Ring(P, nc, "LTP", [128, 384], BF16, 2, alloc=sbx)
        SQP = Ring(P, nc, "SQP", [128, 1024], F32, 2, alloc=sbx)
        SSP = Ring(P, nc, "SSP", [128, 8], F32, 2, alloc=sbx)
        RSP = Ring(P, nc, "RSP", [128, 8], F32, 2, alloc=sbx)
        T1 = Ring(P, nc, "T1", [128, 32], F32, 2, alloc=sbx)
        PTS = Ring(P, nc, "PTS", [128, 32], BF16, 2, alloc=sbx)
        SNW = sbx("SNW", [64, 512]); r_SNW = Res("SNW")
        PNW = sbx("PNW", [64, 512], BF16); r_PNW = Res("PNW")
        PN2 = sbx("PN2", [64, 16, 32], BF16); r_PN2 = Res("PN2")
        RCs = Ring(P, nc, "RCs", [32, 1], F32, 2, alloc=sbx)
        OLN = Ring(P, nc, "OLN", [32, 256], BF16, 2, alloc=sbx)
        OLT = sbx("OLT", [128, 2, 8, 64], BF16); r_OLT = Res("OLT")
        PSK = PsRing([0, 1, 2, 3])
        PSX = PsRing([4, 5])
        PSO = PsRing([6, 7])
        kcol = GV["kn"]

        P.dma("pool", lambda e: e.dma_start(out=WUK[:, :, :], in_=w_uk.rearrange("(c p) m -> p c m", p=128)), s_W, writes=[r_WUK])
        P.dma("pool", lambda e: e.dma_start(out=WUV[:, :, :], in_=w_uv.rearrange("(c p) m -> p c m", p=128)), s_W2, writes=[r_WUV])
        P.dma("sp", lambda e: e.dma_start(out=MN[:, :], in_=mask_n_d), s_MN, writes=[r_MN])
        for h in range(8):
            px, pxr, _ = PSX.next()
            pxv = psb(px)
            for c in range(2):
                P.op("pe", lambda e, pxv=pxv, c=c, h=h: e.transpose(out=pxv[:, c * 128:(c + 1) * 128], in_=WUK[:, c, h * 128:(h + 1) * 128], identity=ident_b[:, :]),
                     reads=[r_WUK, r_ident_b] if c == 0 else [], writes=[pxr] if c == 0 else [], inc=(c == 1))
            P.op("dve", lambda e, pxv=pxv, h=h: e.tensor_scalar(out=WUKT[:, h, :], in0=pxv[:, 0:256], scalar1=gv[:, kcol:kcol + 1], scalar2=None, op0=ALU.mult),
                 reads=[pxr, r_gv, r_WUKT], writes=[r_WUKT])
        for h in range(8):
            for c in range(2):
                px, pxr, _ = PSX.next()
                P.op("pe", lambda e, px=px, c=c, h=h: e.matmul(px[:, 0:64], WUKT[:, h, c * 128:(c + 1) * 128], QNS[:, h, :], start=True, stop=True),
                     reads=[r_WUKT, r_QNS], writes=[pxr])
                P.op("act", lambda e, px=px, c=c, h=h: e.activation(out=QABS[:, c, h, :], in_=px[:, 0:64], func=AF.Copy), reads=[pxr, r_QABS], writes=[r_QABS])

        def key_block(lat_bf, kpe_bf, nk, qabs_rhs, qp_rhs, ncol):
            px, pxr, _ = PSX.next()
            pxv = psb(px)
            return px, pxr, pxv

        _breg = {}

        def breg(e):
            if "r" not in _breg:
                _breg["r"] = e.to_reg(HALFQ - 1)
            return _breg["r"]

        pk0, pkr0, _ = PSK.next()
        pk1, pkr1, _ = PSK.next()
        for hf, (pk, pkr) in enumerate(((pk0, pkr0), (pk1, pkr1))):
            for c in range(2):
                P.op("pe", lambda e, pk=pk, c=c, hf=hf: e.matmul(pk[0:64, :], latnT[:, c, :], WUK[:, c, hf * 512:(hf + 1) * 512], start=(c == 0), stop=(c == 1)),
                     reads=[r_latnT, r_WUK] if c == 0 else [], writes=[pkr] if c == 0 else [], inc=(c == 1))
        sq, sqr, _ = SQP.next()
        P.op("act", lambda e, sq=sq, pk0=pk0: e.activation(out=sq[0:64, 0:512], in_=pk0[0:64, :], func=AF.Square), reads=[pkr0], writes=[sqr])
        P.op("act", lambda e, sq=sq, pk1=pk1: e.activation(out=sq[0:64, 512:1024], in_=pk1[0:64, :], func=AF.Square), reads=[pkr1, sqr], writes=[sqr])
        ss, ssr, _ = SSP.next()
        P.op("dve", lambda e, ss=ss, sq=sq: e.tensor_reduce(out=ss[0:64, :], in_=sq[0:64, :].rearrange("p (h d) -> p h d", d=128), axis=AX.X, op=ALU.add), reads=[sqr], writes=[ssr])
        rsn, rsnr, _ = RSP.next()
        rstd_from_ss(ss[0:64, :], rsn[0:64, :], 128.0, [ssr], rsnr, 64)
        pa, par, _ = PSK.next()
        pb_, pbr, _ = PSK.next()
        for c in range(2):
            P.op("pe", lambda e, pa=pa, c=c: e.matmul(pa[0:64, :], latnT[:, c, :], QABS[:, c, :, :], start=(c == 0), stop=(c == 1)),
                 reads=[r_latnT, r_QABS] if c == 0 else [], writes=[par] if c == 0 else [], inc=(c == 1))
        P.op("pe", lambda e, pb_=pb_: e.matmul(pb_[0:64, :], latnT[0:64, 2, :], qpT_s[:, :, :], start=True, stop=True), reads=[r_latnT, r_qpT_s], writes=[pbr])
        P.op("dve", lambda e, pa=pa, rsn=rsn: e.tensor_tensor(out=SNW[:, :].rearrange("p (h n) -> p h n", n=64), in0=pa[0:64, :].rearrange("p (h n) -> p h n", n=64),
                                                             in1=rsn[0:64, :].unsqueeze(2).to_broadcast([64, 8, 64]), op=ALU.mult), reads=[par, rsnr], writes=[r_SNW])
        P.op("dve", lambda e, pb_=pb_: e.tensor_tensor(out=SNW[:, :], in0=SNW[:, :], in1=pb_[0:64, :], op=ALU.add), reads=[pbr, r_SNW], writes=[r_SNW])
        P.op("act", lambda e: e.activation(out=PNW[:, :], in_=SNW[:, :], func=AF.Exp, scale=SCALE), reads=[r_SNW], writes=[r_PNW])
        P.op("dve", lambda e: e.tensor_tensor(out=PNW[:, :], in0=PNW[:, :], in1=MN[:, :], op=ALU.mult), reads=[r_PNW, r_MN], writes=[r_PNW])
        P.op("dve", lambda e: e.tensor_copy(out=PN2[:, :, :].rearrange("p b (h t) -> p b h t", t=4), in_=PNW[:, :].rearrange("p (h b t) -> p b h t", h=8, b=16, t=4)),
             reads=[r_PNW], writes=[r_PN2])

        def finalize_seq(b, ol, olr):
            P.op("pe", lambda e, ol=ol, b=b: e.matmul(ol[0:32, 0:257], PN2[:, b, :], latn_aug[:, 0:257], start=False, stop=True), reads=[r_PN2, r_latn], touch=[olr])
            rc, rcr, _ = RCs.next()
            P.op("dve", lambda e, rc=rc, ol=ol: e.reciprocal(out=rc[:, :], in_=ol[0:32, 256:257]), reads=[olr], writes=[rcr])
            on, onr, _ = OLN.next()
            P.op("dve", lambda e, on=on, ol=ol, rc=rc: e.tensor_scalar(out=on[:, :], in0=ol[0:32, 0:256], scalar1=rc[:, 0:1], scalar2=None, op0=ALU.mult), reads=[olr, rcr], writes=[onr])
            px, pxr, _ = PSX.next()
            pxv = psb(px)
            for c in range(2):
                P.op("pe", lambda e, pxv=pxv, on=on, c=c: e.transpose(out=pxv[:, c * 32:(c + 1) * 32], in_=on[:, c * 128:(c + 1) * 128], identity=ident_b[0:32, 0:32]),
                     reads=[onr, r_ident_b] if c == 0 else [], writes=[pxr] if c == 0 else [], inc=(c == 1))
            P.op("act", lambda e, pxv=pxv, b=b: e.activation(out=OLT[:, :, :, b * 4:(b + 1) * 4], in_=pxv[:, 0:64].rearrange("p (c h t) -> p c h t", c=2, h=8, t=4), func=AF.Copy),
                 reads=[pxr, r_OLT], writes=[r_OLT])

        LTP4 = Ring(P, nc, "LTQ", [128, 384], BF16, 4, alloc=sbx)
        RSP3 = Ring(P, nc, "RSQ", [128, 8], F32, 3, alloc=sbx)
        PTS3 = Ring(P, nc, "PTQ", [128, 32], BF16, 3, alloc=sbx)
        NG = NPAGE // 4
        tiles = [(b, g, j) for b in range(16) for g in range(NG) for j in range(4)]
        groups = {}
        ST = {}
        olmap = {}

        def gather(b, g):
            if (b, g) in groups or b >= 16:
                return
            col = b * NPAGE + 4 * g
            i4, i4r, _ = I4.next()
            P.op("dve", lambda e, i4=i4, col=col: e.tensor_tensor(out=i4[:, :], in0=ptab_t[:, col:col + 4], in1=sel_t[:, :], op=ALU.mult), reads=[r_ptab, r_sel], writes=[i4r])
            i1, i1r, _ = I1.next()
            P.op("dve", lambda e, i4=i4, i1=i1: e.tensor_reduce(out=i1[:, :], in_=i4[:, :], axis=AX.X, op=ALU.add), reads=[i4r], writes=[i1r])
            ix, ixr, _ = IDX.next()
            P.op("dve", lambda e, ix=ix, i1=i1: e.scalar_tensor_tensor(out=ix[:, :], in0=i1[:, :], scalar=32.0, in1=iota32_t[:, :], op0=ALU.mult, op1=ALU.add),
                 reads=[i1r, r_sel], writes=[ixr])
            ix1, ixr1, _ = IDX1.next()
            P.op("dve", lambda e, ix=ix, ix1=ix1: e.tensor_scalar(out=ix1[:, :], in0=ix[:, :], scalar1=-HALFQ, scalar2=None, op0=ALU.add), reads=[ixr], writes=[ixr1])
            lpk, lpr, lps = LPK.next()
            for hf, (ixx, ixxr) in enumerate(((ix, ixr), (ix1, ixr1))):
                P.dma("pool", lambda e, lpk=lpk, ixx=ixx, hf=hf: e.indirect_dma_start(out=lpk[:, :], out_offset=None, in_=cache_q[hf],
                                                                                     in_offset=bass.IndirectOffsetOnAxis(ap=ixx[:, :], axis=0),
                                                                                     bounds_check=breg(e), oob_is_err=False),
                      lps, reads=[ixxr], writes=[lpr] if hf == 0 else [], touch=[lpr] if hf == 1 else [])
            groups[(b, g)] = (lpk, lpr)

        def nxt_group(b, g):
            return (b, g + 1) if g + 1 < NG else (b + 1, 0)

        def S1(t):
            b, g, j = tiles[t]
            if j == 0:
                gather(b, g)
                gather(*nxt_group(b, g))
            lpk, lpr = groups[(b, g)]
            px, pxr, _ = PSX.next()
            pxv = psb(px)
            for c in range(2):
                P.op("pe", lambda e, pxv=pxv, lpk=lpk, c=c, j=j: e.transpose(out=pxv[:, c * 128:(c + 1) * 128], in_=lpk[:, j * 320 + c * 128:j * 320 + (c + 1) * 128], identity=ident_b[:, :]),
                     reads=[lpr, r_ident_b] if c == 0 else [], writes=[pxr] if c == 0 else [], inc=False)
            P.op("pe", lambda e, pxv=pxv, lpk=lpk, j=j: e.transpose(out=pxv[0:64, 256:384], in_=lpk[:, j * 320 + 256:j * 320 + 320], identity=ident_b[:, :]), reads=[lpr], touch=[pxr])
            lt, ltr, _ = LTP4.next()
            P.op("act", lambda e, lt=lt, pxv=pxv: e.activation(out=lt[:, 0:256], in_=pxv[:, 0:256], func=AF.Copy), reads=[pxr], writes=[ltr])
            P.op("act", lambda e, lt=lt, pxv=pxv: e.activation(out=lt[0:64, 256:384], in_=pxv[0:64, 256:384], func=AF.Copy), reads=[pxr, ltr], writes=[ltr])
            ST[t] = dict(lt=lt, ltr=ltr, lpk=lpk, lpr=lpr)

        def S2(t):
            d = ST[t]
            lt, ltr = d["lt"], d["ltr"]
            pk0, pkr0, _ = PSK.next()
            pk1, pkr1, _ = PSK.next()
            for hf, (pk, pkr) in enumerate(((pk0, pkr0), (pk1, pkr1))):
                for c in range(2):
                    P.op("pe", lambda e, pk=pk, lt=lt, c=c, hf=hf: e.matmul(pk[:, :], lt[:, c * 128:(c + 1) * 128], WUK[:, c, hf * 512:(hf + 1) * 512], start=(c == 0), stop=(c == 1)),
                         reads=[ltr, r_WUK] if c == 0 else [], writes=[pkr] if c == 0 else [], inc=(c == 1))
            sq, sqr, _ = SQP.next()
            P.op("act", lambda e, sq=sq, pk0=pk0: e.activation(out=sq[:, 0:512], in_=pk0[:, :], func=AF.Square), reads=[pkr0], writes=[sqr])
            P.op("act", lambda e, sq=sq, pk1=pk1: e.activation(out=sq[:, 512:1024], in_=pk1[:, :], func=AF.Square), reads=[pkr1, sqr], writes=[sqr])
            ss, ssr, _ = SSP.next()
            P.op("dve", lambda e, ss=ss, sq=sq: e.tensor_reduce(out=ss[:, :], in_=sq[:, :].rearrange("p (h d) -> p h d", d=128), axis=AX.X, op=ALU.add), reads=[sqr], writes=[ssr])
            rs, rsr, _ = RSP3.next()
            rstd_from_ss(ss[:, :], rs[:, :], 128.0, [ssr], rsr)
            d.update(rs=rs, rsr=rsr)

        def S3(t):
            d = ST[t]
            b = tiles[t][0]
            lt, ltr, rs, rsr = d["lt"], d["ltr"], d["rs"], d["rsr"]
            ps_, psr, _ = PSX.next()
            for c in range(2):
                P.op("pe", lambda e, ps_=ps_, lt=lt, c=c, b=b: e.matmul(ps_[:, 0:32], lt[:, c * 128:(c + 1) * 128], QABS[:, c, :, b * 4:(b + 1) * 4], start=(c == 0), stop=(c == 1)),
                     reads=[ltr, r_QABS] if c == 0 else [], writes=[psr] if c == 0 else [], inc=False)
            P.op("pe", lambda e, ps_=ps_, lt=lt, b=b: e.matmul(ps_[:, 32:64], lt[0:64, 256:384], qpT_s[:, :, b * 4:(b + 1) * 4], start=True, stop=True),
                 reads=[r_qpT_s], touch=[psr])
            t1, t1r, _ = T1.next()
            P.op("dve", lambda e, t1=t1, ps_=ps_, rs=rs: e.tensor_tensor(out=t1[:, :].rearrange("p (h t) -> p h t", t=4), in0=ps_[:, 0:32].rearrange("p (h t) -> p h t", t=4),
                                                                        in1=rs[:, :].unsqueeze(2).to_broadcast([128, 8, 4]), op=ALU.mult), reads=[psr, rsr], writes=[t1r])
            P.op("dve", lambda e, t1=t1, ps_=ps_: e.tensor_tensor(out=t1[:, :], in0=t1[:, :], in1=ps_[:, 32:64], op=ALU.add), reads=[psr, t1r], writes=[t1r])
            pts, ptsr, _ = PTS3.next()
            P.op("act", lambda e, pts=pts, t1=t1: e.activation(out=pts[:, :], in_=t1[:, :], func=AF.Exp, scale=SCALE), reads=[t1r], writes=[ptsr])
            d.update(pts=pts, ptsr=ptsr)

        def S4(t):
            d = ST.pop(t)
            b, g, j = tiles[t]
            pg = 4 * g + j
            if b not in olmap:
                olmap[b] = PSO.next()
            ol, olr, _ = olmap[b]
            pts, ptsr, lpk, lpr = d["pts"], d["ptsr"], d["lpk"], d["lpr"]
            P.op("pe", lambda e, ol=ol, pts=pts, lpk=lpk, j=j, pg=pg: e.matmul(ol[0:32, 0:256], pts[:, :], lpk[:, j * 320:j * 320 + 256], start=(pg == 0), stop=False),
                 reads=[ptsr, lpr], writes=[olr] if pg == 0 else [], touch=[] if pg == 0 else [olr], inc=False)
            P.op("pe", lambda e, ol=ol, pts=pts: e.matmul(ol[0:32, 256:257], pts[:, :], ones_b[:, 0:1], start=False, stop=False, skip_group_check=True),
                 reads=[r_ones], touch=[olr])
            if pg == NPAGE - 1:
                finalize_seq(b, ol, olr)

        NTL = len(tiles)
        for k in range(NTL + 3):
            if k < NTL:
                S1(k)
            if 0 <= k - 1 < NTL:
                S2(k - 1)
            if 0 <= k - 2 < NTL:
                S3(k - 2)
            if 0 <= k - 3 < NTL:
                S4(k - 3)
        for h in range(8):
            px, pxr, _ = PSX.next()
            for c in range(2):
                P.op("pe", lambda e, px=px, c=c, h=h: e.matmul(px[:, 0:64], WUV[:, c, h * 128:(h + 1) * 128], OLT[:, c, h, :], start=(c == 0), stop=(c == 1)),
                     reads=[r_WUV, r_OLT] if c == 0 else [], writes=[pxr] if c == 0 else [], inc=(c == 1))
            P.op("act", lambda e, px=px, h=h: e.activation(out=oT[:, h, 2048:2112], in_=px[:, 0:64], func=AF.Copy), reads=[pxr, r_oT[4]], writes=[r_oT[4]])
        if stop == "S":
            P.wait_all("sp", out_tokens)
        P.emit()
    if stop == "S":
        return nc

    with ExitStack() as es:
        C = make_common(es)
        sbx = C["sbx"]
        X, XN, Hh = C["X"], C["XN"], C["Hh"]
        r_X, r_XN, r_H = C["r_X"], C["r_XN"], C["r_H"]
        YT = Ring(P, nc, "YT", [128, D], F32, 2, dma=True, alloc=sbx)
        s_XL = P.dma_sem("XL")
        for t in range(NT):
            N = 512 if t < 4 else 64
            nb, bp = (4, 128) if t < 4 else (1, 64)
            col0 = t * 512
            P.dma("sp", lambda e, t=t, N=N: e.dma_start(out=X[:, :, 0:N], in_=xsc[t].rearrange("p (k n) -> p k n", n=512)[:, :, 0:N]), s_XL, reads=[r_xsc[t]], writes=r_X)
            Wo = w_o[0]
            for mb in range(0, 8, 4):
                tw, rw = load_wA(C, wview(Wo, mb * 128, 512))
                for mm in range(4):
                    m = mb + mm
                    pt, pr, _ = PS.next()
                    for k in range(8):
                        P.op("pe", lambda e, pt=pt, tw=tw, k=k, mm=mm, N=N, col0=col0: e.matmul(pt[:, 0:N], tw[:, k, mm * 128:(mm + 1) * 128], oT[:, k, col0:col0 + N], start=(k == 0), stop=(k == 7)),
                             reads=[rw, r_oT[t]] if k == 0 else [], writes=[pr] if k == 0 else [], inc=(k == 7))
                    P.op("dve", lambda e, pt=pt, m=m, N=N: e.tensor_tensor(out=X[:, m, 0:N], in0=pt[:, 0:N], in1=X[:, m, 0:N], op=ALU.add), reads=[pr, r_X[m]], writes=[r_X[m]])
            rmsnorm_fm(C, N, False, GV["nf2"][1])
            ffn(C, N, False, w_ffn2_gu[1], w_ffn2_down[1])
            for b in range(nb):
                yt, yr, yts = YT.next()
                for half in range(2):
                    pt, pr, _ = PS.next()
                    for kk in range(4):
                        k = half * 4 + kk
                        P.op("pe", lambda e, pt=pt, k=k, kk=kk, b=b, bp=bp: e.transpose(out=pt[0:bp, kk * 128:(kk + 1) * 128], in_=X[:, k, b * bp:(b + 1) * bp], identity=ident_f[:, :]),
                             reads=[r_X[k], r_ident_f], writes=[pr] if kk == 0 else [], touch=[] if kk == 0 else [pr], inc=(kk == 3))
                    P.op("act", lambda e, pt=pt, yt=yt, half=half, bp=bp: e.activation(out=yt[0:bp, half * 512:(half + 1) * 512], in_=pt[0:bp, :], func=AF.Copy), reads=[pr, yr], writes=[yr])
                if t < 4:
                    out_tokens.append(P.dma("sp", lambda e, yt=yt, t=t, b=b: e.dma_start(out=y_p[t, b * 128:(b + 1) * 128, :], in_=yt[:, :]), yts, reads=[yr], writes=[r_out]))
                else:
                    out_tokens.append(P.dma("sp", lambda e, yt=yt: e.dma_start(out=y_s, in_=yt[0:64, :]), yts, reads=[yr], writes=[r_out]))
        P.wait_all("sp", out_tokens)
        P.emit()
    _NC_CACHE["cnt"] = dict(P.cnt)
    return nc


DEBUG_STOP = ""


def _get_nc(n_phys=10240, stop=""):
    key = ("nc", n_phys, stop)
    if key not in _NC_CACHE:
        _NC_CACHE[key] = build_program(n_phys, stop)
    return _NC_CACHE[key]


def _host_layout(inp):
    f32 = np.float32
    g = {k: np.asarray(v) for k, v in inp.items()}
    x_prompt, x_sample = g["x_prompt"], g["x_sample"]
    cache_lat = np.ascontiguousarray(g["cache_latent"]).reshape(-1, 256)
    cache_kpe = np.ascontiguousarray(g["cache_kpe"]).reshape(-1, 64)
    rows = [g["norm_ffn1"][0], g["norm_ffn1"][1], g["norm_mix"][0], g["norm_mix"][1], g["norm_ffn2"][0], g["norm_ffn2"][1],
            g["norm_kv_in"], g["conv_w"][0, 0], g["conv_w"][0, 1], g["conv_w"][0, 2]]
    gvec = np.zeros((88, 128), f32)
    gvec[0:80] = np.concatenate([r.reshape(8, 128) for r in rows], 0)
    gvec[80] = g["k_nope_norm"]
    freqs = (np.float32(10000.0) ** (-np.arange(32, dtype=f32) / np.float32(32))).astype(f32)

    def cs_table(pos):
        ang = (pos.astype(f32)[:, None] * freqs[None, :]).astype(f32)
        return np.concatenate([np.cos(ang), np.sin(ang)], 1).astype(f32)

    ident = np.eye(128, dtype=f32)
    iota = np.arange(128, dtype=np.int32)[:, None]
    kk = np.arange(128)[:, None, None]
    jj = np.arange(16)[None, :, None]
    qq = np.arange(512)[None, None, :]
    bp_, tp_ = np.divmod(np.arange(64), 4)
    mn = np.zeros((64, 8, 16, 4), f32)
    for b in range(16):
        for t in range(4):
            mn[:, :, b, t] = ((bp_ == b) & (tp_ <= t))[:, None]
    mask_n = mn.reshape(64, 512).astype(ml_dtypes.bfloat16)
    cs_s = cs_table(8192 + (np.arange(64) % 4))
    hr = cache_lat.shape[0] // 2
    cache_cat = np.concatenate([cache_lat, cache_kpe], axis=1)
    sel = (np.arange(128)[:, None] // 32 == np.arange(4)[None, :]).astype(f32)
    iota32 = (np.arange(128) % 32).astype(f32)[:, None]
    shared = dict(cache_q0=cache_cat[:hr].reshape(hr // 4, 1280), cache_q1=cache_cat[hr:].reshape(hr // 4, 1280), sel=sel, iota32=iota32, gvec=gvec, ident=ident, iota=iota, mask_n=mask_n, cs_s=cs_s,
                  q_norm=g["q_norm"], q_nope_norm=g["q_nope_norm"], q_pe_norm=g["q_pe_norm"], kv_norm=g["kv_norm"], k_pe_norm=g["k_pe_norm"])
    for k in ("w_ffn1_gu", "w_ffn1_down", "w_ffn2_gu", "w_ffn2_down", "w_conv_in", "w_conv_out", "w_dq", "w_uq", "w_o", "w_dkv", "w_uk", "w_uv"):
        shared[k] = g[k]
    in_maps = []
    for c in range(8):
        q, cp = divmod(c, 4)
        xp = np.zeros((4, 514, D), f32)
        cs_p = np.zeros((4, 512, 64), f32)
        for s in range(4):
            st = (4 * s + cp) * 512
            xp[s, 2:] = x_prompt[q, st:st + 512]
            if st > 0:
                xp[s, 0:2] = x_prompt[q, st - 2:st]
            cs_p[s] = cs_table(st + np.arange(512))
        mask_p = (128 * jj + kk <= cp * 512 + qq).astype(f32).reshape(128, 16 * 512).astype(ml_dtypes.bfloat16)
        m = dict(shared)
        m.update(xp=xp, xs=np.ascontiguousarray(x_sample[16 * c:16 * c + 16]).reshape(64, D),
                 sconv=np.ascontiguousarray(g["state_conv"][0, 16 * c:16 * c + 16]).reshape(32, D),
                 ptab=np.ascontiguousarray(g["page_table"][16 * c:16 * c + 16]).reshape(1, 16 * NPAGE).astype(np.int32),
                 cs_p=cs_p, mask_p=mask_p)
        in_maps.append(m)
    return in_maps


def kernel(_stop="", _trace=False, **inputs):
    nc = _get_nc(int(np.asarray(inputs["cache_latent"]).shape[0]), _stop)
    in_maps = _host_layout(inputs)
    res = run_bass_kernel_spmd(nc, in_maps, core_ids=list(range(8)), **({"trace": True} if _trace else {}))
    if _trace:
        print("exec_time_ns", res.exec_time_ns)
    R = res.results
    f32 = np.float32
    y_prompt = np.zeros((2, 8192, D), f32)
    y_sample = np.zeros((128, 4, D), f32)
    conv_p = np.zeros((1, 2, 2, D), f32)
    conv_s = np.zeros((1, 128, 2, D), f32)
    lat_p = np.zeros((2, 8192, 256), f32)
    kpe_p = np.zeros((2, 8192, 64), f32)
    lat_s = np.zeros((128, 4, 256), f32)
    kpe_s = np.zeros((128, 4, 64), f32)
    for c in range(8):
        q, cp = divmod(c, 4)
        r = R[c]
        for s in range(4):
            st = (4 * s + cp) * 512
            y_prompt[q, st:st + 512] = r["y_p"][s]
            lat_p[q, st:st + 512] = r["lat_p"][s]
            kpe_p[q, st:st + 512] = r["kpe_p"][s]
        if cp == 3:
            conv_p[0, q] = r["cso_p"][3]
        y_sample[16 * c:16 * c + 16] = r["y_s"].reshape(16, 4, D)
        conv_s[0, 16 * c:16 * c + 16] = r["cso_s"].reshape(16, 2, D)
        lat_s[16 * c:16 * c + 16] = r["lat_s"].reshape(16, 4, 256)
        kpe_s[16 * c:16 * c + 16] = r["kpe_s"].reshape(16, 4, 64)
    return (y_prompt, y_sample, conv_p, conv_s, lat_p, kpe_p, lat_s, kpe_s)
```
